# Optimizing a Trainium2 kernel written in Bass

```python
import jax, jax.numpy as jnp
from jax import lax
import numpy as np

D_MODEL = 1024
BATCH = 8
SEQ = 4096
DEPTH = 4

N_A_LAYERS = DEPTH // 2
N_B_LAYERS = DEPTH - N_A_LAYERS
SSM_EXPAND = 2
D_INNER = SSM_EXPAND * D_MODEL
SSM_HEAD_DIM = 64
SSM_HEADS = D_INNER // SSM_HEAD_DIM
SSM_GROUPS = 4
SSM_HPG = SSM_HEADS // SSM_GROUPS
D_STATE = 128
CONV_WIDTH = 4
CONV_CH = D_INNER + 2 * SSM_GROUPS * D_STATE
SSM_IN = D_INNER + CONV_CH + SSM_HEADS
CHUNK = 128
ATT_HEADS = 16
KV_HEADS = 4
HEAD_DIM = 64
Q_PER_KV = ATT_HEADS // KV_HEADS
ATT_WIDTH = ATT_HEADS * HEAD_DIM
KV_WIDTH = KV_HEADS * HEAD_DIM
WINDOW = 128
BLOCK = WINDOW
ROPE_THETA = 10000.0
PLE_DIM = 256
EPS = 1e-6

kernel_name = 'yoco_ssd_swa_sink_hybrid'


def rms_norm(x, w):
    xf = x.astype(jnp.float32)
    y = xf * lax.rsqrt(jnp.mean(xf * xf, axis=-1, keepdims=True) + EPS)
    return (y * w.astype(jnp.float32)).astype(x.dtype)


def rope(x, positions):
    half = HEAD_DIM // 2
    inv_freq = ROPE_THETA ** (-(jnp.arange(half, dtype=jnp.float32) * 2.0 / HEAD_DIM))
    ang = positions.astype(jnp.float32)[..., None] * inv_freq
    cos = jnp.cos(ang)[:, :, None, :]
    sin = jnp.sin(ang)[:, :, None, :]
    xf = x.astype(jnp.float32)
    x1, x2 = xf[..., :half], xf[..., half:]
    return jnp.concatenate([x1 * cos - x2 * sin, x2 * cos + x1 * sin], axis=-1).astype(x.dtype)


def ssd_chunked(xs, dt, a, bm, cm):
    b, s = xs.shape[:2]
    nc = s // CHUNK
    x = xs.astype(jnp.float32).reshape(b, nc, CHUNK, SSM_GROUPS, SSM_HPG, SSM_HEAD_DIM)
    dt = dt.reshape(b, nc, CHUNK, SSM_GROUPS, SSM_HPG)
    bm = bm.astype(jnp.float32).reshape(b, nc, CHUNK, SSM_GROUPS, D_STATE)
    cm = cm.astype(jnp.float32).reshape(b, nc, CHUNK, SSM_GROUPS, D_STATE)
    a_cum = jnp.cumsum(dt * a.reshape(SSM_GROUPS, SSM_HPG), axis=2)
    xdt = x * dt[..., None]
    seg = a_cum[:, :, :, None] - a_cum[:, :, None, :]
    causal = jnp.tril(jnp.ones((CHUNK, CHUNK), bool))[None, None, :, :, None, None]
    decay = jnp.exp(jnp.where(causal, seg, -jnp.inf))
    cb = jnp.einsum('bcign,bcjgn->bcijg', cm, bm)
    y_diag = jnp.einsum('bcijg,bcijgr,bcjgrp->bcigrp', cb, decay, xdt)
    decay_to_end = jnp.exp(a_cum[:, :, -1:] - a_cum)
    states = jnp.einsum('bcjgn,bcjgr,bcjgrp->bcgrpn', bm, decay_to_end, xdt)
    chunk_decay = jnp.exp(a_cum[:, :, -1])

    def step(carry, inp):
        st, dec = inp
        return carry * dec[..., None, None] + st, carry

    init = jnp.zeros((b, SSM_GROUPS, SSM_HPG, SSM_HEAD_DIM, D_STATE), jnp.float32)
    _, prev = lax.scan(step, init, (jnp.moveaxis(states, 1, 0), jnp.moveaxis(chunk_decay, 1, 0)))
    prev = jnp.moveaxis(prev, 0, 1)
    y_off = jnp.einsum('bcign,bcgrpn,bcigr->bcigrp', cm, prev, jnp.exp(a_cum))
    return (y_diag + y_off).reshape(b, s, SSM_HEADS, SSM_HEAD_DIM)


def mamba2_mixer(h, norm_w, in_w, conv_w, conv_b, dt_bias, a_log, d_skip, gnorm_w, out_w):
    b, s, _ = h.shape
    u = rms_norm(h, norm_w)
    zxbcdt = u @ in_w
    z, xbc, dt = jnp.split(zxbcdt, [D_INNER, D_INNER + CONV_CH], axis=-1)
    xbc = lax.conv_general_dilated(xbc, conv_w[:, None, :], window_strides=(1,),
                                   padding=[(CONV_WIDTH - 1, 0)],
                                   dimension_numbers=('NWC', 'WIO', 'NWC'),
                                   feature_group_count=CONV_CH) + conv_b
    xbc = jax.nn.silu(xbc)
    xs, bm, cm = jnp.split(xbc, [D_INNER, D_INNER + SSM_GROUPS * D_STATE], axis=-1)
    xs = xs.reshape(b, s, SSM_HEADS, SSM_HEAD_DIM)
    bm = bm.reshape(b, s, SSM_GROUPS, D_STATE)
    cm = cm.reshape(b, s, SSM_GROUPS, D_STATE)
    dt = jax.nn.softplus(dt.astype(jnp.float32) + dt_bias.astype(jnp.float32))
    a = -jnp.exp(a_log.astype(jnp.float32))
    y = ssd_chunked(xs, dt, a, bm, cm)
    y = y + d_skip.astype(jnp.float32)[:, None] * xs.astype(jnp.float32)
    y = y.reshape(b, s, D_INNER) * jax.nn.silu(z.astype(jnp.float32))
    yg = y.reshape(b, s, SSM_GROUPS, D_INNER // SSM_GROUPS)
    yg = yg * lax.rsqrt(jnp.mean(yg * yg, axis=-1, keepdims=True) + EPS)
    y = yg.reshape(b, s, D_INNER) * gnorm_w.astype(jnp.float32)
    return y.astype(h.dtype) @ out_w


def shared_kv(h, positions, kv_norm_w, kv_w, k_norm_w):
    b, s, _ = h.shape
    u = rms_norm(h, kv_norm_w)
    k, v = jnp.split(u @ kv_w, 2, axis=-1)
    k = k.reshape(b, s, KV_HEADS, HEAD_DIM)
    v = v.reshape(b, s, KV_HEADS, HEAD_DIM)
    k = rope(rms_norm(k, k_norm_w), positions)
    return k, v


def swa_sink_attention(q, k, v, sinks):
    b, s = q.shape[:2]
    nb = s // BLOCK
    qb = q.reshape(b, nb, BLOCK, KV_HEADS, Q_PER_KV, HEAD_DIM)

    def with_prev(t):
        t = t.reshape(b, nb, BLOCK, KV_HEADS, HEAD_DIM)
        prev = jnp.pad(t, ((0, 0), (1, 0), (0, 0), (0, 0), (0, 0)))[:, :-1]
        return jnp.concatenate([prev, t], axis=2)

    kk, vv = with_prev(k), with_prev(v)
    scores = jnp.einsum('bnqhgd,bnkhd->bnhgqk', qb, kk).astype(jnp.float32) * (HEAD_DIM ** -0.5)
    blk = jnp.arange(nb)[:, None, None] * BLOCK
    q_pos = blk + jnp.arange(BLOCK)[None, :, None]
    k_pos = blk - BLOCK + jnp.arange(2 * BLOCK)[None, None, :]
    valid = (k_pos <= q_pos) & (q_pos - k_pos < WINDOW) & (k_pos >= 0)
    scores = jnp.where(valid[None, :, None, None], scores, -jnp.inf)
    sink = sinks.astype(jnp.float32).reshape(KV_HEADS, Q_PER_KV)[None, None, :, :, None, None]
    m = jnp.maximum(jnp.max(scores, axis=-1, keepdims=True), sink)
    e = jnp.exp(scores - m)
    probs = e / (jnp.sum(e, axis=-1, keepdims=True) + jnp.exp(sink - m))
    out = jnp.einsum('bnhgqk,bnkhd->bnqhgd', probs.astype(v.dtype), vv)
    return out.reshape(b, s, ATT_WIDTH)


def swa_layer(h, k, v, positions, norm_w, in_w, q_norm_w, sinks, out_w):
    b, s, _ = h.shape
    u = rms_norm(h, norm_w)
    q, gate = jnp.split(u @ in_w, 2, axis=-1)
    q = rope(rms_norm(q.reshape(b, s, ATT_HEADS, HEAD_DIM), q_norm_w), positions)
    o = swa_sink_attention(q, k, v, sinks)
    return (o * jax.nn.silu(gate)) @ out_w


def per_layer_embedding(h, p_i, norm_w, gate_w, proj_w):
    g = jax.nn.sigmoid((rms_norm(h, norm_w) @ gate_w).astype(jnp.float32))
    return (g * (p_i @ proj_w).astype(jnp.float32)).astype(h.dtype)


def setup_inputs(seed: int = 0) -> dict:
    key = jax.random.key(seed)
    ks = jax.random.split(key, 24)
    f32 = jnp.float32

    def nrm(k, shape, scale):
        return jax.random.normal(k, shape, f32) * scale

    dt0 = jnp.exp(jax.random.uniform(ks[7], (N_A_LAYERS, SSM_HEADS), f32, np.log(1e-3), np.log(1e-1)))
    offs = jax.random.randint(ks[3], (BATCH, 1), 0, 1024, jnp.int32)
    return {
        'x': nrm(ks[0], (BATCH, SEQ, D_MODEL), 1.0),
        'p': nrm(ks[1], (DEPTH, BATCH, SEQ, PLE_DIM), 1.0),
        'positions': (offs + jnp.arange(SEQ, dtype=jnp.int32)[None, :]).astype(jnp.int32),
        'ssm_norm_w': 1.0 + nrm(ks[2], (N_A_LAYERS, D_MODEL), 0.05),
        'ssm_in_w': nrm(ks[4], (N_A_LAYERS, D_MODEL, SSM_IN), D_MODEL ** -0.5),
        'ssm_conv_w': nrm(ks[5], (N_A_LAYERS, CONV_WIDTH, CONV_CH), CONV_WIDTH ** -0.5),
        'ssm_conv_b': nrm(ks[6], (N_A_LAYERS, CONV_CH), 0.02),
        'ssm_dt_bias': dt0 + jnp.log(-jnp.expm1(-dt0)),
        'ssm_a_log': jnp.log(jax.random.uniform(ks[8], (N_A_LAYERS, SSM_HEADS), f32, 1.0, 16.0)),
        'ssm_d': 1.0 + nrm(ks[9], (N_A_LAYERS, SSM_HEADS), 0.1),
        'ssm_gnorm_w': 1.0 + nrm(ks[10], (N_A_LAYERS, D_INNER), 0.05),
        'ssm_out_w': nrm(ks[11], (N_A_LAYERS, D_INNER, D_MODEL), D_INNER ** -0.5),
        'kv_norm_w': 1.0 + nrm(ks[12], (D_MODEL,), 0.05),
        'kv_w': nrm(ks[13], (D_MODEL, 2 * KV_WIDTH), D_MODEL ** -0.5),
        'k_norm_w': 1.0 + nrm(ks[14], (HEAD_DIM,), 0.05),
        'attn_norm_w': 1.0 + nrm(ks[15], (N_B_LAYERS, D_MODEL), 0.05),
        'attn_in_w': nrm(ks[16], (N_B_LAYERS, D_MODEL, 2 * ATT_WIDTH), D_MODEL ** -0.5),
        'q_norm_w': 1.0 + nrm(ks[17], (N_B_LAYERS, HEAD_DIM), 0.05),
        'attn_sinks': nrm(ks[18], (N_B_LAYERS, ATT_HEADS), 0.5),
        'attn_out_w': nrm(ks[19], (N_B_LAYERS, ATT_WIDTH, D_MODEL), ATT_WIDTH ** -0.5),
        'ple_norm_w': 1.0 + nrm(ks[20], (DEPTH, D_MODEL), 0.05),
        'ple_gate_w': nrm(ks[21], (DEPTH, D_MODEL, D_MODEL), D_MODEL ** -0.5),
        'ple_proj_w': nrm(ks[22], (DEPTH, PLE_DIM, D_MODEL), 0.5 * PLE_DIM ** -0.5),
    }


def reference(x, p, positions, ssm_norm_w, ssm_in_w, ssm_conv_w, ssm_conv_b, ssm_dt_bias, ssm_a_log,
              ssm_d, ssm_gnorm_w, ssm_out_w, kv_norm_w, kv_w, k_norm_w, attn_norm_w, attn_in_w,
              q_norm_w, attn_sinks, attn_out_w, ple_norm_w, ple_gate_w, ple_proj_w):
    h = x
    k_sh = None
    v_sh = None
    for i in range(DEPTH):
        if i < N_A_LAYERS:
            h = h + mamba2_mixer(h, ssm_norm_w[i], ssm_in_w[i], ssm_conv_w[i], ssm_conv_b[i],
                                 ssm_dt_bias[i], ssm_a_log[i], ssm_d[i], ssm_gnorm_w[i], ssm_out_w[i])
        else:
            if i == N_A_LAYERS:
                k_sh, v_sh = shared_kv(h, positions, kv_norm_w, kv_w, k_norm_w)
            j = i - N_A_LAYERS
            h = h + swa_layer(h, k_sh, v_sh, positions, attn_norm_w[j], attn_in_w[j], q_norm_w[j],
                              attn_sinks[j], attn_out_w[j])
        h = h + per_layer_embedding(h, p[i], ple_norm_w[i], ple_gate_w[i], ple_proj_w[i])
    return h
```

```python
import contextlib
import numpy as np
import concourse.bass as bass
import concourse.mybir as mybir
from concourse.bass_utils import run_bass_kernel_spmd

F32 = mybir.dt.float32
BF16 = mybir.dt.bfloat16
I32 = mybir.dt.int32
AF = mybir.ActivationFunctionType
ALU = mybir.AluOpType
AX = mybir.AxisListType

EPOCH = 512
D = 1024
S = 4096
NCH = 32
EPS = 1e-6
SSM_IN = 5152
LVW = 256
PERM = [0, 4, 1, 5, 2, 6, 3, 7, 8, 12, 9, 13, 10, 14, 11, 15]


class Buf:
    __slots__ = ("name", "w", "wd", "rs", "rd", "psum")

    def __init__(self, name, psum=False):
        self.name = name
        self.w = {}
        self.wd = []
        self.rs = {}
        self.rd = []
        self.psum = psum


class Ins:
    __slots__ = ("eng", "fn", "deps", "sig", "dma", "key", "val", "sem", "n")

    def __init__(self, eng, fn, dma=False, key=None):
        self.eng = eng
        self.fn = fn
        self.deps = []
        self.sig = False
        self.dma = dma
        self.key = key
        self.val = None
        self.sem = None


class _Rec:
    def __init__(self):
        self.calls = []

    def __getattr__(self, name):
        def f(*a, **k):
            self.calls.append((name, a, k))
            return None
        return f


def _replay(calls, eobj):
    return [getattr(eobj, name)(*a, **k) for name, a, k in calls]


class Prog:
    ENGS = ("pe", "act", "dve", "pool", "sp")

    def __init__(self, nc):
        self.nc = nc
        self.streams = {e: [] for e in self.ENGS}
        self.dma_cnt = {}
        self.nins = 0

    def _track(self, ins, R, W):
        deps = ins.deps
        eng = ins.eng
        dma = ins.dma
        for b in R:
            deps.extend(b.w.values())
            deps.extend(b.wd)
            if b.psum:
                for e, x in b.rs.items():
                    if e != eng:
                        deps.append(x)
        for b in W:
            for e, x in b.w.items():
                if dma or e != eng or eng != "pe":
                    deps.append(x)
            deps.extend(b.wd)
            for e, x in b.rs.items():
                if dma or e != eng or eng != "pe":
                    deps.append(x)
            deps.extend(b.rd)
        for b in R:
            if dma:
                b.rd.append(ins)
            else:
                b.rs[eng] = ins
        for b in W:
            b.rs = {}
            b.rd = []
            if dma:
                b.w = {}
                b.wd = [ins]
            else:
                b.w = {eng: ins}
                b.wd = []

    def op(self, eng, fn, R=(), W=()):
        rec = _Rec()
        fn(rec)
        assert len(rec.calls) == 1
        ins = Ins(eng, rec.calls)
        self._track(ins, R, W)
        self.streams[eng].append(ins)
        self.nins += 1
        return ins

    def dma(self, eng, fn, key, n, R=(), W=()):
        rec = _Rec()
        fn(rec)
        assert len(rec.calls) == n, (len(rec.calls), n)
        ins = Ins(eng, rec.calls, dma=True, key=key)
        ins.n = n
        self._track(ins, R, W)
        c = self.dma_cnt.get(key, 0) + n
        self.dma_cnt[key] = c
        ins.val = 16 * c
        self.streams[eng].append(ins)
        self.nins += 1
        return ins

    def emit(self, final_waits=()):
        nc = self.nc
        for e in self.ENGS:
            for ins in self.streams[e]:
                for d in ins.deps:
                    if not d.dma:
                        d.sig = True
        nsig = {}
        for e in self.ENGS:
            s = 0
            for ins in self.streams[e]:
                if ins.sig and not ins.dma:
                    ins.sem = (e, s // EPOCH)
                    ins.val = s % EPOCH + 1
                    s += 1
            nsig[e] = s
        sem_names = []
        for e in self.ENGS:
            for k in range((nsig[e] + EPOCH - 1) // EPOCH):
                sem_names.append((e, k))
        for key in self.dma_cnt:
            sem_names.append(("dma", key))
        print(f"[prog] instructions={self.nins} sems={len(sem_names)} sig={nsig}", flush=True)
        with contextlib.ExitStack() as st:
            sems = {}
            for nm in sem_names:
                sems[nm] = st.enter_context(nc.semaphore(f"s_{nm[0]}_{nm[1]}"))
            block = st.enter_context(nc.Block())

            def run_stream(ename, eobj):
                waited = {}
                for ins in self.streams[ename]:
                    need = {}
                    for d in ins.deps:
                        k = ("dma", d.key) if d.dma else d.sem
                        v = d.val
                        if waited.get(k, 0) >= v:
                            continue
                        if need.get(k, 0) < v:
                            need[k] = v
                    for k, v in need.items():
                        eobj.wait_ge(sems[k], v)
                        waited[k] = v
                    if ins.dma:
                        for r in _replay(ins.fn, eobj):
                            r.then_inc(sems[("dma", ins.key)], 16)
                    else:
                        r = _replay(ins.fn, eobj)[0]
                        if ins.sig:
                            r.then_inc(sems[ins.sem], 1)
                if ename == "sp":
                    for d in final_waits:
                        eobj.wait_ge(sems[("dma", d.key)], d.val)

            @block.tensor
            def _(pe):
                run_stream("pe", pe)

            @block.scalar
            def _(act):
                run_stream("act", act)

            @block.vector
            def _(dve):
                run_stream("dve", dve)

            @block.gpsimd
            def _(pool):
                run_stream("pool", pool)

            @block.sync
            def _(sp):
                run_stream("sp", sp)


def build(layers=(0, 1, 2, 3), nch=NCH):
    nc = bass.Bass("TRN2", target_bir_lowering=False)

    def dram(name, shape, dt, kind="ExternalInput"):
        return nc.dram_tensor(name, shape, dt, kind=kind).ap()

    x_d = dram("x", [S, D], F32)
    pT_d = dram("pT", [4, 256, S], F32)
    pos_d = dram("pos", [128, 32], I32)
    cst_d = dram("cst", [128, 4, 128], F32)
    invf_d = dram("invf", [128, 32], F32)
    w_in_d = dram("w_in", [2, 128, 8 * SSM_IN], F32)
    w_out_d = dram("w_out", [2, 128, 16 * 1024], F32)
    w_g_d = dram("w_g", [4, 128, 8 * 1024], F32)
    w_p_d = dram("w_p", [4, 128, 2 * 1024], F32)
    w_kv_d = dram("w_kv", [128, 8 * 512], F32)
    w_ai_d = dram("w_ai", [2, 128, 8 * 2048], F32)
    w_ao_d = dram("w_ao", [2, 128, 8 * 1024], F32)
    lv_d = dram("lv", [4, 128, LVW], F32)
    out_d = dram("out", [S, D], F32, kind="ExternalOutput")

    st = contextlib.ExitStack()
    with st:
        def sb(name, shape, dt):
            return st.enter_context(nc.sbuf_tensor(name, shape, dt))

        def ps(name, shape, dt):
            return st.enter_context(nc.psum_tensor(name, shape, dt))

        P = Prog(nc)
        WB = sb("WB", [128, 67840], BF16)
        B_WB = Buf("WB")
        LV = sb("LV", [128, LVW], F32)
        B_LV = Buf("LV")
        cbf = sb("cbf", [128, 4, 128], BF16)
        c32 = sb("c32", [128, 4, 128], F32)
        B_cst = Buf("cst")
        ident = cbf[:, 0, :]
        Ubf = cbf[:, 1, :]
        Lbf = cbf[:, 2, :]
        U32 = c32[:, 1, :]
        L32 = c32[:, 2, :]
        ones32 = c32[:, 3, :]
        h_t = [sb(f"h{i}", [128, D], F32) for i in range(2)]
        B_h = [Buf(f"h{i}") for i in range(2)]
        pT_t = [sb(f"pTt{i}", [128, 2, 128], BF16) for i in range(2)]
        B_pT = [Buf(f"pTt{i}") for i in range(2)]
        stat = [sb(f"stat{i}", [128, 64], F32) for i in range(2)]
        B_stat = [Buf(f"stat{i}") for i in range(2)]
        u_bf = sb("u_bf", [128, D], BF16)
        B_u = Buf("u_bf")
        uT = sb("uT", [128, 8, 128], BF16)
        B_uT = Buf("uT")
        sm = sb("sm", [128, 16, 32], F32)
        B_sm = Buf("sm")
        lsm = sb("lsm", [128, 4, 32], F32)
        B_lsm = Buf("lsm")
        xraw = sb("xraw", [128, 6, 131], F32)
        B_xraw = Buf("xraw")
        B_xr = [Buf(f"xr{t}") for t in range(6)]
        B_ca = [Buf(f"ca{t}") for t in range(6)]
        ctail = sb("ctail", [128, 24, 3], F32)
        B_ctail = [Buf(f"ctail{g}") for g in range(4)]
        cacc = sb("cacc", [128, 6, 128], F32)
        B_cacc = Buf("cacc")
        actT = sb("actT", [128, 6, 128], BF16)
        B_actT = Buf("actT")
        xdt = sb("xdt", [128, 512], BF16)
        B_xdt = Buf("xdt")
        xsD = sb("xsD", [128, 512], BF16)
        B_xsD = Buf("xsD")
        xw = sb("xw", [128, 512], BF16)
        B_xw = Buf("xw")
        Btm = sb("Btm", [128, 128], BF16)
        B_Btm = Buf("Btm")
        cbm = sb("cbm", [128, 128], F32)
        B_cbm = Buf("cbm")
        scrA = sb("scrA", [128, 2048], F32)
        B_A = [Buf("scrA0"), Buf("scrA1")]
        scrB = sb("scrB", [128, 2048], F32)
        B_B = [Buf(f"scrB{i}") for i in range(4)]
        scrC = sb("scrC", [128, 2048], BF16)
        B_C = [Buf("scrC0"), Buf("scrC1"), Buf("scrC2")]
        scrD = sb("scrD", [128, 2048], BF16)
        B_D = [Buf(f"scrD{i}") for i in range(4)]
        St = sb("St", [128, 2048], F32)
        B_St = [Buf(f"St{i}") for i in range(4)]
        _cb16 = cacc[:].rearrange("p t n -> p (t n)").bitcast(BF16)
        Eatt = _cb16[:, 0:1024].rearrange("p (a n) -> p a n", a=2)
        B_Eatt = [Buf("Eatt0"), Buf("Eatt1")]
        krbuf = _cb16[:, 1024:1280]
        B_kr = Buf("kr")
        kf = xraw[:].rearrange("p t n -> p (t n)")[:, 0:768].rearrange("p (a b) -> p a b", a=3)
        B_kf = B_xraw
        actTb = sb("actTb", [128, 6, 128], BF16)
        xdtb = sb("xdtb", [128, 512], BF16)
        xsDb = sb("xsDb", [128, 512], BF16)
        xwb = sb("xwb", [128, 512], BF16)
        Btmb = sb("Btmb", [128, 128], BF16)
        actT2, xdt2, xsD2, xw2, Btm2 = [actT, actTb], [xdt, xdtb], [xsD, xsDb], [xw, xwb], [Btm, Btmb]
        B_actT2 = [B_actT, Buf("actTb")]
        B_xdt2 = [B_xdt, Buf("xdtb")]
        B_xsD2 = [B_xsD, Buf("xsDb")]
        B_xw2 = [B_xw, Buf("xwb")]
        B_Btm2 = [B_Btm, Buf("Btmb")]
        PT = ps("PT", [128, 8, 128], BF16)
        B_PT = Buf("PT", psum=True)
        PS = [ps(f"PS{i}", [128, 512], F32) for i in range(7)]
        B_PS = [Buf(f"PS{i}", psum=True) for i in range(7)]

        def wv(off, k, n):
            return WB[:, off:off + k * n].rearrange("p (k n) -> p k n", k=k)
        W_in = wv(0, 8, SSM_IN)
        W_out = wv(41216, 16, 1024)
        W_g = wv(57600, 8, 1024)
        W_p = wv(65792, 2, 1024)
        W_ai = wv(0, 8, 2048)
        W_ao = wv(16384, 8, 1024)
        W_kv = wv(24576, 8, 512)
        KT = wv(28672, 2, 4096)
        Vst = WB[:, 36864:36864 + 8320].rearrange("p (c g e) -> p c g e", c=32, g=4)
        B_KT = [Buf(f"KT{c}") for c in range(NCH)]
        B_V = [Buf(f"V{c}") for c in range(NCH)]
        B_Vones = Buf("Vones")
        cosT = St[:, 0:1024].rearrange("p (c f) -> p c f", c=32)
        sinT = St[:, 1024:2048].rearrange("p (c f) -> p c f", c=32)

        P.dma("sp", lambda e: [e.dma_start(out=c32[:], in_=cst_d)], "cst", 1, W=[B_cst])
        P.dma("pool", lambda e: [e.dma_start(out=cbf[:], in_=cst_d)], "cstb", 1, W=[B_cst])

        hstores = {}

        def load_weights(pairs, L):
            fns = []
            for dst, src, ns in pairs:
                n = dst.shape[1]
                step = n // ns
                for i in range(ns):
                    fns.append((dst[:, i * step:(i + 1) * step], src[:, i * step:(i + 1) * step]))
            P.dma("pool", lambda e, fns=fns: [e.dma_start(out=d_, in_=s_) for d_, s_ in fns], f"wload{L}", len(fns),
                  W=[B_WB])

        def rstd_from_ss(stt, col, n, bst):
            P.op("act", lambda e: e.activation(out=stt[:, col + 1:col + 2], in_=stt[:, col:col + 1], func=AF.Ln,
                                               scale=1.0 / n, bias=lsm[:, 2, 0:1]), R=[bst, B_lsm], W=[bst])
            P.op("act", lambda e: e.activation(out=stt[:, col + 2:col + 3], in_=stt[:, col + 1:col + 2], func=AF.Exp,
                                               scale=-0.5), R=[bst], W=[bst])

        def norm_T(hh, bh, stt, bst, col, nwoff, have_rstd=False):
            if not have_rstd:
                P.op("act", lambda e: e.activation(out=u_bf[:], in_=hh[:], func=AF.Square,
                                                   accum_out=stt[:, col:col + 1]), R=[bh, bst], W=[B_u, bst])
                rstd_from_ss(stt, col, D, bst)
            P.op("act", lambda e: e.activation(out=u_bf[:], in_=hh[:], func=AF.Copy, scale=stt[:, col + 2:col + 3]),
                 R=[bh, bst], W=[B_u])
            for k in range(8):
                P.op("pe", lambda e, k=k: e.transpose(out=PT[:, k, :], in_=u_bf[:, k * 128:(k + 1) * 128],
                                                      identity=ident), R=[B_u, B_cst], W=[B_PT])
            P.op("dve", lambda e: e.tensor_tensor(out=uT[:], in0=PT[:],
                                                  in1=LV[:, nwoff:nwoff + 8].unsqueeze(2).to_broadcast([128, 8, 128]),
                                                  op=ALU.mult), R=[B_PT, B_LV], W=[B_uT])

        def ple(hh, bh, stt, bst, pt, bpt, pnwoff):
            norm_T(hh, bh, stt, bst, 3, pnwoff)
            tg = scrA[:, 0:1024]
            vv = scrA[:, 1024:2048]
            for nb in range(2):
                for k in range(8):
                    P.op("pe", lambda e, nb=nb, k=k: e.matmul(PS[nb][:], lhsT=uT[:, k, :],
                                                             rhs=W_g[:, k, nb * 512:(nb + 1) * 512],
                                                             start=(k == 0), stop=(k == 7)),
                         R=[B_uT, B_WB], W=[B_PS[nb]])
            for nb in range(2):
                for k in range(2):
                    P.op("pe", lambda e, nb=nb, k=k: e.matmul(PS[2 + nb][:], lhsT=pt[:, k, :],
                                                             rhs=W_p[:, k, nb * 512:(nb + 1) * 512],
                                                             start=(k == 0), stop=(k == 1)),
                         R=[bpt, B_WB], W=[B_PS[2 + nb]])
            for nb in range(2):
                sl = slice(nb * 512, (nb + 1) * 512)
                P.op("act", lambda e, nb=nb, sl=sl: e.activation(out=tg[:, sl], in_=PS[nb][:], func=AF.Tanh, scale=0.5),
                     R=[B_PS[nb]], W=[B_A[0]])
                P.op("dve", lambda e, nb=nb, sl=sl: e.scalar_tensor_tensor(out=vv[:, sl], in0=tg[:, sl], scalar=1.0,
                                                                           in1=PS[2 + nb][:], op0=ALU.add, op1=ALU.mult),
                     R=[B_A[0], B_PS[2 + nb]], W=[B_A[1]])
            P.op("dve", lambda e: e.scalar_tensor_tensor(out=hh[:], in0=vv, scalar=0.5, in1=hh[:], op0=ALU.mult,
                                                         op1=ALU.add), R=[B_A[1], bh], W=[bh])

        def load_chunk(L, c):
            slot = c % 2
            src = x_d if L == layers[0] else out_d
            deps = [] if L == layers[0] else [hstores[(L - 1, c)]]
            i1 = P.dma("sp", lambda e: [e.dma_start(out=h_t[slot][:], in_=src[c * 128:(c + 1) * 128, :])],
                       f"hld{slot}_{L}", 1, W=[B_h[slot]])
            i1.deps.extend(deps)
            P.dma("pool", lambda e: [e.dma_start(out=pT_t[slot][:],
                                                 in_=pT_d[L, :, c * 128:(c + 1) * 128].rearrange("(k p) t -> p k t", p=128))],
                  f"pld{slot}_{L}", 1, W=[B_pT[slot]])

        def store_chunk(L, c):
            slot = c % 2
            hstores[(L, c)] = P.dma("sp", lambda e: [e.dma_start(out=out_d[c * 128:(c + 1) * 128, :], in_=h_t[slot][:])],
                                    f"hst{slot}_{L}", 1, R=[B_h[slot]])

        def layer_common_prep(L):
            P.dma("sp", lambda e: [e.dma_start(out=LV[:], in_=lv_d[L])], f"lv{L}", 1, W=[B_LV])
            P.op("dve", lambda e: e.memset(lsm[:, 2, :], EPS), W=[B_lsm])

        def mamba_layer(L):
            layer_common_prep(L)
            load_weights([(WB[:, 0:41216], w_in_d[L], 8), (WB[:, 41216:57600], w_out_d[L], 4),
                          (WB[:, 57600:65792], w_g_d[L], 2), (WB[:, 65792:67840], w_p_d[L], 1)], L)
            NW, PNW, GW, CW, CB, DTB, ALOG, DSK = 0, 8, 16, 32, 128, 152, 184, 216
            cw = LV[:, CW:CW + 96].rearrange("p (t k) -> p t k", k=4)
            P.op("act", lambda e: e.activation(out=lsm[:, 0, :], in_=LV[:, ALOG:ALOG + 32], func=AF.Exp),
                 R=[B_LV], W=[B_lsm])
            P.op("dve", lambda e: e.tensor_scalar(out=lsm[:, 0, :], in0=lsm[:, 0, :], scalar1=-1.0, scalar2=None,
                                                  op0=ALU.mult), R=[B_lsm], W=[B_lsm])
            P.op("pool", lambda e: e.memset(St[:], 0.0), W=B_St)
            P.op("pool", lambda e: e.memset(scrD[:], 0.0), W=B_D)
            P.op("pool", lambda e: e.memset(ctail[:], 0.0), W=B_ctail)
            load_chunk(L, 0)
            for c in range(nch):
                if c + 1 < nch:
                    load_chunk(L, c + 1)
                slot = c % 2
                hh, bh, stt, bst = h_t[slot], B_h[slot], stat[slot], B_stat[slot]
                P.op("pool", lambda e, stt=stt: e.memset(stt[:], 0.0), W=[bst])
                norm_T(hh, bh, stt, bst, 0, NW)
                SMB = B_PS[1]
                for k in range(8):
                    P.op("pe", lambda e, k=k: e.matmul(PS[1][:, 0:32], lhsT=uT[:, k, :], rhs=W_in[:, k, 5120:5152],
                                                       start=(k == 0), stop=(k == 7)), R=[B_uT, B_WB], W=[SMB])
                dtr, dt_, dta, acs, tmp, dte, ea, cd, ee = (sm[:, i, :] for i in range(9))
                P.op("dve", lambda e: e.tensor_tensor(out=dtr, in0=PS[1][:, 0:32], in1=LV[:, DTB:DTB + 32], op=ALU.add),
                     R=[SMB, B_LV], W=[B_sm])
                P.op("act", lambda e: e.activation(out=ee, in_=dtr, func=AF.Exp), R=[B_sm], W=[B_sm])
                P.op("act", lambda e: e.activation(out=dt_, in_=ee, func=AF.Ln, bias=1.0), R=[B_sm], W=[B_sm])
                P.op("dve", lambda e: e.tensor_tensor(out=dta, in0=dt_, in1=lsm[:, 0, :], op=ALU.mult),
                     R=[B_sm, B_lsm], W=[B_sm])
                P.op("pe", lambda e: e.matmul(PS[1][:, 32:64], lhsT=U32, rhs=dta, start=True, stop=True),
                     R=[B_sm, B_cst], W=[SMB])
                P.op("pe", lambda e: e.matmul(PS[1][:, 64:96], lhsT=ones32, rhs=dta, start=True, stop=True),
                     R=[B_sm, B_cst], W=[SMB])
                P.op("dve", lambda e: e.tensor_copy(out=acs, in_=PS[1][:, 32:64]), R=[SMB], W=[B_sm])
                P.op("dve", lambda e: e.tensor_tensor(out=tmp, in0=PS[1][:, 64:96], in1=acs, op=ALU.subtract),
                     R=[SMB, B_sm], W=[B_sm])
                P.op("act", lambda e: e.activation(out=dte, in_=tmp, func=AF.Exp), R=[B_sm], W=[B_sm])
                P.op("act", lambda e: e.activation(out=ea, in_=acs, func=AF.Exp), R=[B_sm], W=[B_sm])
                P.op("act", lambda e: e.activation(out=cd, in_=PS[1][:, 64:96], func=AF.Exp), R=[SMB], W=[B_sm])
                def stageA(g):
                    q = g % 2
                    aT, xd, xs_, xw_, bt = actT2[q], xdt2[q], xsD2[q], xw2[q], Btm2[q]
                    BaT, Bxd, Bxs, Bxw, Bbt = B_actT2[q], B_xdt2[q], B_xsD2[q], B_xw2[q], B_Btm2[q]
                    cols = [2048 + 512 * g + 128 * t for t in range(4)] + [4096 + 128 * g, 4608 + 128 * g]
                    tiles = [4 * g + t for t in range(4)] + [16 + g, 20 + g]
                    for t in range(6):
                        bank, pos = (0, t) if t < 4 else (1, t - 4)
                        for k in range(8):
                            P.op("pe", lambda e, t=t, k=k, bank=bank, pos=pos: e.matmul(
                                PS[bank][:, pos * 128:(pos + 1) * 128], lhsT=W_in[:, k, cols[t]:cols[t] + 128],
                                rhs=uT[:, k, :], start=(k == 0), stop=(k == 7)), R=[B_uT, B_WB], W=[B_PS[bank]])
                        P.op("pool", lambda e, t=t: e.tensor_copy(out=xraw[:, t, 0:3], in_=ctail[:, tiles[t], :]),
                             R=[B_ctail[g]], W=[B_xr[t]])
                        yield
                    P.op("act", lambda e: e.activation(out=xraw[:, 0:4, 3:131],
                                                       in_=PS[0][:].rearrange("p (t n) -> p t n", t=4), func=AF.Copy),
                         R=[B_PS[0]], W=B_xr[0:4])
                    P.op("act", lambda e: e.activation(out=xraw[:, 4:6, 3:131],
                                                       in_=PS[1][:, 0:256].rearrange("p (t n) -> p t n", t=2), func=AF.Copy),
                         R=[B_PS[1]], W=B_xr[4:6])
                    yield
                    for k in range(8):
                        P.op("pe", lambda e, k=k: e.matmul(PS[0][:], lhsT=uT[:, k, :], rhs=W_in[:, k, 512 * g:512 * (g + 1)],
                                                           start=(k == 0), stop=(k == 7)), R=[B_uT, B_WB], W=[B_PS[0]])
                    yield
                    for t in range(6):
                        ti = tiles[t]
                        P.op("pool", lambda e, t=t: e.tensor_copy(out=ctail[:, tiles[t], :], in_=xraw[:, t, 128:131]),
                             R=[B_xr[t]], W=[B_ctail[g]])
                        P.op("act", lambda e, t=t, ti=ti: e.activation(out=cacc[:, t, :], in_=xraw[:, t, 3:131],
                                                                       func=AF.Identity, scale=cw[:, ti, 3:4],
                                                                       bias=LV[:, CB + ti:CB + ti + 1]),
                             R=[B_xr[t], B_LV], W=[B_ca[t]])
                        for k in range(3):
                            P.op("dve", lambda e, t=t, ti=ti, k=k: e.scalar_tensor_tensor(
                                out=cacc[:, t, :], in0=xraw[:, t, k:k + 128], scalar=cw[:, ti, k:k + 1],
                                in1=cacc[:, t, :], op0=ALU.mult, op1=ALU.add), R=[B_xr[t], B_LV, B_ca[t]], W=[B_ca[t]])
                        yield
                    P.op("act", lambda e: e.activation(out=scrB[:, 1024 + 512 * q:1536 + 512 * q], in_=PS[0][:], func=AF.Silu),
                         R=[B_PS[0]], W=[B_B[2 + q]])
                    P.op("act", lambda e: e.activation(out=aT[:], in_=cacc[:], func=AF.Silu), R=B_ca, W=[BaT])
                    yield
                    for t in range(5):
                        P.op("pe", lambda e, t=t: e.transpose(out=PT[:, t, :], in_=aT[:, t, :], identity=ident),
                             R=[BaT, B_cst], W=[B_PT])
                    yield
                    PTx = PT[:, 0:4, :].rearrange("p t (a d) -> p (t a) d", a=2)
                    P.op("dve", lambda e: e.tensor_tensor(out=xd[:].rearrange("p (h d) -> p h d", h=8), in0=PTx,
                                                          in1=dt_[:, 8 * g:8 * g + 8].unsqueeze(2).to_broadcast([128, 8, 64]),
                                                          op=ALU.mult), R=[B_PT, B_sm], W=[Bxd])
                    yield
                    P.op("dve", lambda e: e.tensor_tensor(out=xs_[:].rearrange("p (h d) -> p h d", h=8), in0=PTx,
                                                          in1=LV[:, DSK + 8 * g:DSK + 8 * g + 8].unsqueeze(2).to_broadcast([128, 8, 64]),
                                                          op=ALU.mult), R=[B_PT, B_LV], W=[Bxs])
                    P.op("act", lambda e: e.activation(out=bt[:], in_=PT[:, 4, :], func=AF.Copy), R=[B_PT], W=[Bbt])
                    yield
                    P.op("pool", lambda e: e.tensor_tensor(out=xw_[:].rearrange("p (h d) -> p h d", h=8),
                                                          in0=xd[:].rearrange("p (h d) -> p h d", h=8),
                                                          in1=dte[:, 8 * g:8 * g + 8].unsqueeze(2).to_broadcast([128, 8, 64]),
                                                          op=ALU.mult), R=[Bxd, B_sm], W=[Bxw])
                    yield

                def stageB(g):
                    q = g % 2
                    aT, xd, xs_, xw_, bt = actT2[q], xdt2[q], xsD2[q], xw2[q], Btm2[q]
                    BaT, Bxd, Bxs, Bxw, Bbt = B_actT2[q], B_xdt2[q], B_xsD2[q], B_xw2[q], B_Btm2[q]
                    szz = scrB[:, 1024 + 512 * q:1536 + 512 * q]
                    Bsz = B_B[2 + q]
                    P.op("pe", lambda e: e.matmul(PS[4][:, 0:128], lhsT=aT[:, 4, :], rhs=aT[:, 5, :], start=True,
                                                  stop=True), R=[BaT], W=[B_PS[4]])
                    rseg = scrA[:, 0:1024].rearrange("p (h i) -> p h i", h=8)
                    Ex = scrA[:, 1024:2048].rearrange("p (h i) -> p h i", h=8)
                    P.op("pool", lambda e: e.tensor_tensor(out=rseg, in0=U32.unsqueeze(1).to_broadcast([128, 8, 128]),
                                                          in1=dta[:, 8 * g:8 * g + 8].unsqueeze(2).to_broadcast([128, 8, 128]),
                                                          op=ALU.mult), R=[B_cst, B_sm], W=[B_A[0]])
                    yield
                    P.op("dve", lambda e: e.tensor_tensor(out=cbm[:], in0=PS[4][:, 0:128], in1=U32, op=ALU.mult),
                         R=[B_PS[4], B_cst], W=[B_cbm])
                    for hb in range(2):
                        P.op("pe", lambda e, hb=hb: e.matmul(PS[2 + hb][:], lhsT=L32,
                                                             rhs=scrA[:, hb * 512:(hb + 1) * 512], start=True, stop=True),
                             R=[B_A[0], B_cst], W=[B_PS[2 + hb]])
                    yield
                    for hb in range(2):
                        P.op("act", lambda e, hb=hb: e.activation(out=scrA[:, 1024 + hb * 512:1024 + (hb + 1) * 512],
                                                                  in_=PS[2 + hb][:], func=AF.Exp),
                             R=[B_PS[2 + hb]], W=[B_A[1]])
                        yield
                    MT = scrC[:, 0:1024].rearrange("p (h i) -> p h i", h=8)
                    P.op("dve", lambda e: e.tensor_tensor(out=MT, in0=Ex, in1=cbm[:].unsqueeze(1).to_broadcast([128, 8, 128]),
                                                          op=ALU.mult), R=[B_A[1], B_cbm], W=[B_C[0]])
                    P.op("pe", lambda e: e.matmul(PS[4][:], lhsT=bt[:], rhs=xw_[:], start=True, stop=True),
                         R=[Bbt, Bxw], W=[B_PS[4]])
                    yield
                    for hd in range(8):
                        P.op("pe", lambda e, hd=hd: e.matmul(PS[2][:, hd * 64:(hd + 1) * 64], lhsT=MT[:, hd, :],
                                                             rhs=xd[:, hd * 64:(hd + 1) * 64], start=True, stop=False),
                             R=[B_C[0], Bxd], W=[B_PS[2]])
                        P.op("pe", lambda e, hd=hd: e.matmul(PS[2][:, hd * 64:(hd + 1) * 64], lhsT=ident,
                                                             rhs=xs_[:, hd * 64:(hd + 1) * 64], start=False, stop=True),
                             R=[B_cst, Bxs], W=[B_PS[2]])
                    P.op("pe", lambda e: e.matmul(PS[3][:], lhsT=aT[:, 5, :], rhs=scrD[:, g * 512:(g + 1) * 512],
                                                  start=True, stop=True), R=[BaT, B_D[g]], W=[B_PS[3]])
                    yield
                    Sg = St[:, g * 512:(g + 1) * 512]
                    P.op("pool", lambda e: e.tensor_tensor(out=Sg.rearrange("p (h d) -> p h d", h=8),
                                                          in0=Sg.rearrange("p (h d) -> p h d", h=8),
                                                          in1=cd[:, 8 * g:8 * g + 8].unsqueeze(2).to_broadcast([128, 8, 64]),
                                                          op=ALU.mult), R=[B_St[g], B_sm], W=[B_St[g]])
                    yield
                    P.op("dve", lambda e: e.tensor_tensor(out=Sg, in0=PS[4][:], in1=Sg, op=ALU.add),
                         R=[B_PS[4], B_St[g]], W=[B_St[g]])
                    yield
                    ty = scrB[:, 0:512]
                    yy = scrB[:, 512:1024]
                    P.op("dve", lambda e: e.tensor_tensor(out=ty.rearrange("p (h d) -> p h d", h=8),
                                                          in0=PS[3][:].rearrange("p (h d) -> p h d", h=8),
                                                          in1=ea[:, 8 * g:8 * g + 8].unsqueeze(2).to_broadcast([128, 8, 64]),
                                                          op=ALU.mult), R=[B_PS[3], B_sm], W=[B_B[0]])
                    P.op("act", lambda e: e.activation(out=scrD[:, g * 512:(g + 1) * 512], in_=Sg, func=AF.Copy),
                         R=[B_St[g]], W=[B_D[g]])
                    yield
                    P.op("dve", lambda e: e.tensor_tensor(out=yy, in0=PS[2][:], in1=ty, op=ALU.add),
                         R=[B_PS[2], B_B[0]], W=[B_B[1]])
                    yield
                    P.op("pool", lambda e: e.tensor_tensor(out=yy, in0=yy, in1=szz, op=ALU.mult), R=[B_B[1], Bsz],
                         W=[B_B[1]])
                    yield
                    yn = scrC[:, 1024:1536]
                    ynT = scrC[:, 1536:2048].rearrange("p (t n) -> p t n", t=4)
                    P.op("act", lambda e: e.activation(out=yn, in_=yy, func=AF.Square,
                                                       accum_out=stt[:, 8 + 3 * g:9 + 3 * g]), R=[B_B[1], bst],
                         W=[B_C[1], bst])
                    yield
                    rstd_from_ss(stt, 8 + 3 * g, 512, bst)
                    yield
                    P.op("act", lambda e: e.activation(out=yn, in_=yy, func=AF.Copy, scale=stt[:, 10 + 3 * g:11 + 3 * g]),
                         R=[B_B[1], bst], W=[B_C[1]])
                    yield
                    for half in range(2):
                        for t in range(2):
                            tt = 2 * half + t
                            P.op("pe", lambda e, t=t, tt=tt: e.transpose(out=PT[:, 5 + t, :], in_=yn[:, tt * 128:(tt + 1) * 128],
                                                                         identity=ident), R=[B_C[1], B_cst], W=[B_PT])
                        P.op("dve", lambda e, half=half: e.tensor_tensor(
                            out=ynT[:, 2 * half:2 * half + 2, :], in0=PT[:, 5:7, :],
                            in1=LV[:, GW + 4 * g + 2 * half:GW + 4 * g + 2 * half + 2].unsqueeze(2).to_broadcast([128, 2, 128]),
                            op=ALU.mult), R=[B_PT, B_LV], W=[B_C[2]])
                        yield
                    for nb in range(2):
                        for t in range(4):
                            P.op("pe", lambda e, nb=nb, t=t: e.matmul(PS[5 + nb][:], lhsT=ynT[:, t, :],
                                                                     rhs=W_out[:, 4 * g + t, nb * 512:(nb + 1) * 512],
                                                                     start=(g == 0 and t == 0), stop=(g == 3 and t == 3)),
                                 R=[B_C[2], B_WB], W=[B_PS[5 + nb]])
                    yield

                def interleave(*gens):
                    gens = list(gens)
                    while gens:
                        for gn in list(gens):
                            try:
                                next(gn)
                            except StopIteration:
                                gens.remove(gn)

                interleave(stageA(0))
                interleave(stageA(1), stageB(0))
                interleave(stageA(2), stageB(1))
                interleave(stageA(3), stageB(2))
                interleave(stageB(3))
                for nb in range(2):
                    sl = slice(nb * 512, (nb + 1) * 512)
                    P.op("dve", lambda e, nb=nb, sl=sl: e.tensor_tensor(out=hh[:, sl], in0=PS[5 + nb][:], in1=hh[:, sl],
                                                                        op=ALU.add), R=[B_PS[5 + nb], bh], W=[bh])
                ple(hh, bh, stt, bst, pT_t[slot], B_pT[slot], PNW)
                store_chunk(L, c)

        def rope_tables():
            posf = kf[:, 0, 0:32]
            ivf = kf[:, 0, 32:64]
            pi32 = kf[:, 0, 64:96].bitcast(I32)
            P.dma("sp", lambda e: [e.dma_start(out=pi32, in_=pos_d)], "pos", 1, W=[B_kf])
            P.dma("sp", lambda e: [e.dma_start(out=ivf, in_=invf_d)], "invf", 1, W=[B_kf])
            P.op("dve", lambda e: e.tensor_copy(out=posf, in_=pi32), R=[B_kf], W=[B_kf])
            ang = scrA[:, 0:1024].rearrange("p (c f) -> p c f", c=32)
            red = scrA[:, 1024:2048].rearrange("p (c f) -> p c f", c=32)
            redi = scrB[:, 0:1024].bitcast(I32).rearrange("p (c f) -> p c f", c=32)
            redf = scrB[:, 1024:2048].rearrange("p (c f) -> p c f", c=32)
            P.op("dve", lambda e: e.tensor_tensor(out=ang, in0=posf.unsqueeze(2).to_broadcast([128, 32, 32]),
                                                  in1=ivf.unsqueeze(1).to_broadcast([128, 32, 32]), op=ALU.mult),
                 R=[B_kf], W=[B_A[0]])
            for which, dst in ((0, sinT), (1, cosT)):
                shift = 0.0 if which == 0 else float(np.pi / 2)
                P.op("dve", lambda e, shift=shift: e.tensor_scalar(out=red, in0=ang, scalar1=shift, scalar2=None, op0=ALU.add),
                     R=[B_A[0]], W=[B_A[1]])
                P.op("dve", lambda e: e.tensor_scalar(out=redi, in0=red, scalar1=float(1 / (2 * np.pi)), scalar2=None,
                                                      op0=ALU.mult), R=[B_A[1]], W=[B_B[0], B_B[1]])
                P.op("dve", lambda e: e.tensor_copy(out=redf, in_=redi), R=[B_B[0], B_B[1]], W=[B_B[2], B_B[3]])
                P.op("dve", lambda e: e.scalar_tensor_tensor(out=red, in0=redf, scalar=float(-2 * np.pi), in1=red,
                                                             op0=ALU.mult, op1=ALU.add), R=[B_B[2], B_B[3], B_A[1]], W=[B_A[1]])
                P.op("dve", lambda e: e.tensor_scalar(out=redf, in0=red, scalar1=float(np.pi), scalar2=float(-2 * np.pi),
                                                      op0=ALU.is_ge, op1=ALU.mult), R=[B_A[1]], W=[B_B[2], B_B[3]])
                P.op("dve", lambda e: e.tensor_tensor(out=red, in0=red, in1=redf, op=ALU.add), R=[B_A[1], B_B[2], B_B[3]],
                     W=[B_A[1]])
                P.op("dve", lambda e: e.tensor_scalar(out=redf, in0=red, scalar1=float(-np.pi), scalar2=float(2 * np.pi),
                                                      op0=ALU.is_lt, op1=ALU.mult), R=[B_A[1]], W=[B_B[2], B_B[3]])
                P.op("dve", lambda e: e.tensor_tensor(out=red, in0=red, in1=redf, op=ALU.add), R=[B_A[1], B_B[2], B_B[3]],
                     W=[B_A[1]])
                P.op("act", lambda e, dst=dst: e.activation(out=dst, in_=red, func=AF.Sin), R=[B_A[1]], W=B_St)

        def rope_apply(dst_bf, src, nh, c, R, W):
            s3 = src.rearrange("p (h d) -> p h d", h=nh)
            d3 = dst_bf.rearrange("p (h d) -> p h d", h=nh)
            n = nh * 32
            ta = (scrA[:, 1024:1024 + n] if nh == 16 else kf[:, 1, 0:n]).rearrange("p (h d) -> p h d", h=nh)
            tb = (scrA[:, 1536:1536 + n] if nh == 16 else kf[:, 1, 128:128 + n]).rearrange("p (h d) -> p h d", h=nh)
            Bt = [B_A[1]] if nh == 16 else [B_kf]
            cosb = cosT[:, c, :].unsqueeze(1).to_broadcast([128, nh, 32])
            sinb = sinT[:, c, :].unsqueeze(1).to_broadcast([128, nh, 32])
            x1 = s3[:, :, 0:32]
            x2 = s3[:, :, 32:64]
            P.op("dve", lambda e: e.tensor_tensor(out=ta, in0=x1, in1=cosb, op=ALU.mult), R=R + B_St, W=Bt)
            P.op("dve", lambda e: e.tensor_tensor(out=tb, in0=x2, in1=sinb, op=ALU.mult), R=R + B_St, W=Bt)
            P.op("dve", lambda e: e.tensor_tensor(out=d3[:, :, 0:32], in0=ta, in1=tb, op=ALU.subtract), R=Bt, W=W)
            P.op("dve", lambda e: e.tensor_tensor(out=ta, in0=x2, in1=cosb, op=ALU.mult), R=R + B_St, W=Bt)
            P.op("dve", lambda e: e.tensor_tensor(out=tb, in0=x1, in1=sinb, op=ALU.mult), R=R + B_St, W=Bt)
            P.op("dve", lambda e: e.tensor_tensor(out=d3[:, :, 32:64], in0=ta, in1=tb, op=ALU.add), R=Bt, W=W)

        def head_norm(src, nh, woff, stt, bst, scol, sqbuf, Bsq, R):
            s3 = src.rearrange("p (h d) -> p h d", h=nh)
            q3 = sqbuf.rearrange("p (h d) -> p h d", h=nh)
            P.op("dve", lambda e: e.tensor_tensor(out=sqbuf, in0=src, in1=src, op=ALU.mult), R=R, W=Bsq)
            P.op("dve", lambda e: e.tensor_reduce(out=stt[:, scol:scol + nh], in_=q3, axis=AX.X, op=ALU.add),
                 R=Bsq, W=[bst])
            P.op("act", lambda e: e.activation(out=stt[:, scol + nh:scol + 2 * nh], in_=stt[:, scol:scol + nh], func=AF.Ln,
                                               scale=1.0 / 64, bias=lsm[:, 2, 0:1]), R=[bst, B_lsm], W=[bst])
            P.op("act", lambda e: e.activation(out=stt[:, scol + 2 * nh:scol + 3 * nh], in_=stt[:, scol + nh:scol + 2 * nh],
                                               func=AF.Exp, scale=-0.5), R=[bst], W=[bst])
            P.op("dve", lambda e: e.tensor_tensor(out=s3, in0=s3,
                                                  in1=stt[:, scol + 2 * nh:scol + 3 * nh].unsqueeze(2).to_broadcast([128, nh, 64]),
                                                  op=ALU.mult), R=R + [bst], W=R)
            P.op("dve", lambda e: e.tensor_tensor(out=s3, in0=s3,
                                                  in1=LV[:, woff:woff + 64].unsqueeze(1).to_broadcast([128, nh, 64]),
                                                  op=ALU.mult), R=R + [B_LV], W=R)

        def attn_layer(L):
            j = L - 2
            layer_common_prep(L)
            ANW, PNW, KVNW, KNW, QNW, SNK = 0, 8, 16, 32, 96, 160
            pairs = [(WB[:, 0:16384], w_ai_d[j], 4), (WB[:, 16384:24576], w_ao_d[j], 2),
                     (WB[:, 57600:65792], w_g_d[L], 2), (WB[:, 65792:67840], w_p_d[L], 1)]
            if j == 0:
                pairs.append((WB[:, 24576:28672], w_kv_d, 1))
            load_weights(pairs, L)
            if j == 0:
                rope_tables()
                Vflat = WB[:, 36864:36864 + 8320].rearrange("p (n e) -> p n e", e=65)
                P.op("dve", lambda e: e.memset(Vflat[:, :, 64:65], 1.0), R=[B_WB], W=[B_Vones])
            P.op("pool", lambda e: e.memset(scrD[:], 0.0), W=B_D)
            P.op("act", lambda e: e.activation(out=lsm[:, 1, 0:16], in_=LV[:, SNK:SNK + 16], func=AF.Exp), R=[B_LV],
                 W=[B_lsm])
            load_chunk(L, 0)
            for c in range(nch):
                if c + 1 < nch:
                    load_chunk(L, c + 1)
                slot = c % 2
                hh, bh, stt, bst = h_t[slot], B_h[slot], stat[slot], B_stat[slot]
                P.op("pool", lambda e, stt=stt: e.memset(stt[:], 0.0), W=[bst])
                if j == 0:
                    norm_T(hh, bh, stt, bst, 0, KVNW)
                    for k in range(8):
                        P.op("pe", lambda e, k=k: e.matmul(PS[0][:], lhsT=uT[:, k, :], rhs=W_kv[:, k, :], start=(k == 0),
                                                           stop=(k == 7)), R=[B_uT, B_WB], W=[B_PS[0]])
                    kfl = kf[:, 0, :]
                    P.op("act", lambda e: e.activation(out=kfl, in_=PS[0][:, 0:256], func=AF.Copy), R=[B_PS[0]], W=[B_kf])
                    P.op("act", lambda e, c=c: e.activation(out=Vst[:, c, :, 0:64],
                                                            in_=PS[0][:, 256:512].rearrange("p (g d) -> p g d", g=4),
                                                            func=AF.Copy), R=[B_PS[0], B_WB], W=[B_V[c]])
                    head_norm(kfl, 4, KNW, stt, bst, 24, kf[:, 2, :], [B_kf], [B_kf])
                    kr = krbuf
                    rope_apply(kr, kfl, 4, c, [B_kf], [B_kr])
                    for t in range(2):
                        P.op("pe", lambda e, t=t: e.transpose(out=PT[:, t, :], in_=kr[:, t * 128:(t + 1) * 128],
                                                              identity=ident), R=[B_kr, B_cst], W=[B_PT])
                    P.op("act", lambda e, c=c: e.activation(out=KT[:, :, c * 128:(c + 1) * 128], in_=PT[:, 0:2, :],
                                                            func=AF.Copy), R=[B_PT, B_WB], W=[B_KT[c]])
                norm_T(hh, bh, stt, bst, 0, ANW, have_rstd=(j == 0))
                for nb in range(4):
                    bank = 1 + nb
                    for k in range(8):
                        P.op("pe", lambda e, nb=nb, k=k, bank=bank: e.matmul(PS[bank][:], lhsT=uT[:, k, :],
                                                                             rhs=W_ai[:, k, nb * 512:(nb + 1) * 512],
                                                                             start=(k == 0), stop=(k == 7)),
                             R=[B_uT, B_WB], W=[B_PS[bank]])
                qf = scrA[:, 0:1024]
                for nb in range(2):
                    P.op("act", lambda e, nb=nb: e.activation(out=qf[:, nb * 512:(nb + 1) * 512], in_=PS[1 + nb][:],
                                                              func=AF.Copy), R=[B_PS[1 + nb]], W=[B_A[0]])
                head_norm(qf, 16, QNW, stt, bst, 8, scrA[:, 1024:2048], [B_A[1]], [B_A[0]])
                qr = scrC[:, 0:1024]
                rope_apply(qr, qf, 16, c, [B_A[0]], [B_C[0]])
                qT2 = scrD[:].rearrange("p (t a n) -> p t a n", t=8, a=2)
                for t in range(8):
                    P.op("pe", lambda e, t=t: e.transpose(out=PT[:, t, :], in_=qr[:, t * 128:(t + 1) * 128], identity=ident),
                         R=[B_C[0], B_cst], W=[B_PT])
                P.op("act", lambda e: e.activation(out=qT2[0:64, :, 0, :], in_=PT[0:64, :, :], func=AF.Copy), R=[B_PT],
                     W=[B_D[0], B_D[1]])
                P.op("act", lambda e: e.activation(out=qT2[64:128, :, 1, :], in_=PT[64:128, :, :], func=AF.Copy), R=[B_PT],
                     W=[B_D[2], B_D[3]])
                blks = [c - 1, c] if c > 0 else [c]
                nb_ = len(blks)
                obank = [0, 1, 2]
                for r in range(8):
                    sb_ = 5 + (r % 2)
                    es = r % 2
                    for hh_ in range(2):
                        hp = 2 * r + hh_
                        g = PERM[hp] // 4
                        half = hp % 2
                        pr = slice(half * 64, (half + 1) * 64)
                        for bi, blk in enumerate(blks):
                            col = (hh_ * 2 + bi) * 128
                            P.op("pe", lambda e, g=g, half=half, blk=blk, col=col, r=r, sb_=sb_: e.matmul(
                                PS[sb_][:, col:col + 128], lhsT=KT[:, g // 2, blk * 128:(blk + 1) * 128],
                                rhs=qT2[:, r, half, :], start=True, stop=True),
                                 R=[B_KT[blk]] + B_D, W=[B_PS[sb_]])
                    Ev = Eatt[:, es, :].rearrange("p (a b n) -> p a b n", a=2, b=2)
                    Pv = PS[sb_][:].rearrange("p (a b n) -> p a b n", a=2, b=2)
                    if nb_ == 2:
                        P.op("act", lambda e, es=es, sb_=sb_: e.activation(out=Eatt[:, es, :], in_=PS[sb_][:], func=AF.Exp,
                                                                           scale=0.125), R=[B_PS[sb_]], W=[B_Eatt[es]])
                        msk = cbf[:, 1:3, :]
                        P.op("dve", lambda e, Ev=Ev: e.tensor_tensor(out=Ev[:, :, 0, :], in0=Ev[:, :, 0, :],
                                                                     in1=Lbf.unsqueeze(1).to_broadcast([128, 2, 128]),
                                                                     op=ALU.mult), R=[B_Eatt[es], B_cst], W=[B_Eatt[es]])
                        P.op("dve", lambda e, Ev=Ev: e.tensor_tensor(out=Ev[:, :, 1, :], in0=Ev[:, :, 1, :],
                                                                     in1=Ubf.unsqueeze(1).to_broadcast([128, 2, 128]),
                                                                     op=ALU.mult), R=[B_Eatt[es], B_cst], W=[B_Eatt[es]])
                    else:
                        P.op("act", lambda e, Ev=Ev, Pv=Pv: e.activation(out=Ev[:, :, 0, :], in_=Pv[:, :, 0, :], func=AF.Exp,
                                                                         scale=0.125), R=[B_PS[sb_]], W=[B_Eatt[es]])
                        P.op("dve", lambda e, Ev=Ev: e.tensor_tensor(out=Ev[:, :, 0, :], in0=Ev[:, :, 0, :],
                                                                     in1=Ubf.unsqueeze(1).to_broadcast([128, 2, 128]),
                                                                     op=ALU.mult), R=[B_Eatt[es], B_cst], W=[B_Eatt[es]])
                    for hh_ in range(2):
                        hp = 2 * r + hh_
                        g = PERM[hp] // 4
                        ob = hp // 6
                        oc = (hp % 6) * 65
                        for bi, blk in enumerate(blks):
                            P.op("pe", lambda e, hh_=hh_, bi=bi, blk=blk, g=g, ob=ob, oc=oc, Ev=Ev: e.matmul(
                                PS[ob][:, oc:oc + 65], lhsT=Ev[:, hh_, bi, :], rhs=Vst[:, blk, g, :],
                                start=(bi == 0), stop=(bi == nb_ - 1)),
                                 R=[B_Eatt[es], B_V[blk], B_Vones], W=[B_PS[ob]])
                den = kf[:, 1, 0:16]
                rden = kf[:, 1, 16:32]
                o_ = scrB[:, 0:1024].rearrange("p (h d) -> p h d", h=16)
                for ob in range(3):
                    nh = 6 if ob < 2 else 4
                    pv = PS[ob][:, 0:nh * 65].rearrange("p (h e) -> p h e", e=65)
                    hs = slice(ob * 6, ob * 6 + nh)
                    P.op("dve", lambda e, pv=pv, hs=hs, nh=nh: e.tensor_tensor(
                        out=den[:, hs].unsqueeze(2), in0=pv[:, :, 64:65], in1=lsm[:, 1, hs].unsqueeze(2), op=ALU.add),
                         R=[B_PS[ob], B_lsm], W=[B_kf])
                P.op("dve", lambda e: e.reciprocal(out=rden, in_=den), R=[B_kf], W=[B_kf])
                for ob in range(3):
                    nh = 6 if ob < 2 else 4
                    pv = PS[ob][:, 0:nh * 65].rearrange("p (h e) -> p h e", e=65)
                    hs = slice(ob * 6, ob * 6 + nh)
                    P.op("dve", lambda e, pv=pv, hs=hs, nh=nh: e.tensor_tensor(
                        out=o_[:, hs, :], in0=pv[:, :, 0:64], in1=rden[:, hs].unsqueeze(2).to_broadcast([128, nh, 64]),
                        op=ALU.mult), R=[B_PS[ob], B_kf], W=[B_B[0], B_B[1]])
                sg = scrB[:, 1024:2048]
                for nb in range(2):
                    P.op("act", lambda e, nb=nb: e.activation(out=sg[:, nb * 512:(nb + 1) * 512], in_=PS[3 + nb][:],
                                                              func=AF.Silu), R=[B_PS[3 + nb]], W=[B_B[2], B_B[3]])
                og = scrC[:, 1024:2048]
                P.op("dve", lambda e: e.tensor_tensor(out=og, in0=scrB[:, 0:1024], in1=sg, op=ALU.mult),
                     R=[B_B[0], B_B[1], B_B[2], B_B[3]], W=[B_C[1], B_C[2]])
                ogT = uT
                for t in range(8):
                    P.op("pe", lambda e, t=t: e.transpose(out=PT[:, t, :], in_=og[:, t * 128:(t + 1) * 128], identity=ident),
                         R=[B_C[1], B_C[2], B_cst], W=[B_PT])
                P.op("act", lambda e: e.activation(out=ogT[:], in_=PT[:], func=AF.Copy), R=[B_PT], W=[B_uT])
                for nb in range(2):
                    for k in range(8):
                        P.op("pe", lambda e, nb=nb, k=k: e.matmul(PS[5 + nb][:], lhsT=ogT[:, k, :],
                                                                 rhs=W_ao[:, k, nb * 512:(nb + 1) * 512], start=(k == 0),
                                                                 stop=(k == 7)), R=[B_uT, B_WB], W=[B_PS[5 + nb]])
                for nb in range(2):
                    sl = slice(nb * 512, (nb + 1) * 512)
                    P.op("dve", lambda e, nb=nb, sl=sl: e.tensor_tensor(out=hh[:, sl], in0=PS[5 + nb][:], in1=hh[:, sl],
                                                                        op=ALU.add), R=[B_PS[5 + nb], bh], W=[bh])
                ple(hh, bh, stt, bst, pT_t[slot], B_pT[slot], PNW)
                store_chunk(L, c)

        for L in layers:
            if L < 2:
                mamba_layer(L)
            else:
                attn_layer(L)
        finals = [hstores[(layers[-1], c)] for c in range(nch)]
        P.emit(final_waits=finals)
    return nc


def _ktile(w):
    K, N = w.shape
    return np.ascontiguousarray(w.reshape(K // 128, 128, N).transpose(1, 0, 2).reshape(128, (K // 128) * N))


def _rep(v):
    return np.broadcast_to(np.asarray(v, np.float32).reshape(1, -1), (128, v.size))


def prepare_inputs(x, p, positions, ssm_norm_w, ssm_in_w, ssm_conv_w, ssm_conv_b, ssm_dt_bias, ssm_a_log, ssm_d,
                   ssm_gnorm_w, ssm_out_w, kv_norm_w, kv_w, k_norm_w, attn_norm_w, attn_in_w, q_norm_w, attn_sinks,
                   attn_out_w, ple_norm_w, ple_gate_w, ple_proj_w):
    f = np.float32
    x = np.asarray(x, f)
    p = np.asarray(p, f)
    positions = np.asarray(positions, np.int32)
    cst = np.zeros((128, 4, 128), f)
    k = np.arange(128)
    cst[:, 0, :] = np.eye(128)
    cst[:, 1, :] = (k[:, None] <= k[None, :])
    cst[:, 2, :] = (k[:, None] > k[None, :])
    cst[:, 3, :] = 1.0
    invf = (10000.0 ** (-(np.arange(32, dtype=np.float32) * 2.0 / 64))).astype(f)
    invf = np.ascontiguousarray(_rep(invf))
    qperm = np.concatenate([np.arange(h * 64, (h + 1) * 64) for h in PERM])
    w_in = np.stack([_ktile(np.asarray(ssm_in_w[l], f)) for l in range(2)])
    w_out = np.stack([_ktile(np.asarray(ssm_out_w[l], f)) for l in range(2)])
    w_g = np.stack([_ktile(np.asarray(ple_gate_w[i], f)) for i in range(4)])
    w_p = np.stack([_ktile(np.asarray(ple_proj_w[i], f)) for i in range(4)])
    w_kv = _ktile(np.asarray(kv_w, f))
    ai = []
    ao = []
    for j in range(2):
        w = np.asarray(attn_in_w[j], f)
        w = np.concatenate([w[:, :1024][:, qperm], w[:, 1024:][:, qperm]], axis=1)
        ai.append(_ktile(w))
        ao.append(_ktile(np.asarray(attn_out_w[j], f)[qperm, :]))
    w_ai = np.stack(ai)
    w_ao = np.stack(ao)
    lv = np.zeros((4, 128, LVW), f)

    def fm(v):
        v = np.asarray(v, f)
        return v.reshape(-1, 128).T

    for l in range(2):
        lv[l, :, 0:8] = fm(ssm_norm_w[l])
        lv[l, :, 8:16] = fm(ple_norm_w[l])
        lv[l, :, 16:32] = fm(ssm_gnorm_w[l])
        cw = np.asarray(ssm_conv_w[l], f)
        lv[l, :, 32:128] = cw.reshape(4, 24, 128).transpose(2, 1, 0).reshape(128, 96)
        lv[l, :, 128:152] = np.asarray(ssm_conv_b[l], f).reshape(24, 128).T
        lv[l, :, 152:184] = _rep(np.asarray(ssm_dt_bias[l], f))
        lv[l, :, 184:216] = _rep(np.asarray(ssm_a_log[l], f))
        lv[l, :, 216:248] = _rep(np.asarray(ssm_d[l], f))
    for j in range(2):
        L = 2 + j
        lv[L, :, 0:8] = fm(attn_norm_w[j])
        lv[L, :, 8:16] = fm(ple_norm_w[L])
        lv[L, :, 16:24] = fm(kv_norm_w)
        lv[L, :, 32:96] = _rep(np.asarray(k_norm_w, f))
        lv[L, :, 96:160] = _rep(np.asarray(q_norm_w[j], f))
        lv[L, :, 160:176] = _rep(np.asarray(attn_sinks[j], f)[PERM])
    shared = dict(cst=cst, invf=invf, w_in=w_in, w_out=w_out, w_g=w_g, w_p=w_p, w_kv=w_kv, w_ai=w_ai, w_ao=w_ao, lv=lv)
    in_maps = []
    for b in range(x.shape[0]):
        m = dict(shared)
        m["x"] = np.ascontiguousarray(x[b])
        m["pT"] = np.ascontiguousarray(p[:, b].transpose(0, 2, 1))
        m["pos"] = np.ascontiguousarray(positions[b].reshape(32, 128).T)
        in_maps.append(m)
    return in_maps


_NC_CACHE = {}
LAUNCHES = [(0, 1, 2, 3)]


def kernel(**inputs):
    in_maps = prepare_inputs(**inputs)
    n = len(in_maps)
    outs = None
    for grp in LAUNCHES:
        if grp not in _NC_CACHE:
            _NC_CACHE[grp] = build(layers=grp)
        nc = _NC_CACHE[grp]
        if outs is not None:
            for b in range(n):
                in_maps[b]["x"] = outs[b]
        res = run_bass_kernel_spmd(nc, in_maps, core_ids=list(range(n)))
        outs = [np.ascontiguousarray(np.asarray(r["out"], np.float32)) for r in res.results]
    return np.stack(outs, axis=0)
```

```python
import contextlib
import numpy as np
import concourse.bass as bass
import concourse.mybir as mybir
from concourse.bass_utils import run_bass_kernel_spmd

F32 = mybir.dt.float32
BF16 = mybir.dt.bfloat16
I32 = mybir.dt.int32
AF = mybir.ActivationFunctionType
ALU = mybir.AluOpType
AX = mybir.AxisListType

EPOCH = 512
SCHEDULE = True
D = 1024
S = 4096
NCH = 32
EPS = 1e-6
SSM_IN = 5152
LVW = 256
PERM = [0, 4, 1, 5, 2, 6, 3, 7, 8, 12, 9, 13, 10, 14, 11, 15]


class Buf:
    __slots__ = ("name", "w", "wd", "rs", "rd", "psum")

    def __init__(self, name, psum=False):
        self.name = name
        self.w = {}
        self.wd = []
        self.rs = {}
        self.rd = []
        self.psum = psum


class Ins:
    __slots__ = ("eng", "fn", "deps", "sig", "dma", "key", "val", "sem", "n", "idx", "odeps", "dur", "end")

    def __init__(self, eng, fn, dma=False, key=None):
        self.eng = eng
        self.fn = fn
        self.deps = []
        self.sig = False
        self.dma = dma
        self.key = key
        self.val = None
        self.sem = None
        self.odeps = []
        self.dur = 0.3
        self.end = 0.0


def _est_dur(eng, calls):
    name, a, k = calls[0]
    out = k.get("out", a[0] if a else None)
    try:
        shp = out.shape
        n = 1
        for d_ in shp[1:]:
            n *= d_
    except Exception:
        n = 256
    if eng == "pe":
        f32 = False
        try:
            f32 = (k.get("lhsT").dtype == F32)
        except Exception:
            pass
        return 0.07 + n / 2000.0 * (4.0 if f32 else 1.0)
    if eng == "act":
        return 0.25 + n / 1200.0
    if eng == "dve":
        return 0.12 + n / 1100.0
    if eng == "pool":
        return 0.2 + n / 450.0
    return 0.3


class _Rec:
    def __init__(self):
        self.calls = []

    def __getattr__(self, name):
        def f(*a, **k):
            self.calls.append((name, a, k))
            return None
        return f


def _replay(calls, eobj):
    return [getattr(eobj, name)(*a, **k) for name, a, k in calls]


class Prog:
    ENGS = ("pe", "act", "dve", "pool", "sp")

    def __init__(self, nc):
        self.nc = nc
        self.streams = {e: [] for e in self.ENGS}
        self.dma_cnt = {}
        self.nins = 0
        self._last_dma = {}

    def _track(self, ins, R, W):
        deps = ins.deps
        eng = ins.eng
        dma = ins.dma
        for b in R:
            deps.extend(b.w.values())
            deps.extend(b.wd)
            if b.psum:
                for e, xs in b.rs.items():
                    if e != eng:
                        deps.extend(xs)
        for b in W:
            for e, x in b.w.items():
                if dma or e != eng or eng != "pe":
                    deps.append(x)
                else:
                    ins.odeps.append(x)
            deps.extend(b.wd)
            for e, xs in b.rs.items():
                if dma or e != eng or eng != "pe":
                    deps.extend(xs)
            deps.extend(b.rd)
        for b in R:
            if dma:
                b.rd.append(ins)
            else:
                b.rs.setdefault(eng, []).append(ins)
        for b in W:
            b.rs = {}
            b.rd = []
            if dma:
                b.w = {}
                b.wd = [ins]
            else:
                b.w = {eng: ins}
                b.wd = []

    def op(self, eng, fn, R=(), W=()):
        rec = _Rec()
        fn(rec)
        assert len(rec.calls) == 1
        ins = Ins(eng, rec.calls)
        ins.dur = _est_dur(eng, rec.calls)
        ins.idx = self.nins
        self._track(ins, R, W)
        self.streams[eng].append(ins)
        self.nins += 1
        return ins

    def dma(self, eng, fn, key, n, R=(), W=()):
        rec = _Rec()
        fn(rec)
        assert len(rec.calls) == n, (len(rec.calls), n)
        ins = Ins(eng, rec.calls, dma=True, key=key)
        ins.n = n
        ins.idx = self.nins
        ins.dur = 2.5 * n
        self._track(ins, R, W)
        if self._last_dma.get(eng) is not None:
            ins.odeps.append(self._last_dma[eng])
        self._last_dma[eng] = ins
        c = self.dma_cnt.get(key, 0) + n
        self.dma_cnt[key] = c
        ins.val = 16 * c
        self.streams[eng].append(ins)
        self.nins += 1
        return ins

    def schedule(self):
        import heapq
        HOP = 0.35
        allins = [i for e in self.ENGS for i in self.streams[e]]
        succ = {}
        nrem = {}
        for i in allins:
            ds = set(id(d) for d in i.deps) | set(id(d) for d in i.odeps)
            nrem[id(i)] = len(ds)
            seen = set()
            for d in list(i.deps) + list(i.odeps):
                if id(d) in seen:
                    continue
                seen.add(id(d))
                succ.setdefault(id(d), []).append(i)
        free = {e: 0.0 for e in self.ENGS}
        ready_t = {}
        heap = []
        for i in allins:
            if nrem[id(i)] == 0:
                ready_t[id(i)] = 0.0
                heapq.heappush(heap, (0.0, i.idx, i))
        order = {e: [] for e in self.ENGS}
        done = 0
        while heap:
            st, _, i = heapq.heappop(heap)
            real = max(free[i.eng], ready_t[id(i)])
            if real > st + 1e-9:
                heapq.heappush(heap, (real, i.idx, i))
                continue
            if i.dma:
                free[i.eng] = real + 0.1
            else:
                free[i.eng] = real + i.dur
            i.end = real + i.dur
            order[i.eng].append(i)
            done += 1
            for sx in succ.get(id(i), ()):
                r = max(ready_t.get(id(sx), 0.0), i.end + (HOP if sx.eng != i.eng or i.dma else 0.1))
                ready_t[id(sx)] = r
                nrem[id(sx)] -= 1
                if nrem[id(sx)] == 0:
                    heapq.heappush(heap, (max(r, free[sx.eng]), sx.idx, sx))
        assert done == len(allins), (done, len(allins))
        self.streams = order
        print(f"[prog] scheduled: modelled makespan {max(free.values()):.1f} us", flush=True)

    def emit(self, final_waits=()):
        nc = self.nc
        if SCHEDULE:
            self.schedule()
        pos = {}
        for e in self.ENGS:
            for k_, ins in enumerate(self.streams[e]):
                pos[id(ins)] = k_
        for e in self.ENGS:
            for ins in self.streams[e]:
                best = {}
                nd = []
                for d in ins.deps:
                    if d.dma:
                        nd.append(d)
                    else:
                        b_ = best.get(d.eng)
                        if b_ is None or pos[id(d)] > pos[id(b_)]:
                            best[d.eng] = d
                ins.deps = nd + list(best.values())
        for e in self.ENGS:
            for ins in self.streams[e]:
                for d in ins.deps:
                    if not d.dma:
                        d.sig = True
        nsig = {}
        for e in self.ENGS:
            s = 0
            for ins in self.streams[e]:
                if ins.sig and not ins.dma:
                    ins.sem = (e, s // EPOCH)
                    ins.val = s % EPOCH + 1
                    s += 1
            nsig[e] = s
        sem_names = []
        for e in self.ENGS:
            for k in range((nsig[e] + EPOCH - 1) // EPOCH):
                sem_names.append((e, k))
        for key in self.dma_cnt:
            sem_names.append(("dma", key))
        print(f"[prog] instructions={self.nins} sems={len(sem_names)} sig={nsig}", flush=True)
        with contextlib.ExitStack() as st:
            sems = {}
            for nm in sem_names:
                sems[nm] = st.enter_context(nc.semaphore(f"s_{nm[0]}_{nm[1]}"))
            block = st.enter_context(nc.Block())

            def run_stream(ename, eobj):
                waited = {}
                for ins in self.streams[ename]:
                    need = {}
                    for d in ins.deps:
                        k = ("dma", d.key) if d.dma else d.sem
                        v = d.val
                        if waited.get(k, 0) >= v:
                            continue
                        if need.get(k, 0) < v:
                            need[k] = v
                    for k, v in need.items():
                        eobj.wait_ge(sems[k], v)
                        waited[k] = v
                    if ins.dma:
                        for r in _replay(ins.fn, eobj):
                            r.then_inc(sems[("dma", ins.key)], 16)
                    else:
                        r = _replay(ins.fn, eobj)[0]
                        if ins.sig:
                            r.then_inc(sems[ins.sem], 1)
                if ename == "sp":
                    for d in final_waits:
                        eobj.wait_ge(sems[("dma", d.key)], d.val)

            @block.tensor
            def _(pe):
                run_stream("pe", pe)

            @block.scalar
            def _(act):
                run_stream("act", act)

            @block.vector
            def _(dve):
                run_stream("dve", dve)

            @block.gpsimd
            def _(pool):
                run_stream("pool", pool)

            @block.sync
            def _(sp):
                run_stream("sp", sp)


def build(layers=(0, 1, 2, 3), nch=NCH):
    nc = bass.Bass("TRN2", target_bir_lowering=False)

    def dram(name, shape, dt, kind="ExternalInput"):
        return nc.dram_tensor(name, shape, dt, kind=kind).ap()

    x_d = dram("x", [S, D], F32)
    pT_d = dram("pT", [4, 256, S], F32)
    pos_d = dram("pos", [128, 32], I32)
    cst_d = dram("cst", [128, 4, 128], F32)
    invf_d = dram("invf", [128, 32], F32)
    w_in_d = dram("w_in", [2, 128, 8 * SSM_IN], F32)
    w_out_d = dram("w_out", [2, 128, 16 * 1024], F32)
    w_g_d = dram("w_g", [4, 128, 8 * 1024], F32)
    w_p_d = dram("w_p", [4, 128, 2 * 1024], F32)
    w_kv_d = dram("w_kv", [128, 8 * 512], F32)
    w_ai_d = dram("w_ai", [2, 128, 8 * 2048], F32)
    w_ao_d = dram("w_ao", [2, 128, 8 * 1024], F32)
    lv_d = dram("lv", [4, 128, LVW], F32)
    out_d = dram("out", [S, D], F32, kind="ExternalOutput")

    st = contextlib.ExitStack()
    with st:
        def sb(name, shape, dt):
            return st.enter_context(nc.sbuf_tensor(name, shape, dt))

        def ps(name, shape, dt):
            return st.enter_context(nc.psum_tensor(name, shape, dt))

        P = Prog(nc)
        WB = sb("WB", [128, 67840], BF16)
        B_WB = Buf("WB")
        LV = sb("LV", [128, LVW], F32)
        B_LV = Buf("LV")
        cbf = sb("cbf", [128, 4, 128], BF16)
        c32 = sb("c32", [128, 4, 128], F32)
        B_cst = Buf("cst")
        ident = cbf[:, 0, :]
        Ubf = cbf[:, 1, :]
        Lbf = cbf[:, 2, :]
        U32 = c32[:, 1, :]
        L32 = c32[:, 2, :]
        ones32 = c32[:, 3, :]
        h_t = [sb(f"h{i}", [128, D], F32) for i in range(2)]
        B_h = [Buf(f"h{i}") for i in range(2)]
        pT_t = [sb(f"pTt{i}", [128, 2, 128], BF16) for i in range(2)]
        B_pT = [Buf(f"pTt{i}") for i in range(2)]
        stat = [sb(f"stat{i}", [128, 64], F32) for i in range(2)]
        B_stat = [Buf(f"stat{i}") for i in range(2)]
        u_bf = sb("u_bf", [128, D], BF16)
        B_u = Buf("u_bf")
        uT = sb("uT", [128, 8, 128], BF16)
        B_uT = Buf("uT")
        sm = sb("sm", [128, 16, 32], F32)
        B_sm = Buf("sm")
        lsm = sb("lsm", [128, 4, 32], F32)
        B_lsm = Buf("lsm")
        xraw = sb("xraw", [128, 6, 131], F32)
        B_xraw = Buf("xraw")
        B_xr = [Buf(f"xr{t}") for t in range(6)]
        B_ca = [Buf(f"ca{t}") for t in range(6)]
        ctail = sb("ctail", [128, 24, 3], F32)
        B_ctail = [Buf(f"ctail{g}") for g in range(4)]
        cacc = sb("cacc", [128, 6, 128], F32)
        B_cacc = Buf("cacc")
        actT = sb("actT", [128, 6, 128], BF16)
        B_actT = Buf("actT")
        xdt = sb("xdt", [128, 512], BF16)
        B_xdt = Buf("xdt")
        xsD = sb("xsD", [128, 512], BF16)
        B_xsD = Buf("xsD")
        xw = sb("xw", [128, 512], BF16)
        B_xw = Buf("xw")
        Btm = sb("Btm", [128, 128], BF16)
        B_Btm = Buf("Btm")
        cbm = sb("cbm", [128, 128], F32)
        B_cbm = Buf("cbm")
        scrA = sb("scrA", [128, 2048], F32)
        B_A = [Buf("scrA0"), Buf("scrA1")]
        scrB = sb("scrB", [128, 2048], F32)
        B_B = [Buf(f"scrB{i}") for i in range(4)]
        scrC = sb("scrC", [128, 2048], BF16)
        B_C = [Buf("scrC0"), Buf("scrC1"), Buf("scrC2")]
        scrD = sb("scrD", [128, 2048], BF16)
        B_D = [Buf(f"scrD{i}") for i in range(4)]
        St = sb("St", [128, 2048], F32)
        B_St = [Buf(f"St{i}") for i in range(4)]
        _cb16 = cacc[:].rearrange("p t n -> p (t n)").bitcast(BF16)
        Eatt = _cb16[:, 0:1024].rearrange("p (a n) -> p a n", a=2)
        B_Eatt = [Buf("Eatt0"), Buf("Eatt1")]
        krbuf = _cb16[:, 1024:1280]
        B_kr = Buf("kr")
        kf = xraw[:].rearrange("p t n -> p (t n)")[:, 0:768].rearrange("p (a b) -> p a b", a=3)
        B_kf = B_xraw
        actTb = sb("actTb", [128, 6, 128], BF16)
        xdtb = sb("xdtb", [128, 512], BF16)
        xsDb = sb("xsDb", [128, 512], BF16)
        xwb = sb("xwb", [128, 512], BF16)
        Btmb = sb("Btmb", [128, 128], BF16)
        actT2, xdt2, xsD2, xw2, Btm2 = [actT, actTb], [xdt, xdtb], [xsD, xsDb], [xw, xwb], [Btm, Btmb]
        B_actT2 = [B_actT, Buf("actTb")]
        B_xdt2 = [B_xdt, Buf("xdtb")]
        B_xsD2 = [B_xsD, Buf("xsDb")]
        B_xw2 = [B_xw, Buf("xwb")]
        B_Btm2 = [B_Btm, Buf("Btmb")]
        PT = ps("PT", [128, 8, 128], BF16)
        B_PT = Buf("PT", psum=True)
        PS = [ps(f"PS{i}", [128, 512], F32) for i in range(7)]
        B_PS = [Buf(f"PS{i}", psum=True) for i in range(7)]

        def wv(off, k, n):
            return WB[:, off:off + k * n].rearrange("p (k n) -> p k n", k=k)
        W_in = wv(0, 8, SSM_IN)
        W_out = wv(41216, 16, 1024)
        W_g = wv(57600, 8, 1024)
        W_p = wv(65792, 2, 1024)
        W_ai = wv(0, 8, 2048)
        W_ao = wv(16384, 8, 1024)
        W_kv = wv(24576, 8, 512)
        KT = wv(28672, 2, 4096)
        Vst = WB[:, 36864:36864 + 8320].rearrange("p (c g e) -> p c g e", c=32, g=4)
        B_KT = [Buf(f"KT{c}") for c in range(NCH)]
        B_V = [Buf(f"V{c}") for c in range(NCH)]
        B_Vones = Buf("Vones")
        cosT = St[:, 0:1024].rearrange("p (c f) -> p c f", c=32)
        sinT = St[:, 1024:2048].rearrange("p (c f) -> p c f", c=32)

        P.dma("sp", lambda e: [e.dma_start(out=c32[:], in_=cst_d)], "cst", 1, W=[B_cst])
        P.dma("pool", lambda e: [e.dma_start(out=cbf[:], in_=cst_d)], "cstb", 1, W=[B_cst])

        hstores = {}

        def load_weights(pairs, L):
            fns = []
            for dst, src, ns in pairs:
                n = dst.shape[1]
                step = n // ns
                for i in range(ns):
                    fns.append((dst[:, i * step:(i + 1) * step], src[:, i * step:(i + 1) * step]))
            P.dma("pool", lambda e, fns=fns: [e.dma_start(out=d_, in_=s_) for d_, s_ in fns], f"wload{L}", len(fns),
                  W=[B_WB])

        def rstd_from_ss(stt, col, n, bst):
            P.op("act", lambda e: e.activation(out=stt[:, col + 1:col + 2], in_=stt[:, col:col + 1], func=AF.Ln,
                                               scale=1.0 / n, bias=lsm[:, 2, 0:1]), R=[bst, B_lsm], W=[bst])
            P.op("act", lambda e: e.activation(out=stt[:, col + 2:col + 3], in_=stt[:, col + 1:col + 2], func=AF.Exp,
                                               scale=-0.5), R=[bst], W=[bst])

        def norm_T(hh, bh, stt, bst, col, nwoff, have_rstd=False, alt=None):
            ub, Bub, uTx, BuTx, pt0 = alt if alt is not None else (u_bf[:], B_u, uT[:], B_uT, None)
            if not have_rstd:
                P.op("act", lambda e: e.activation(out=ub, in_=hh[:], func=AF.Square,
                                                   accum_out=stt[:, col:col + 1]), R=[bh, bst], W=[Bub, bst])
                rstd_from_ss(stt, col, D, bst)
            P.op("act", lambda e: e.activation(out=ub, in_=hh[:], func=AF.Copy, scale=stt[:, col + 2:col + 3]),
                 R=[bh, bst], W=[Bub])
            if pt0 is None:
                for k in range(8):
                    P.op("pe", lambda e, k=k: e.transpose(out=PT[:, k, :], in_=ub[:, k * 128:(k + 1) * 128],
                                                          identity=ident), R=[Bub, B_cst], W=[B_PT])
                P.op("dve", lambda e: e.tensor_tensor(out=uTx, in0=PT[:],
                                                      in1=LV[:, nwoff:nwoff + 8].unsqueeze(2).to_broadcast([128, 8, 128]),
                                                      op=ALU.mult), R=[B_PT, B_LV], W=[BuTx])
            else:
                for hf in range(2):
                    for i in range(4):
                        k = 4 * hf + i
                        P.op("pe", lambda e, k=k, i=i: e.transpose(out=PT[:, pt0 + i, :], in_=ub[:, k * 128:(k + 1) * 128],
                                                                   identity=ident), R=[Bub, B_cst], W=[B_PT])
                    P.op("dve", lambda e, hf=hf: e.tensor_tensor(
                        out=uTx[:, 4 * hf:4 * hf + 4, :], in0=PT[:, pt0:pt0 + 4, :],
                        in1=LV[:, nwoff + 4 * hf:nwoff + 4 * hf + 4].unsqueeze(2).to_broadcast([128, 4, 128]),
                        op=ALU.mult), R=[B_PT, B_LV], W=[BuTx])

        def ple(hh, bh, stt, bst, pt, bpt, pnwoff, alt=None, banks=(0, 1, 2, 3), tv=None):
            norm_T(hh, bh, stt, bst, 3, pnwoff, alt=alt)
            uTx, BuTx = (alt[2], alt[3]) if alt is not None else (uT[:], B_uT)
            if tv is None:
                tg, vv, Btg, Bvv = scrA[:, 0:1024], scrA[:, 1024:2048], B_A[0], B_A[1]
            else:
                tg, vv, Btg, Bvv = tv
            for nb in range(2):
                for k in range(8):
                    P.op("pe", lambda e, nb=nb, k=k: e.matmul(PS[banks[nb]][:], lhsT=uTx[:, k, :],
                                                             rhs=W_g[:, k, nb * 512:(nb + 1) * 512],
                                                             start=(k == 0), stop=(k == 7)),
                         R=[BuTx, B_WB], W=[B_PS[banks[nb]]])
            for nb in range(2):
                for k in range(2):
                    P.op("pe", lambda e, nb=nb, k=k: e.matmul(PS[banks[2 + nb]][:], lhsT=pt[:, k, :],
                                                             rhs=W_p[:, k, nb * 512:(nb + 1) * 512],
                                                             start=(k == 0), stop=(k == 1)),
                         R=[bpt, B_WB], W=[B_PS[banks[2 + nb]]])
            for nb in range(2):
                sl = slice(nb * 512, (nb + 1) * 512)
                P.op("act", lambda e, nb=nb, sl=sl: e.activation(out=tg[:, sl], in_=PS[banks[nb]][:], func=AF.Tanh, scale=0.5),
                     R=[B_PS[banks[nb]]], W=[Btg])
                P.op("dve", lambda e, nb=nb, sl=sl: e.scalar_tensor_tensor(out=vv[:, sl], in0=tg[:, sl], scalar=1.0,
                                                                           in1=PS[banks[2 + nb]][:], op0=ALU.add, op1=ALU.mult),
                     R=[Btg, B_PS[banks[2 + nb]]], W=[Bvv])
            P.op("dve", lambda e: e.scalar_tensor_tensor(out=hh[:], in0=vv, scalar=0.5, in1=hh[:], op0=ALU.mult,
                                                         op1=ALU.add), R=[Bvv, bh], W=[bh])

        def load_chunk(L, c):
            slot = c % 2
            src = x_d if L == layers[0] else out_d
            deps = [] if L == layers[0] else [hstores[(L - 1, c)]]
            i1 = P.dma("sp", lambda e: [e.dma_start(out=h_t[slot][:], in_=src[c * 128:(c + 1) * 128, :])],
                       f"hld{slot}_{L}", 1, W=[B_h[slot]])
            i1.deps.extend(deps)
            P.dma("pool", lambda e: [e.dma_start(out=pT_t[slot][:],
                                                 in_=pT_d[L, :, c * 128:(c + 1) * 128].rearrange("(k p) t -> p k t", p=128))],
                  f"pld{slot}_{L}", 1, W=[B_pT[slot]])

        def store_chunk(L, c):
            slot = c % 2
            hstores[(L, c)] = P.dma("sp", lambda e: [e.dma_start(out=out_d[c * 128:(c + 1) * 128, :], in_=h_t[slot][:])],
                                    f"hst{slot}_{L}", 1, R=[B_h[slot]])

        def layer_common_prep(L):
            P.dma("sp", lambda e: [e.dma_start(out=LV[:], in_=lv_d[L])], f"lv{L}", 1, W=[B_LV])
            P.op("dve", lambda e: e.memset(lsm[:, 2, :], EPS), W=[B_lsm])

        def mamba_layer(L):
            layer_common_prep(L)
            load_weights([(WB[:, 0:41216], w_in_d[L], 8), (WB[:, 41216:57600], w_out_d[L], 4),
                          (WB[:, 57600:65792], w_g_d[L], 2), (WB[:, 65792:67840], w_p_d[L], 1)], L)
            NW, PNW, GW, CW, CB, DTB, ALOG, DSK = 0, 8, 16, 32, 128, 152, 184, 216
            cw = LV[:, CW:CW + 96].rearrange("p (t k) -> p t k", k=4)
            P.op("act", lambda e: e.activation(out=lsm[:, 0, :], in_=LV[:, ALOG:ALOG + 32], func=AF.Exp),
                 R=[B_LV], W=[B_lsm])
            P.op("dve", lambda e: e.tensor_scalar(out=lsm[:, 0, :], in0=lsm[:, 0, :], scalar1=-1.0, scalar2=None,
                                                  op0=ALU.mult), R=[B_lsm], W=[B_lsm])
            P.op("pool", lambda e: e.memset(St[:], 0.0), W=B_St)
            P.op("pool", lambda e: e.memset(scrD[:], 0.0), W=B_D)
            P.op("pool", lambda e: e.memset(ctail[:], 0.0), W=B_ctail)
            load_chunk(L, 0)
            for c in range(nch):
                if c + 1 < nch:
                    load_chunk(L, c + 1)
                slot = c % 2
                hh, bh, stt, bst = h_t[slot], B_h[slot], stat[slot], B_stat[slot]
                P.op("pool", lambda e, stt=stt: e.memset(stt[:], 0.0), W=[bst])
                norm_T(hh, bh, stt, bst, 0, NW)
                SMB = B_PS[1]
                for k in range(8):
                    P.op("pe", lambda e, k=k: e.matmul(PS[1][:, 0:32], lhsT=uT[:, k, :], rhs=W_in[:, k, 5120:5152],
                                                       start=(k == 0), stop=(k == 7)), R=[B_uT, B_WB], W=[SMB])
                dtr, dt_, dta, acs, tmp, dte, ea, cd, ee = (sm[:, i, :] for i in range(9))
                P.op("dve", lambda e: e.tensor_tensor(out=dtr, in0=PS[1][:, 0:32], in1=LV[:, DTB:DTB + 32], op=ALU.add),
                     R=[SMB, B_LV], W=[B_sm])
                P.op("act", lambda e: e.activation(out=ee, in_=dtr, func=AF.Exp), R=[B_sm], W=[B_sm])
                P.op("act", lambda e: e.activation(out=dt_, in_=ee, func=AF.Ln, bias=1.0), R=[B_sm], W=[B_sm])
                P.op("dve", lambda e: e.tensor_tensor(out=dta, in0=dt_, in1=lsm[:, 0, :], op=ALU.mult),
                     R=[B_sm, B_lsm], W=[B_sm])
                P.op("pe", lambda e: e.matmul(PS[1][:, 32:64], lhsT=U32, rhs=dta, start=True, stop=True),
                     R=[B_sm, B_cst], W=[SMB])
                P.op("pe", lambda e: e.matmul(PS[1][:, 64:96], lhsT=ones32, rhs=dta, start=True, stop=True),
                     R=[B_sm, B_cst], W=[SMB])
                P.op("dve", lambda e: e.tensor_copy(out=acs, in_=PS[1][:, 32:64]), R=[SMB], W=[B_sm])
                P.op("dve", lambda e: e.tensor_tensor(out=tmp, in0=PS[1][:, 64:96], in1=acs, op=ALU.subtract),
                     R=[SMB, B_sm], W=[B_sm])
                P.op("act", lambda e: e.activation(out=dte, in_=tmp, func=AF.Exp), R=[B_sm], W=[B_sm])
                P.op("act", lambda e: e.activation(out=ea, in_=acs, func=AF.Exp), R=[B_sm], W=[B_sm])
                P.op("act", lambda e: e.activation(out=cd, in_=PS[1][:, 64:96], func=AF.Exp), R=[SMB], W=[B_sm])
                def stageA(g):
                    q = g % 2
                    aT, xd, xs_, xw_, bt = actT2[q], xdt2[q], xsD2[q], xw2[q], Btm2[q]
                    BaT, Bxd, Bxs, Bxw, Bbt = B_actT2[q], B_xdt2[q], B_xsD2[q], B_xw2[q], B_Btm2[q]
                    cols = [2048 + 512 * g + 128 * t for t in range(4)] + [4096 + 128 * g, 4608 + 128 * g]
                    tiles = [4 * g + t for t in range(4)] + [16 + g, 20 + g]
                    for t in range(6):
                        bank, pos = (0, t) if t < 4 else (1, t - 4)
                        for k in range(8):
                            P.op("pe", lambda e, t=t, k=k, bank=bank, pos=pos: e.matmul(
                                PS[bank][:, pos * 128:(pos + 1) * 128], lhsT=W_in[:, k, cols[t]:cols[t] + 128],
                                rhs=uT[:, k, :], start=(k == 0), stop=(k == 7)), R=[B_uT, B_WB], W=[B_PS[bank]])
                        P.op("pool", lambda e, t=t: e.tensor_copy(out=xraw[:, t, 0:3], in_=ctail[:, tiles[t], :]),
                             R=[B_ctail[g]], W=[B_xr[t]])
                        yield
                    P.op("act", lambda e: e.activation(out=xraw[:, 0:4, 3:131],
                                                       in_=PS[0][:].rearrange("p (t n) -> p t n", t=4), func=AF.Copy),
                         R=[B_PS[0]], W=B_xr[0:4])
                    P.op("act", lambda e: e.activation(out=xraw[:, 4:6, 3:131],
                                                       in_=PS[1][:, 0:256].rearrange("p (t n) -> p t n", t=2), func=AF.Copy),
                         R=[B_PS[1]], W=B_xr[4:6])
                    yield
                    for k in range(8):
                        P.op("pe", lambda e, k=k: e.matmul(PS[0][:], lhsT=uT[:, k, :], rhs=W_in[:, k, 512 * g:512 * (g + 1)],
                                                           start=(k == 0), stop=(k == 7)), R=[B_uT, B_WB], W=[B_PS[0]])
                    yield
                    for t in range(6):
                        ti = tiles[t]
                        P.op("pool", lambda e, t=t: e.tensor_copy(out=ctail[:, tiles[t], :], in_=xraw[:, t, 128:131]),
                             R=[B_xr[t]], W=[B_ctail[g]])
                        P.op("act", lambda e, t=t, ti=ti: e.activation(out=cacc[:, t, :], in_=xraw[:, t, 3:131],
                                                                       func=AF.Identity, scale=cw[:, ti, 3:4],
                                                                       bias=LV[:, CB + ti:CB + ti + 1]),
                             R=[B_xr[t], B_LV], W=[B_ca[t]])
                        for k in range(3):
                            P.op("dve", lambda e, t=t, ti=ti, k=k: e.scalar_tensor_tensor(
                                out=cacc[:, t, :], in0=xraw[:, t, k:k + 128], scalar=cw[:, ti, k:k + 1],
                                in1=cacc[:, t, :], op0=ALU.mult, op1=ALU.add), R=[B_xr[t], B_LV, B_ca[t]], W=[B_ca[t]])
                        yield
                    P.op("act", lambda e: e.activation(out=scrB[:, 1024 + 512 * q:1536 + 512 * q], in_=PS[0][:], func=AF.Silu),
                         R=[B_PS[0]], W=[B_B[2 + q]])
                    P.op("act", lambda e: e.activation(out=aT[:], in_=cacc[:], func=AF.Silu), R=B_ca, W=[BaT])
                    yield
                    for t in range(5):
                        P.op("pe", lambda e, t=t: e.transpose(out=PT[:, t, :], in_=aT[:, t, :], identity=ident),
                             R=[BaT, B_cst], W=[B_PT])
                    yield
                    PTx = PT[:, 0:4, :].rearrange("p t (a d) -> p (t a) d", a=2)
                    P.op("dve", lambda e: e.tensor_tensor(out=xd[:].rearrange("p (h d) -> p h d", h=8), in0=PTx,
                                                          in1=dt_[:, 8 * g:8 * g + 8].unsqueeze(2).to_broadcast([128, 8, 64]),
                                                          op=ALU.mult), R=[B_PT, B_sm], W=[Bxd])
                    yield
                    P.op("dve", lambda e: e.tensor_tensor(out=xs_[:].rearrange("p (h d) -> p h d", h=8), in0=PTx,
                                                          in1=LV[:, DSK + 8 * g:DSK + 8 * g + 8].unsqueeze(2).to_broadcast([128, 8, 64]),
                                                          op=ALU.mult), R=[B_PT, B_LV], W=[Bxs])
                    P.op("act", lambda e: e.activation(out=bt[:], in_=PT[:, 4, :], func=AF.Copy), R=[B_PT], W=[Bbt])
                    yield
                    P.op("pool", lambda e: e.tensor_tensor(out=xw_[:].rearrange("p (h d) -> p h d", h=8),
                                                          in0=xd[:].rearrange("p (h d) -> p h d", h=8),
                                                          in1=dte[:, 8 * g:8 * g + 8].unsqueeze(2).to_broadcast([128, 8, 64]),
                                                          op=ALU.mult), R=[Bxd, B_sm], W=[Bxw])
                    yield

                def stageB(g):
                    q = g % 2
                    aT, xd, xs_, xw_, bt = actT2[q], xdt2[q], xsD2[q], xw2[q], Btm2[q]
                    BaT, Bxd, Bxs, Bxw, Bbt = B_actT2[q], B_xdt2[q], B_xsD2[q], B_xw2[q], B_Btm2[q]
                    szz = scrB[:, 1024 + 512 * q:1536 + 512 * q]
                    Bsz = B_B[2 + q]
                    P.op("pe", lambda e: e.matmul(PS[4][:, 0:128], lhsT=aT[:, 4, :], rhs=aT[:, 5, :], start=True,
                                                  stop=True), R=[BaT], W=[B_PS[4]])
                    rseg = scrA[:, 0:1024].rearrange("p (h i) -> p h i", h=8)
                    Ex = scrA[:, 1024:2048].rearrange("p (h i) -> p h i", h=8)
                    P.op("pool", lambda e: e.tensor_tensor(out=rseg, in0=U32.unsqueeze(1).to_broadcast([128, 8, 128]),
                                                          in1=dta[:, 8 * g:8 * g + 8].unsqueeze(2).to_broadcast([128, 8, 128]),
                                                          op=ALU.mult), R=[B_cst, B_sm], W=[B_A[0]])
                    yield
                    P.op("dve", lambda e: e.tensor_tensor(out=cbm[:], in0=PS[4][:, 0:128], in1=U32, op=ALU.mult),
                         R=[B_PS[4], B_cst], W=[B_cbm])
                    for hb in range(2):
                        P.op("pe", lambda e, hb=hb: e.matmul(PS[2 + hb][:], lhsT=L32,
                                                             rhs=scrA[:, hb * 512:(hb + 1) * 512], start=True, stop=True),
                             R=[B_A[0], B_cst], W=[B_PS[2 + hb]])
                    yield
                    for hb in range(2):
                        P.op("act", lambda e, hb=hb: e.activation(out=scrA[:, 1024 + hb * 512:1024 + (hb + 1) * 512],
                                                                  in_=PS[2 + hb][:], func=AF.Exp),
                             R=[B_PS[2 + hb]], W=[B_A[1]])
                        yield
                    MT = scrC[:, 0:1024].rearrange("p (h i) -> p h i", h=8)
                    P.op("dve", lambda e: e.tensor_tensor(out=MT, in0=Ex, in1=cbm[:].unsqueeze(1).to_broadcast([128, 8, 128]),
                                                          op=ALU.mult), R=[B_A[1], B_cbm], W=[B_C[0]])
                    P.op("pe", lambda e: e.matmul(PS[4][:], lhsT=bt[:], rhs=xw_[:], start=True, stop=True),
                         R=[Bbt, Bxw], W=[B_PS[4]])
                    yield
                    for hd in range(8):
                        P.op("pe", lambda e, hd=hd: e.matmul(PS[2][:, hd * 64:(hd + 1) * 64], lhsT=MT[:, hd, :],
                                                             rhs=xd[:, hd * 64:(hd + 1) * 64], start=True, stop=False),
                             R=[B_C[0], Bxd], W=[B_PS[2]])
                        P.op("pe", lambda e, hd=hd: e.matmul(PS[2][:, hd * 64:(hd + 1) * 64], lhsT=ident,
                                                             rhs=xs_[:, hd * 64:(hd + 1) * 64], start=False, stop=True),
                             R=[B_cst, Bxs], W=[B_PS[2]])
                    P.op("pe", lambda e: e.matmul(PS[3][:], lhsT=aT[:, 5, :], rhs=scrD[:, g * 512:(g + 1) * 512],
                                                  start=True, stop=True), R=[BaT, B_D[g]], W=[B_PS[3]])
                    yield
                    Sg = St[:, g * 512:(g + 1) * 512]
                    P.op("pool", lambda e: e.tensor_tensor(out=Sg.rearrange("p (h d) -> p h d", h=8),
                                                          in0=Sg.rearrange("p (h d) -> p h d", h=8),
                                                          in1=cd[:, 8 * g:8 * g + 8].unsqueeze(2).to_broadcast([128, 8, 64]),
                                                          op=ALU.mult), R=[B_St[g], B_sm], W=[B_St[g]])
                    yield
                    P.op("dve", lambda e: e.tensor_tensor(out=Sg, in0=PS[4][:], in1=Sg, op=ALU.add),
                         R=[B_PS[4], B_St[g]], W=[B_St[g]])
                    yield
                    ty = scrB[:, 0:512]
                    yy = scrB[:, 512:1024]
                    P.op("dve", lambda e: e.tensor_tensor(out=ty.rearrange("p (h d) -> p h d", h=8),
                                                          in0=PS[3][:].rearrange("p (h d) -> p h d", h=8),
                                                          in1=ea[:, 8 * g:8 * g + 8].unsqueeze(2).to_broadcast([128, 8, 64]),
                                                          op=ALU.mult), R=[B_PS[3], B_sm], W=[B_B[0]])
                    P.op("act", lambda e: e.activation(out=scrD[:, g * 512:(g + 1) * 512], in_=Sg, func=AF.Copy),
                         R=[B_St[g]], W=[B_D[g]])
                    yield
                    P.op("dve", lambda e: e.tensor_tensor(out=yy, in0=PS[2][:], in1=ty, op=ALU.add),
                         R=[B_PS[2], B_B[0]], W=[B_B[1]])
                    yield
                    P.op("pool", lambda e: e.tensor_tensor(out=yy, in0=yy, in1=szz, op=ALU.mult), R=[B_B[1], Bsz],
                         W=[B_B[1]])
                    yield
                    yn = scrC[:, 1024:1536]
                    ynT = scrC[:, 1536:2048].rearrange("p (t n) -> p t n", t=4)
                    P.op("act", lambda e: e.activation(out=yn, in_=yy, func=AF.Square,
                                                       accum_out=stt[:, 8 + 3 * g:9 + 3 * g]), R=[B_B[1], bst],
                         W=[B_C[1], bst])
                    yield
                    rstd_from_ss(stt, 8 + 3 * g, 512, bst)
                    yield
                    P.op("act", lambda e: e.activation(out=yn, in_=yy, func=AF.Copy, scale=stt[:, 10 + 3 * g:11 + 3 * g]),
                         R=[B_B[1], bst], W=[B_C[1]])
                    yield
                    for half in range(2):
                        for t in range(2):
                            tt = 2 * half + t
                            P.op("pe", lambda e, t=t, tt=tt: e.transpose(out=PT[:, 5 + t, :], in_=yn[:, tt * 128:(tt + 1) * 128],
                                                                         identity=ident), R=[B_C[1], B_cst], W=[B_PT])
                        P.op("dve", lambda e, half=half: e.tensor_tensor(
                            out=ynT[:, 2 * half:2 * half + 2, :], in0=PT[:, 5:7, :],
                            in1=LV[:, GW + 4 * g + 2 * half:GW + 4 * g + 2 * half + 2].unsqueeze(2).to_broadcast([128, 2, 128]),
                            op=ALU.mult), R=[B_PT, B_LV], W=[B_C[2]])
                        yield
                    for nb in range(2):
                        for t in range(4):
                            P.op("pe", lambda e, nb=nb, t=t: e.matmul(PS[5 + nb][:], lhsT=ynT[:, t, :],
                                                                     rhs=W_out[:, 4 * g + t, nb * 512:(nb + 1) * 512],
                                                                     start=(g == 0 and t == 0), stop=(g == 3 and t == 3)),
                                 R=[B_C[2], B_WB], W=[B_PS[5 + nb]])
                    yield

                def interleave(*gens):
                    gens = list(gens)
                    while gens:
                        for gn in list(gens):
                            try:
                                next(gn)
                            except StopIteration:
                                gens.remove(gn)

                interleave(stageA(0))
                interleave(stageA(1), stageB(0))
                interleave(stageA(2), stageB(1))
                interleave(stageA(3), stageB(2))
                interleave(stageB(3))
                for nb in range(2):
                    sl = slice(nb * 512, (nb + 1) * 512)
                    P.op("dve", lambda e, nb=nb, sl=sl: e.tensor_tensor(out=hh[:, sl], in0=PS[5 + nb][:], in1=hh[:, sl],
                                                                        op=ALU.add), R=[B_PS[5 + nb], bh], W=[bh])
                ple(hh, bh, stt, bst, pT_t[slot], B_pT[slot], PNW)
                store_chunk(L, c)

        def rope_tables():
            posf = kf[:, 0, 0:32]
            ivf = kf[:, 0, 32:64]
            pi32 = kf[:, 0, 64:96].bitcast(I32)
            P.dma("sp", lambda e: [e.dma_start(out=pi32, in_=pos_d)], "pos", 1, W=[B_kf])
            P.dma("sp", lambda e: [e.dma_start(out=ivf, in_=invf_d)], "invf", 1, W=[B_kf])
            P.op("dve", lambda e: e.tensor_copy(out=posf, in_=pi32), R=[B_kf], W=[B_kf])
            ang = scrA[:, 0:1024].rearrange("p (c f) -> p c f", c=32)
            red = scrA[:, 1024:2048].rearrange("p (c f) -> p c f", c=32)
            redi = scrB[:, 0:1024].bitcast(I32).rearrange("p (c f) -> p c f", c=32)
            redf = scrB[:, 1024:2048].rearrange("p (c f) -> p c f", c=32)
            P.op("dve", lambda e: e.tensor_tensor(out=ang, in0=posf.unsqueeze(2).to_broadcast([128, 32, 32]),
                                                  in1=ivf.unsqueeze(1).to_broadcast([128, 32, 32]), op=ALU.mult),
                 R=[B_kf], W=[B_A[0]])
            for which, dst in ((0, sinT), (1, cosT)):
                shift = 0.0 if which == 0 else float(np.pi / 2)
                P.op("dve", lambda e, shift=shift: e.tensor_scalar(out=red, in0=ang, scalar1=shift, scalar2=None, op0=ALU.add),
                     R=[B_A[0]], W=[B_A[1]])
                P.op("dve", lambda e: e.tensor_scalar(out=redi, in0=red, scalar1=float(1 / (2 * np.pi)), scalar2=None,
                                                      op0=ALU.mult), R=[B_A[1]], W=[B_B[0], B_B[1]])
                P.op("dve", lambda e: e.tensor_copy(out=redf, in_=redi), R=[B_B[0], B_B[1]], W=[B_B[2], B_B[3]])
                P.op("dve", lambda e: e.scalar_tensor_tensor(out=red, in0=redf, scalar=float(-2 * np.pi), in1=red,
                                                             op0=ALU.mult, op1=ALU.add), R=[B_B[2], B_B[3], B_A[1]], W=[B_A[1]])
                P.op("dve", lambda e: e.tensor_scalar(out=redf, in0=red, scalar1=float(np.pi), scalar2=float(-2 * np.pi),
                                                      op0=ALU.is_ge, op1=ALU.mult), R=[B_A[1]], W=[B_B[2], B_B[3]])
                P.op("dve", lambda e: e.tensor_tensor(out=red, in0=red, in1=redf, op=ALU.add), R=[B_A[1], B_B[2], B_B[3]],
                     W=[B_A[1]])
                P.op("dve", lambda e: e.tensor_scalar(out=redf, in0=red, scalar1=float(-np.pi), scalar2=float(2 * np.pi),
                                                      op0=ALU.is_lt, op1=ALU.mult), R=[B_A[1]], W=[B_B[2], B_B[3]])
                P.op("dve", lambda e: e.tensor_tensor(out=red, in0=red, in1=redf, op=ALU.add), R=[B_A[1], B_B[2], B_B[3]],
                     W=[B_A[1]])
                P.op("act", lambda e, dst=dst: e.activation(out=dst, in_=red, func=AF.Sin), R=[B_A[1]], W=B_St)

        def rope_apply(dst_bf, src, nh, c, R, W):
            s3 = src.rearrange("p (h d) -> p h d", h=nh)
            d3 = dst_bf.rearrange("p (h d) -> p h d", h=nh)
            n = nh * 32
            ta = (scrA[:, 1024:1024 + n] if nh == 16 else kf[:, 1, 0:n]).rearrange("p (h d) -> p h d", h=nh)
            tb = (scrA[:, 1536:1536 + n] if nh == 16 else kf[:, 1, 128:128 + n]).rearrange("p (h d) -> p h d", h=nh)
            Bt = [B_A[1]] if nh == 16 else [B_kf]
            cosb = cosT[:, c, :].unsqueeze(1).to_broadcast([128, nh, 32])
            sinb = sinT[:, c, :].unsqueeze(1).to_broadcast([128, nh, 32])
            x1 = s3[:, :, 0:32]
            x2 = s3[:, :, 32:64]
            P.op("dve", lambda e: e.tensor_tensor(out=ta, in0=x1, in1=cosb, op=ALU.mult), R=R + B_St, W=Bt)
            P.op("dve", lambda e: e.tensor_tensor(out=tb, in0=x2, in1=sinb, op=ALU.mult), R=R + B_St, W=Bt)
            P.op("dve", lambda e: e.tensor_tensor(out=d3[:, :, 0:32], in0=ta, in1=tb, op=ALU.subtract), R=Bt, W=W)
            P.op("dve", lambda e: e.tensor_tensor(out=ta, in0=x2, in1=cosb, op=ALU.mult), R=R + B_St, W=Bt)
            P.op("dve", lambda e: e.tensor_tensor(out=tb, in0=x1, in1=sinb, op=ALU.mult), R=R + B_St, W=Bt)
            P.op("dve", lambda e: e.tensor_tensor(out=d3[:, :, 32:64], in0=ta, in1=tb, op=ALU.add), R=Bt, W=W)

        def head_norm(src, nh, woff, stt, bst, scol, sqbuf, Bsq, R):
            s3 = src.rearrange("p (h d) -> p h d", h=nh)
            q3 = sqbuf.rearrange("p (h d) -> p h d", h=nh)
            P.op("dve", lambda e: e.tensor_tensor(out=sqbuf, in0=src, in1=src, op=ALU.mult), R=R, W=Bsq)
            P.op("dve", lambda e: e.tensor_reduce(out=stt[:, scol:scol + nh], in_=q3, axis=AX.X, op=ALU.add),
                 R=Bsq, W=[bst])
            P.op("act", lambda e: e.activation(out=stt[:, scol + nh:scol + 2 * nh], in_=stt[:, scol:scol + nh], func=AF.Ln,
                                               scale=1.0 / 64, bias=lsm[:, 2, 0:1]), R=[bst, B_lsm], W=[bst])
            P.op("act", lambda e: e.activation(out=stt[:, scol + 2 * nh:scol + 3 * nh], in_=stt[:, scol + nh:scol + 2 * nh],
                                               func=AF.Exp, scale=-0.5), R=[bst], W=[bst])
            P.op("dve", lambda e: e.tensor_tensor(out=s3, in0=s3,
                                                  in1=stt[:, scol + 2 * nh:scol + 3 * nh].unsqueeze(2).to_broadcast([128, nh, 64]),
                                                  op=ALU.mult), R=R + [bst], W=R)
            P.op("dve", lambda e: e.tensor_tensor(out=s3, in0=s3,
                                                  in1=LV[:, woff:woff + 64].unsqueeze(1).to_broadcast([128, nh, 64]),
                                                  op=ALU.mult), R=R + [B_LV], W=R)

        def attn_layer(L):
            j = L - 2
            layer_common_prep(L)
            ANW, PNW, KVNW, KNW, QNW, SNK = 0, 8, 16, 32, 96, 160
            pairs = [(WB[:, 0:16384], w_ai_d[j], 4), (WB[:, 16384:24576], w_ao_d[j], 2),
                     (WB[:, 57600:65792], w_g_d[L], 2), (WB[:, 65792:67840], w_p_d[L], 1)]
            if j == 0:
                pairs.append((WB[:, 24576:28672], w_kv_d, 1))
            load_weights(pairs, L)
            if j == 0:
                rope_tables()
                Vflat = WB[:, 36864:36864 + 8320].rearrange("p (n e) -> p n e", e=65)
                P.op("dve", lambda e: e.memset(Vflat[:, :, 64:65], 1.0), R=[B_WB], W=[B_Vones])
            off = 45184
            q2 = [scrD[:].rearrange("p (t a n) -> p t a n", t=8, a=2),
                  WB[:, off:off + 2048].rearrange("p (t a n) -> p t a n", t=8, a=2)]
            B_q2 = [Buf("q2a"), Buf("q2b")]
            sgb = [scrB[:, 1024:2048], WB[:, off + 2048:off + 4096].bitcast(F32)]
            B_sg = [Buf("sga"), Buf("sgb")]
            uTb = WB[:, off + 4096:off + 5120].rearrange("p (k n) -> p k n", k=8)
            ubb = WB[:, off + 5120:off + 6144]
            B_uTb, B_ubb = Buf("uTb"), Buf("ubb")
            tgb = WB[:, off + 6144:off + 8192].bitcast(F32)
            vvb = WB[:, off + 8192:off + 10240].bitcast(F32)
            B_tgb, B_vvb = Buf("tgb"), Buf("vvb")
            B_den = Buf("den")
            P.op("pool", lambda e: e.memset(scrD[:], 0.0), R=[B_WB], W=B_D + [B_q2[0]])
            P.op("pool", lambda e: e.memset(WB[:, off:off + 2048], 0.0), R=[B_WB], W=[B_q2[1]])
            P.op("act", lambda e: e.activation(out=lsm[:, 1, 0:16], in_=LV[:, SNK:SNK + 16], func=AF.Exp), R=[B_LV],
                 W=[B_lsm])

            def front(c):
                slot = c % 2
                hh, bh, stt, bst = h_t[slot], B_h[slot], stat[slot], B_stat[slot]
                altF = (u_bf[:], B_u, uT[:], B_uT, 0)
                P.op("pool", lambda e, stt=stt: e.memset(stt[:], 0.0), W=[bst])
                yield
                if j == 0:
                    norm_T(hh, bh, stt, bst, 0, KVNW, alt=altF)
                    yield
                    for k in range(8):
                        P.op("pe", lambda e, k=k: e.matmul(PS[3][:], lhsT=uT[:, k, :], rhs=W_kv[:, k, :], start=(k == 0),
                                                           stop=(k == 7)), R=[B_uT, B_WB], W=[B_PS[3]])
                    kfl = kf[:, 0, :]
                    P.op("act", lambda e: e.activation(out=kfl, in_=PS[3][:, 0:256], func=AF.Copy), R=[B_PS[3]], W=[B_kf])
                    P.op("act", lambda e, c=c: e.activation(out=Vst[:, c, :, 0:64],
                                                            in_=PS[3][:, 256:512].rearrange("p (g d) -> p g d", g=4),
                                                            func=AF.Copy), R=[B_PS[3], B_WB], W=[B_V[c]])
                    yield
                    head_norm(kfl, 4, KNW, stt, bst, 24, kf[:, 2, :], [B_kf], [B_kf])
                    yield
                    rope_apply(krbuf, kfl, 4, c, [B_kf], [B_kr])
                    yield
                    for t in range(2):
                        P.op("pe", lambda e, t=t: e.transpose(out=PT[:, t, :], in_=krbuf[:, t * 128:(t + 1) * 128],
                                                              identity=ident), R=[B_kr, B_cst], W=[B_PT])
                    P.op("act", lambda e, c=c: e.activation(out=KT[:, :, c * 128:(c + 1) * 128], in_=PT[:, 0:2, :],
                                                            func=AF.Copy), R=[B_PT, B_WB], W=[B_KT[c]])
                    yield
                norm_T(hh, bh, stt, bst, 0, ANW, have_rstd=(j == 0), alt=altF)
                yield
                qf = scrA[:, 0:1024]
                for nb in range(2):
                    for k in range(8):
                        P.op("pe", lambda e, nb=nb, k=k: e.matmul(PS[3 + nb][:], lhsT=uT[:, k, :],
                                                                 rhs=W_ai[:, k, nb * 512:(nb + 1) * 512],
                                                                 start=(k == 0), stop=(k == 7)),
                             R=[B_uT, B_WB], W=[B_PS[3 + nb]])
                    P.op("act", lambda e, nb=nb: e.activation(out=qf[:, nb * 512:(nb + 1) * 512], in_=PS[3 + nb][:],
                                                              func=AF.Copy), R=[B_PS[3 + nb]], W=[B_A[0]])
                    yield
                for nb in range(2):
                    for k in range(8):
                        P.op("pe", lambda e, nb=nb, k=k: e.matmul(PS[3 + nb][:], lhsT=uT[:, k, :],
                                                                 rhs=W_ai[:, k, 1024 + nb * 512:1024 + (nb + 1) * 512],
                                                                 start=(k == 0), stop=(k == 7)),
                             R=[B_uT, B_WB], W=[B_PS[3 + nb]])
                    P.op("act", lambda e, nb=nb: e.activation(out=sgb[slot][:, nb * 512:(nb + 1) * 512], in_=PS[3 + nb][:],
                                                              func=AF.Silu), R=[B_PS[3 + nb]], W=[B_sg[slot]])
                    yield
                head_norm(qf, 16, QNW, stt, bst, 8, scrA[:, 1024:2048], [B_A[1]], [B_A[0]])
                yield
                qr = scrC[:, 0:1024]
                rope_apply(qr, qf, 16, c, [B_A[0]], [B_C[0]])
                yield
                qT2 = q2[slot]
                for hf in range(2):
                    for i in range(4):
                        t = 4 * hf + i
                        P.op("pe", lambda e, t=t, i=i: e.transpose(out=PT[:, i, :], in_=qr[:, t * 128:(t + 1) * 128],
                                                                   identity=ident), R=[B_C[0], B_cst], W=[B_PT])
                    P.op("act", lambda e, hf=hf: e.activation(out=qT2[0:64, 4 * hf:4 * hf + 4, 0, :], in_=PT[0:64, 0:4, :],
                                                              func=AF.Copy), R=[B_PT], W=[B_q2[slot]])
                    P.op("act", lambda e, hf=hf: e.activation(out=qT2[64:128, 4 * hf:4 * hf + 4, 1, :], in_=PT[64:128, 0:4, :],
                                                              func=AF.Copy), R=[B_PT], W=[B_q2[slot]])
                    yield

            def back(c):
                slot = c % 2
                hh, bh, stt, bst = h_t[slot], B_h[slot], stat[slot], B_stat[slot]
                qT2 = q2[slot]
                blks = [c - 1, c] if c > 0 else [c]
                nb_ = len(blks)
                for r in range(8):
                    sb_ = 5 + (r % 2)
                    es = r % 2
                    for hh_ in range(2):
                        hp = 2 * r + hh_
                        g = PERM[hp] // 4
                        half = hp % 2
                        for bi, blk in enumerate(blks):
                            col = (hh_ * 2 + bi) * 128
                            P.op("pe", lambda e, g=g, half=half, blk=blk, col=col, r=r, sb_=sb_: e.matmul(
                                PS[sb_][:, col:col + 128], lhsT=KT[:, g // 2, blk * 128:(blk + 1) * 128],
                                rhs=qT2[:, r, half, :], start=True, stop=True),
                                 R=[B_KT[blk], B_q2[slot]], W=[B_PS[sb_]])
                    Ev = Eatt[:, es, :].rearrange("p (a b n) -> p a b n", a=2, b=2)
                    Pv = PS[sb_][:].rearrange("p (a b n) -> p a b n", a=2, b=2)
                    if nb_ == 2:
                        P.op("act", lambda e, es=es, sb_=sb_: e.activation(out=Eatt[:, es, :], in_=PS[sb_][:], func=AF.Exp,
                                                                           scale=0.125), R=[B_PS[sb_]], W=[B_Eatt[es]])
                        P.op("dve", lambda e, Ev=Ev: e.tensor_tensor(out=Ev[:, :, 0, :], in0=Ev[:, :, 0, :],
                                                                     in1=Lbf.unsqueeze(1).to_broadcast([128, 2, 128]),
                                                                     op=ALU.mult), R=[B_Eatt[es], B_cst], W=[B_Eatt[es]])
                        P.op("dve", lambda e, Ev=Ev: e.tensor_tensor(out=Ev[:, :, 1, :], in0=Ev[:, :, 1, :],
                                                                     in1=Ubf.unsqueeze(1).to_broadcast([128, 2, 128]),
                                                                     op=ALU.mult), R=[B_Eatt[es], B_cst], W=[B_Eatt[es]])
                    else:
                        P.op("act", lambda e, Ev=Ev, Pv=Pv: e.activation(out=Ev[:, :, 0, :], in_=Pv[:, :, 0, :], func=AF.Exp,
                                                                         scale=0.125), R=[B_PS[sb_]], W=[B_Eatt[es]])
                        P.op("dve", lambda e, Ev=Ev: e.tensor_tensor(out=Ev[:, :, 0, :], in0=Ev[:, :, 0, :],
                                                                     in1=Ubf.unsqueeze(1).to_broadcast([128, 2, 128]),
                                                                     op=ALU.mult), R=[B_Eatt[es], B_cst], W=[B_Eatt[es]])
                    yield
                    for hh_ in range(2):
                        hp = 2 * r + hh_
                        g = PERM[hp] // 4
                        ob = hp // 6
                        oc = (hp % 6) * 65
                        for bi, blk in enumerate(blks):
                            P.op("pe", lambda e, hh_=hh_, bi=bi, blk=blk, g=g, ob=ob, oc=oc, Ev=Ev: e.matmul(
                                PS[ob][:, oc:oc + 65], lhsT=Ev[:, hh_, bi, :], rhs=Vst[:, blk, g, :],
                                start=(bi == 0), stop=(bi == nb_ - 1)),
                                 R=[B_Eatt[es], B_V[blk], B_Vones], W=[B_PS[ob]])
                    yield
                den = lsm[:, 3, 0:16]
                rden = lsm[:, 3, 16:32]
                o_ = scrB[:, 0:1024].rearrange("p (h d) -> p h d", h=16)
                for ob in range(3):
                    nh = 6 if ob < 2 else 4
                    pv = PS[ob][:, 0:nh * 65].rearrange("p (h e) -> p h e", e=65)
                    hs = slice(ob * 6, ob * 6 + nh)
                    P.op("dve", lambda e, pv=pv, hs=hs, nh=nh: e.tensor_tensor(
                        out=den[:, hs].unsqueeze(2), in0=pv[:, :, 64:65], in1=lsm[:, 1, hs].unsqueeze(2), op=ALU.add),
                         R=[B_PS[ob], B_lsm], W=[B_den])
                yield
                P.op("dve", lambda e: e.reciprocal(out=rden, in_=den), R=[B_den], W=[B_den])
                yield
                for ob in range(3):
                    nh = 6 if ob < 2 else 4
                    pv = PS[ob][:, 0:nh * 65].rearrange("p (h e) -> p h e", e=65)
                    hs = slice(ob * 6, ob * 6 + nh)
                    P.op("dve", lambda e, pv=pv, hs=hs, nh=nh: e.tensor_tensor(
                        out=o_[:, hs, :], in0=pv[:, :, 0:64], in1=rden[:, hs].unsqueeze(2).to_broadcast([128, nh, 64]),
                        op=ALU.mult), R=[B_PS[ob], B_den], W=[B_B[0], B_B[1]])
                    yield
                og = scrC[:, 1024:2048]
                P.op("dve", lambda e: e.tensor_tensor(out=og, in0=scrB[:, 0:1024], in1=sgb[slot], op=ALU.mult),
                     R=[B_B[0], B_B[1], B_sg[slot]], W=[B_C[1], B_C[2]])
                yield
                for hf in range(2):
                    for i in range(4):
                        t = 4 * hf + i
                        P.op("pe", lambda e, t=t, i=i: e.transpose(out=PT[:, 4 + i, :], in_=og[:, t * 128:(t + 1) * 128],
                                                                   identity=ident), R=[B_C[1], B_C[2], B_cst], W=[B_PT])
                    P.op("act", lambda e, hf=hf: e.activation(out=uTb[:, 4 * hf:4 * hf + 4, :], in_=PT[:, 4:8, :], func=AF.Copy),
                         R=[B_PT], W=[B_uTb])
                    yield
                for nb in range(2):
                    for k in range(8):
                        P.op("pe", lambda e, nb=nb, k=k: e.matmul(PS[5 + nb][:], lhsT=uTb[:, k, :],
                                                                 rhs=W_ao[:, k, nb * 512:(nb + 1) * 512], start=(k == 0),
                                                                 stop=(k == 7)), R=[B_uTb, B_WB], W=[B_PS[5 + nb]])
                    yield
                for nb in range(2):
                    sl = slice(nb * 512, (nb + 1) * 512)
                    P.op("dve", lambda e, nb=nb, sl=sl: e.tensor_tensor(out=hh[:, sl], in0=PS[5 + nb][:], in1=hh[:, sl],
                                                                        op=ALU.add), R=[B_PS[5 + nb], bh], W=[bh])
                    yield
                ple(hh, bh, stt, bst, pT_t[slot], B_pT[slot], PNW, alt=(ubb, B_ubb, uTb, B_uTb, 4),
                    banks=(0, 1, 5, 6), tv=(tgb, vvb, B_tgb, B_vvb))
                store_chunk(L, c)
                yield

            def interleave(*gens):
                gens = list(gens)
                while gens:
                    for gn in list(gens):
                        try:
                            next(gn)
                        except StopIteration:
                            gens.remove(gn)

            load_chunk(L, 0)
            interleave(front(0))
            for c in range(nch):
                if c + 1 < nch:
                    load_chunk(L, c + 1)
                    interleave(front(c + 1), back(c))
                else:
                    interleave(back(c))

        for L in layers:
            if L < 2:
                mamba_layer(L)
            else:
                attn_layer(L)
        finals = [hstores[(layers[-1], c)] for c in range(nch)]
        P.emit(final_waits=finals)
    return nc


def _ktile(w):
    K, N = w.shape
    return np.ascontiguousarray(w.reshape(K // 128, 128, N).transpose(1, 0, 2).reshape(128, (K // 128) * N))


def _rep(v):
    return np.broadcast_to(np.asarray(v, np.float32).reshape(1, -1), (128, v.size))


def prepare_inputs(x, p, positions, ssm_norm_w, ssm_in_w, ssm_conv_w, ssm_conv_b, ssm_dt_bias, ssm_a_log, ssm_d,
                   ssm_gnorm_w, ssm_out_w, kv_norm_w, kv_w, k_norm_w, attn_norm_w, attn_in_w, q_norm_w, attn_sinks,
                   attn_out_w, ple_norm_w, ple_gate_w, ple_proj_w):
    f = np.float32
    x = np.asarray(x, f)
    p = np.asarray(p, f)
    positions = np.asarray(positions, np.int32)
    cst = np.zeros((128, 4, 128), f)
    k = np.arange(128)
    cst[:, 0, :] = np.eye(128)
    cst[:, 1, :] = (k[:, None] <= k[None, :])
    cst[:, 2, :] = (k[:, None] > k[None, :])
    cst[:, 3, :] = 1.0
    invf = (10000.0 ** (-(np.arange(32, dtype=np.float32) * 2.0 / 64))).astype(f)
    invf = np.ascontiguousarray(_rep(invf))
    qperm = np.concatenate([np.arange(h * 64, (h + 1) * 64) for h in PERM])
    w_in = np.stack([_ktile(np.asarray(ssm_in_w[l], f)) for l in range(2)])
    w_out = np.stack([_ktile(np.asarray(ssm_out_w[l], f)) for l in range(2)])
    w_g = np.stack([_ktile(np.asarray(ple_gate_w[i], f)) for i in range(4)])
    w_p = np.stack([_ktile(np.asarray(ple_proj_w[i], f)) for i in range(4)])
    w_kv = _ktile(np.asarray(kv_w, f))
    ai = []
    ao = []
    for j in range(2):
        w = np.asarray(attn_in_w[j], f)
        w = np.concatenate([w[:, :1024][:, qperm], w[:, 1024:][:, qperm]], axis=1)
        ai.append(_ktile(w))
        ao.append(_ktile(np.asarray(attn_out_w[j], f)[qperm, :]))
    w_ai = np.stack(ai)
    w_ao = np.stack(ao)
    lv = np.zeros((4, 128, LVW), f)

    def fm(v):
        v = np.asarray(v, f)
        return v.reshape(-1, 128).T

    for l in range(2):
        lv[l, :, 0:8] = fm(ssm_norm_w[l])
        lv[l, :, 8:16] = fm(ple_norm_w[l])
        lv[l, :, 16:32] = fm(ssm_gnorm_w[l])
        cw = np.asarray(ssm_conv_w[l], f)
        lv[l, :, 32:128] = cw.reshape(4, 24, 128).transpose(2, 1, 0).reshape(128, 96)
        lv[l, :, 128:152] = np.asarray(ssm_conv_b[l], f).reshape(24, 128).T
        lv[l, :, 152:184] = _rep(np.asarray(ssm_dt_bias[l], f))
        lv[l, :, 184:216] = _rep(np.asarray(ssm_a_log[l], f))
        lv[l, :, 216:248] = _rep(np.asarray(ssm_d[l], f))
    for j in range(2):
        L = 2 + j
        lv[L, :, 0:8] = fm(attn_norm_w[j])
        lv[L, :, 8:16] = fm(ple_norm_w[L])
        lv[L, :, 16:24] = fm(kv_norm_w)
        lv[L, :, 32:96] = _rep(np.asarray(k_norm_w, f))
        lv[L, :, 96:160] = _rep(np.asarray(q_norm_w[j], f))
        lv[L, :, 160:176] = _rep(np.asarray(attn_sinks[j], f)[PERM])
    shared = dict(cst=cst, invf=invf, w_in=w_in, w_out=w_out, w_g=w_g, w_p=w_p, w_kv=w_kv, w_ai=w_ai, w_ao=w_ao, lv=lv)
    in_maps = []
    for b in range(x.shape[0]):
        m = dict(shared)
        m["x"] = np.ascontiguousarray(x[b])
        m["pT"] = np.ascontiguousarray(p[:, b].transpose(0, 2, 1))
        m["pos"] = np.ascontiguousarray(positions[b].reshape(32, 128).T)
        in_maps.append(m)
    return in_maps


_NC_CACHE = {}
LAUNCHES = [(0, 1, 2, 3)]


def kernel(**inputs):
    in_maps = prepare_inputs(**inputs)
    n = len(in_maps)
    outs = None
    for grp in LAUNCHES:
        if grp not in _NC_CACHE:
            _NC_CACHE[grp] = build(layers=grp)
        nc = _NC_CACHE[grp]
        if outs is not None:
            for b in range(n):
                in_maps[b]["x"] = outs[b]
        res = run_bass_kernel_spmd(nc, in_maps, core_ids=list(range(n)))
        outs = [np.ascontiguousarray(np.asarray(r["out"], np.float32)) for r in res.results]
    return np.stack(outs, axis=0)
```

```python
import contextlib
import numpy as np
import concourse.bass as bass
import concourse.mybir as mybir
from concourse.bass_utils import run_bass_kernel_spmd

F32 = mybir.dt.float32
BF16 = mybir.dt.bfloat16
I32 = mybir.dt.int32
AF = mybir.ActivationFunctionType
ALU = mybir.AluOpType
AX = mybir.AxisListType

EPOCH = 512
SCHEDULE = True
QUANT = 0.25
D = 1024
S = 4096
NCH = 32
EPS = 1e-6
SSM_IN = 5152
LVW = 256
PERM = [0, 4, 1, 5, 2, 6, 3, 7, 8, 12, 9, 13, 10, 14, 11, 15]


class Buf:
    __slots__ = ("name", "w", "wd", "rs", "rd", "psum")

    def __init__(self, name, psum=False):
        self.name = name
        self.w = {}
        self.wd = []
        self.rs = {}
        self.rd = []
        self.psum = psum


class Ins:
    __slots__ = ("eng", "fn", "deps", "sig", "dma", "key", "val", "sem", "n", "idx", "odeps", "dur", "end")

    def __init__(self, eng, fn, dma=False, key=None):
        self.eng = eng
        self.fn = fn
        self.deps = []
        self.sig = False
        self.dma = dma
        self.key = key
        self.val = None
        self.sem = None
        self.odeps = []
        self.dur = 0.3
        self.end = 0.0


def _est_dur(eng, calls):
    name, a, k = calls[0]
    out = k.get("out", a[0] if a else None)
    try:
        shp = out.shape
        n = 1
        for d_ in shp[1:]:
            n *= d_
    except Exception:
        n = 256
    if eng == "pe":
        f32 = False
        try:
            f32 = (k.get("lhsT").dtype == F32)
        except Exception:
            pass
        return 0.07 + n / 2000.0 * (4.0 if f32 else 1.0)
    if eng == "act":
        return 0.25 + n / 1200.0
    if eng == "dve":
        return 0.12 + n / 1100.0
    if eng == "pool":
        return 0.2 + n / 450.0
    return 0.3


class _Rec:
    def __init__(self):
        self.calls = []

    def __getattr__(self, name):
        def f(*a, **k):
            self.calls.append((name, a, k))
            return None
        return f


def _replay(calls, eobj):
    return [getattr(eobj, name)(*a, **k) for name, a, k in calls]


class Prog:
    ENGS = ("pe", "act", "dve", "pool", "sp")

    def __init__(self, nc):
        self.nc = nc
        self.streams = {e: [] for e in self.ENGS}
        self.dma_cnt = {}
        self.nins = 0
        self._last_dma = {}

    def _track(self, ins, R, W):
        deps = ins.deps
        eng = ins.eng
        dma = ins.dma
        for b in R:
            deps.extend(b.w.values())
            deps.extend(b.wd)
            if b.psum:
                for e, xs in b.rs.items():
                    if e != eng:
                        deps.extend(xs)
        for b in W:
            for e, x in b.w.items():
                if dma or e != eng or eng != "pe":
                    deps.append(x)
                else:
                    ins.odeps.append(x)
            deps.extend(b.wd)
            for e, xs in b.rs.items():
                if dma or e != eng or eng != "pe":
                    deps.extend(xs)
            deps.extend(b.rd)
        for b in R:
            if dma:
                b.rd.append(ins)
            else:
                b.rs.setdefault(eng, []).append(ins)
        for b in W:
            b.rs = {}
            b.rd = []
            if dma:
                b.w = {}
                b.wd = [ins]
            else:
                b.w = {eng: ins}
                b.wd = []

    def op(self, eng, fn, R=(), W=()):
        rec = _Rec()
        fn(rec)
        assert len(rec.calls) == 1
        ins = Ins(eng, rec.calls)
        ins.dur = _est_dur(eng, rec.calls)
        ins.idx = self.nins
        self._track(ins, R, W)
        self.streams[eng].append(ins)
        self.nins += 1
        return ins

    def dma(self, eng, fn, key, n, R=(), W=()):
        rec = _Rec()
        fn(rec)
        assert len(rec.calls) == n, (len(rec.calls), n)
        ins = Ins(eng, rec.calls, dma=True, key=key)
        ins.n = n
        ins.idx = self.nins
        ins.dur = 2.5 * n
        self._track(ins, R, W)
        if self._last_dma.get(eng) is not None:
            ins.odeps.append(self._last_dma[eng])
        self._last_dma[eng] = ins
        c = self.dma_cnt.get(key, 0) + n
        self.dma_cnt[key] = c
        ins.val = 16 * c
        self.streams[eng].append(ins)
        self.nins += 1
        return ins

    def schedule(self):
        import heapq
        HOP = 0.35
        allins = [i for e in self.ENGS for i in self.streams[e]]
        succ = {}
        nrem = {}
        for i in allins:
            ds = set(id(d) for d in i.deps) | set(id(d) for d in i.odeps)
            nrem[id(i)] = len(ds)
            seen = set()
            for d in list(i.deps) + list(i.odeps):
                if id(d) in seen:
                    continue
                seen.add(id(d))
                succ.setdefault(id(d), []).append(i)
        bl = {}
        for i in sorted(allins, key=lambda x: -x.idx):
            m_ = 0.0
            for sx in succ.get(id(i), ()):
                v_ = bl[id(sx)] + (HOP if sx.eng != i.eng else 0.1)
                if v_ > m_:
                    m_ = v_
            bl[id(i)] = m_ + i.dur
        Q = QUANT

        def key(t_, i):
            return (int(t_ / Q), -bl[id(i)], i.idx)
        free = {e: 0.0 for e in self.ENGS}
        ready_t = {}
        heap = []
        for i in allins:
            if nrem[id(i)] == 0:
                ready_t[id(i)] = 0.0
                heapq.heappush(heap, (key(0.0, i), i.idx, i))
        order = {e: [] for e in self.ENGS}
        done = 0
        while heap:
            st, _, i = heapq.heappop(heap)
            real = max(free[i.eng], ready_t[id(i)])
            if key(real, i) > st:
                heapq.heappush(heap, (key(real, i), i.idx, i))
                continue
            if i.dma:
                free[i.eng] = real + 0.1
            else:
                free[i.eng] = real + i.dur
            i.end = real + i.dur
            order[i.eng].append(i)
            done += 1
            for sx in succ.get(id(i), ()):
                r = max(ready_t.get(id(sx), 0.0), i.end + (HOP if sx.eng != i.eng or i.dma else 0.1))
                ready_t[id(sx)] = r
                nrem[id(sx)] -= 1
                if nrem[id(sx)] == 0:
                    heapq.heappush(heap, (key(max(r, free[sx.eng]), sx), sx.idx, sx))
        assert done == len(allins), (done, len(allins))
        self.streams = order
        print(f"[prog] scheduled: modelled makespan {max(free.values()):.1f} us", flush=True)

    def emit(self, final_waits=()):
        nc = self.nc
        if SCHEDULE:
            self.schedule()
        pos = {}
        for e in self.ENGS:
            for k_, ins in enumerate(self.streams[e]):
                pos[id(ins)] = k_
        for e in self.ENGS:
            for ins in self.streams[e]:
                best = {}
                nd = []
                for d in ins.deps:
                    if d.dma:
                        nd.append(d)
                    else:
                        b_ = best.get(d.eng)
                        if b_ is None or pos[id(d)] > pos[id(b_)]:
                            best[d.eng] = d
                ins.deps = nd + list(best.values())
        for e in self.ENGS:
            for ins in self.streams[e]:
                for d in ins.deps:
                    if not d.dma:
                        d.sig = True
        nsig = {}
        for e in self.ENGS:
            s = 0
            for ins in self.streams[e]:
                if ins.sig and not ins.dma:
                    ins.sem = (e, s // EPOCH)
                    ins.val = s % EPOCH + 1
                    s += 1
            nsig[e] = s
        sem_names = []
        for e in self.ENGS:
            for k in range((nsig[e] + EPOCH - 1) // EPOCH):
                sem_names.append((e, k))
        for key in self.dma_cnt:
            sem_names.append(("dma", key))
        print(f"[prog] instructions={self.nins} sems={len(sem_names)} sig={nsig}", flush=True)
        with contextlib.ExitStack() as st:
            sems = {}
            for nm in sem_names:
                sems[nm] = st.enter_context(nc.semaphore(f"s_{nm[0]}_{nm[1]}"))
            block = st.enter_context(nc.Block())

            def run_stream(ename, eobj):
                waited = {}
                for ins in self.streams[ename]:
                    need = {}
                    for d in ins.deps:
                        k = ("dma", d.key) if d.dma else d.sem
                        v = d.val
                        if waited.get(k, 0) >= v:
                            continue
                        if need.get(k, 0) < v:
                            need[k] = v
                    for k, v in need.items():
                        eobj.wait_ge(sems[k], v)
                        waited[k] = v
                    if ins.dma:
                        for r in _replay(ins.fn, eobj):
                            r.then_inc(sems[("dma", ins.key)], 16)
                    else:
                        r = _replay(ins.fn, eobj)[0]
                        if ins.sig:
                            r.then_inc(sems[ins.sem], 1)
                if ename == "sp":
                    for d in final_waits:
                        eobj.wait_ge(sems[("dma", d.key)], d.val)

            @block.tensor
            def _(pe):
                run_stream("pe", pe)

            @block.scalar
            def _(act):
                run_stream("act", act)

            @block.vector
            def _(dve):
                run_stream("dve", dve)

            @block.gpsimd
            def _(pool):
                run_stream("pool", pool)

            @block.sync
            def _(sp):
                run_stream("sp", sp)


def build(layers=(0, 1, 2, 3), nch=NCH):
    nc = bass.Bass("TRN2", target_bir_lowering=False)

    def dram(name, shape, dt, kind="ExternalInput"):
        return nc.dram_tensor(name, shape, dt, kind=kind).ap()

    x_d = dram("x", [S, D], F32)
    pT_d = dram("pT", [4, 256, S], F32)
    pos_d = dram("pos", [128, 32], I32)
    cst_d = dram("cst", [128, 4, 128], F32)
    invf_d = dram("invf", [128, 32], F32)
    w_in_d = dram("w_in", [2, 128, 8 * SSM_IN], F32)
    w_out_d = dram("w_out", [2, 128, 16 * 1024], F32)
    w_g_d = dram("w_g", [4, 128, 8 * 1024], F32)
    w_p_d = dram("w_p", [4, 128, 2 * 1024], F32)
    w_kv_d = dram("w_kv", [128, 8 * 512], F32)
    w_ai_d = dram("w_ai", [2, 128, 8 * 2048], F32)
    w_ao_d = dram("w_ao", [2, 128, 8 * 1024], F32)
    lv_d = dram("lv", [4, 128, LVW], F32)
    out_d = dram("out", [S, D], F32, kind="ExternalOutput")

    st = contextlib.ExitStack()
    with st:
        def sb(name, shape, dt):
            return st.enter_context(nc.sbuf_tensor(name, shape, dt))

        def ps(name, shape, dt):
            return st.enter_context(nc.psum_tensor(name, shape, dt))

        P = Prog(nc)
        WB = sb("WB", [128, 67840], BF16)
        B_WB = Buf("WB")
        LV = sb("LV", [128, LVW], F32)
        B_LV = Buf("LV")
        cbf = sb("cbf", [128, 4, 128], BF16)
        c32 = sb("c32", [128, 4, 128], F32)
        B_cst = Buf("cst")
        ident = cbf[:, 0, :]
        Ubf = cbf[:, 1, :]
        Lbf = cbf[:, 2, :]
        U32 = c32[:, 1, :]
        L32 = c32[:, 2, :]
        ones32 = c32[:, 3, :]
        h_t = [sb(f"h{i}", [128, D], F32) for i in range(2)]
        B_h = [Buf(f"h{i}") for i in range(2)]
        pT_t = [sb(f"pTt{i}", [128, 2, 128], BF16) for i in range(2)]
        B_pT = [Buf(f"pTt{i}") for i in range(2)]
        stat = [sb(f"stat{i}", [128, 64], F32) for i in range(2)]
        B_stat = [Buf(f"stat{i}") for i in range(2)]
        u_bf = sb("u_bf", [128, D], BF16)
        B_u = Buf("u_bf")
        uT = sb("uT", [128, 8, 128], BF16)
        B_uT = Buf("uT")
        sm = sb("sm", [128, 16, 32], F32)
        B_sm = Buf("sm")
        lsm = sb("lsm", [128, 4, 32], F32)
        B_lsm = Buf("lsm")
        xraw = sb("xraw", [128, 6, 131], F32)
        B_xraw = Buf("xraw")
        B_xr = [Buf(f"xr{t}") for t in range(6)]
        B_ca = [Buf(f"ca{t}") for t in range(6)]
        ctail = sb("ctail", [128, 24, 3], F32)
        B_ctail = [Buf(f"ctail{g}") for g in range(4)]
        cacc = sb("cacc", [128, 6, 128], F32)
        B_cacc = Buf("cacc")
        actT = sb("actT", [128, 6, 128], BF16)
        B_actT = Buf("actT")
        xdt = sb("xdt", [128, 512], BF16)
        B_xdt = Buf("xdt")
        xsD = sb("xsD", [128, 512], BF16)
        B_xsD = Buf("xsD")
        xw = sb("xw", [128, 512], BF16)
        B_xw = Buf("xw")
        Btm = sb("Btm", [128, 128], BF16)
        B_Btm = Buf("Btm")
        cbm = sb("cbm", [128, 128], F32)
        B_cbm = Buf("cbm")
        scrA = sb("scrA", [128, 2048], F32)
        B_A = [Buf("scrA0"), Buf("scrA1")]
        scrB = sb("scrB", [128, 2048], F32)
        B_B = [Buf(f"scrB{i}") for i in range(4)]
        scrC = sb("scrC", [128, 2048], BF16)
        B_C = [Buf("scrC0"), Buf("scrC1"), Buf("scrC2")]
        scrD = sb("scrD", [128, 2048], BF16)
        B_D = [Buf(f"scrD{i}") for i in range(4)]
        St = sb("St", [128, 2048], F32)
        B_St = [Buf(f"St{i}") for i in range(4)]
        _cb16 = cacc[:].rearrange("p t n -> p (t n)").bitcast(BF16)
        Eatt = _cb16[:, 0:1024].rearrange("p (a n) -> p a n", a=2)
        B_Eatt = [Buf("Eatt0"), Buf("Eatt1")]
        krbuf = _cb16[:, 1024:1280]
        B_kr = Buf("kr")
        kf = xraw[:].rearrange("p t n -> p (t n)")[:, 0:768].rearrange("p (a b) -> p a b", a=3)
        B_kf = B_xraw
        actTb = sb("actTb", [128, 6, 128], BF16)
        xdtb = sb("xdtb", [128, 512], BF16)
        xsDb = sb("xsDb", [128, 512], BF16)
        xwb = sb("xwb", [128, 512], BF16)
        Btmb = sb("Btmb", [128, 128], BF16)
        actT2, xdt2, xsD2, xw2, Btm2 = [actT, actTb], [xdt, xdtb], [xsD, xsDb], [xw, xwb], [Btm, Btmb]
        B_actT2 = [B_actT, Buf("actTb")]
        B_xdt2 = [B_xdt, Buf("xdtb")]
        B_xsD2 = [B_xsD, Buf("xsDb")]
        B_xw2 = [B_xw, Buf("xwb")]
        B_Btm2 = [B_Btm, Buf("Btmb")]
        PT = ps("PT", [128, 8, 128], BF16)
        B_PT = Buf("PT", psum=True)
        PS = [ps(f"PS{i}", [128, 512], F32) for i in range(7)]
        B_PS = [Buf(f"PS{i}", psum=True) for i in range(7)]

        def wv(off, k, n):
            return WB[:, off:off + k * n].rearrange("p (k n) -> p k n", k=k)
        W_in = wv(0, 8, SSM_IN)
        W_out = wv(41216, 16, 1024)
        W_g = wv(57600, 8, 1024)
        W_p = wv(65792, 2, 1024)
        W_ai = wv(0, 8, 2048)
        W_ao = wv(16384, 8, 1024)
        W_kv = wv(24576, 8, 512)
        KT = wv(28672, 2, 4096)
        Vst = WB[:, 36864:36864 + 8320].rearrange("p (c g e) -> p c g e", c=32, g=4)
        B_KT = [Buf(f"KT{c}") for c in range(NCH)]
        B_V = [Buf(f"V{c}") for c in range(NCH)]
        B_Vones = Buf("Vones")
        cosT = St[:, 0:1024].rearrange("p (c f) -> p c f", c=32)
        sinT = St[:, 1024:2048].rearrange("p (c f) -> p c f", c=32)

        P.dma("sp", lambda e: [e.dma_start(out=c32[:], in_=cst_d)], "cst", 1, W=[B_cst])
        P.dma("pool", lambda e: [e.dma_start(out=cbf[:], in_=cst_d)], "cstb", 1, W=[B_cst])

        hstores = {}

        def load_weights(pairs, L):
            fns = []
            for dst, src, ns in pairs:
                n = dst.shape[1]
                step = n // ns
                for i in range(ns):
                    fns.append((dst[:, i * step:(i + 1) * step], src[:, i * step:(i + 1) * step]))
            P.dma("pool", lambda e, fns=fns: [e.dma_start(out=d_, in_=s_) for d_, s_ in fns], f"wload{L}", len(fns),
                  W=[B_WB])

        def rstd_from_ss(stt, col, n, bst):
            P.op("act", lambda e: e.activation(out=stt[:, col + 1:col + 2], in_=stt[:, col:col + 1], func=AF.Ln,
                                               scale=1.0 / n, bias=lsm[:, 2, 0:1]), R=[bst, B_lsm], W=[bst])
            P.op("act", lambda e: e.activation(out=stt[:, col + 2:col + 3], in_=stt[:, col + 1:col + 2], func=AF.Exp,
                                               scale=-0.5), R=[bst], W=[bst])

        def norm_T(hh, bh, stt, bst, col, nwoff, have_rstd=False, alt=None):
            ub, Bub, uTx, BuTx, pt0 = alt if alt is not None else (u_bf[:], B_u, uT[:], B_uT, None)
            if not have_rstd:
                P.op("act", lambda e: e.activation(out=ub, in_=hh[:], func=AF.Square,
                                                   accum_out=stt[:, col:col + 1]), R=[bh, bst], W=[Bub, bst])
                rstd_from_ss(stt, col, D, bst)
            P.op("act", lambda e: e.activation(out=ub, in_=hh[:], func=AF.Copy, scale=stt[:, col + 2:col + 3]),
                 R=[bh, bst], W=[Bub])
            if pt0 is None:
                for k in range(8):
                    P.op("pe", lambda e, k=k: e.transpose(out=PT[:, k, :], in_=ub[:, k * 128:(k + 1) * 128],
                                                          identity=ident), R=[Bub, B_cst], W=[B_PT])
                P.op("dve", lambda e: e.tensor_tensor(out=uTx, in0=PT[:],
                                                      in1=LV[:, nwoff:nwoff + 8].unsqueeze(2).to_broadcast([128, 8, 128]),
                                                      op=ALU.mult), R=[B_PT, B_LV], W=[BuTx])
            else:
                for hf in range(2):
                    for i in range(4):
                        k = 4 * hf + i
                        P.op("pe", lambda e, k=k, i=i: e.transpose(out=PT[:, pt0 + i, :], in_=ub[:, k * 128:(k + 1) * 128],
                                                                   identity=ident), R=[Bub, B_cst], W=[B_PT])
                    P.op("dve", lambda e, hf=hf: e.tensor_tensor(
                        out=uTx[:, 4 * hf:4 * hf + 4, :], in0=PT[:, pt0:pt0 + 4, :],
                        in1=LV[:, nwoff + 4 * hf:nwoff + 4 * hf + 4].unsqueeze(2).to_broadcast([128, 4, 128]),
                        op=ALU.mult), R=[B_PT, B_LV], W=[BuTx])

        def ple(hh, bh, stt, bst, pt, bpt, pnwoff, alt=None, banks=(0, 1, 2, 3), tv=None):
            norm_T(hh, bh, stt, bst, 3, pnwoff, alt=alt)
            uTx, BuTx = (alt[2], alt[3]) if alt is not None else (uT[:], B_uT)
            if tv is None:
                tg, vv, Btg, Bvv = scrA[:, 0:1024], scrA[:, 1024:2048], B_A[0], B_A[1]
            else:
                tg, vv, Btg, Bvv = tv
            for nb in range(2):
                for k in range(8):
                    P.op("pe", lambda e, nb=nb, k=k: e.matmul(PS[banks[nb]][:], lhsT=uTx[:, k, :],
                                                             rhs=W_g[:, k, nb * 512:(nb + 1) * 512],
                                                             start=(k == 0), stop=(k == 7)),
                         R=[BuTx, B_WB], W=[B_PS[banks[nb]]])
            for nb in range(2):
                for k in range(2):
                    P.op("pe", lambda e, nb=nb, k=k: e.matmul(PS[banks[2 + nb]][:], lhsT=pt[:, k, :],
                                                             rhs=W_p[:, k, nb * 512:(nb + 1) * 512],
                                                             start=(k == 0), stop=(k == 1)),
                         R=[bpt, B_WB], W=[B_PS[banks[2 + nb]]])
            for nb in range(2):
                sl = slice(nb * 512, (nb + 1) * 512)
                P.op("act", lambda e, nb=nb, sl=sl: e.activation(out=tg[:, sl], in_=PS[banks[nb]][:], func=AF.Tanh, scale=0.5),
                     R=[B_PS[banks[nb]]], W=[Btg])
                P.op("dve", lambda e, nb=nb, sl=sl: e.scalar_tensor_tensor(out=vv[:, sl], in0=tg[:, sl], scalar=1.0,
                                                                           in1=PS[banks[2 + nb]][:], op0=ALU.add, op1=ALU.mult),
                     R=[Btg, B_PS[banks[2 + nb]]], W=[Bvv])
            P.op("dve", lambda e: e.scalar_tensor_tensor(out=hh[:], in0=vv, scalar=0.5, in1=hh[:], op0=ALU.mult,
                                                         op1=ALU.add), R=[Bvv, bh], W=[bh])

        def load_chunk(L, c):
            slot = c % 2
            src = x_d if L == layers[0] else out_d
            deps = [] if L == layers[0] else [hstores[(L - 1, c)]]
            i1 = P.dma("sp", lambda e: [e.dma_start(out=h_t[slot][:], in_=src[c * 128:(c + 1) * 128, :])],
                       f"hld{slot}_{L}", 1, W=[B_h[slot]])
            i1.deps.extend(deps)
            P.dma("pool", lambda e: [e.dma_start(out=pT_t[slot][:],
                                                 in_=pT_d[L, :, c * 128:(c + 1) * 128].rearrange("(k p) t -> p k t", p=128))],
                  f"pld{slot}_{L}", 1, W=[B_pT[slot]])

        def store_chunk(L, c):
            slot = c % 2
            hstores[(L, c)] = P.dma("sp", lambda e: [e.dma_start(out=out_d[c * 128:(c + 1) * 128, :], in_=h_t[slot][:])],
                                    f"hst{slot}_{L}", 1, R=[B_h[slot]])

        def layer_common_prep(L):
            P.dma("sp", lambda e: [e.dma_start(out=LV[:], in_=lv_d[L])], f"lv{L}", 1, W=[B_LV])
            P.op("dve", lambda e: e.memset(lsm[:, 2, :], EPS), W=[B_lsm])

        def mamba_layer(L):
            layer_common_prep(L)
            load_weights([(WB[:, 0:41216], w_in_d[L], 8), (WB[:, 41216:57600], w_out_d[L], 4),
                          (WB[:, 57600:65792], w_g_d[L], 2), (WB[:, 65792:67840], w_p_d[L], 1)], L)
            NW, PNW, GW, CW, CB, DTB, ALOG, DSK = 0, 8, 16, 32, 128, 152, 184, 216
            cw = LV[:, CW:CW + 96].rearrange("p (t k) -> p t k", k=4)
            P.op("act", lambda e: e.activation(out=lsm[:, 0, :], in_=LV[:, ALOG:ALOG + 32], func=AF.Exp),
                 R=[B_LV], W=[B_lsm])
            P.op("dve", lambda e: e.tensor_scalar(out=lsm[:, 0, :], in0=lsm[:, 0, :], scalar1=-1.0, scalar2=None,
                                                  op0=ALU.mult), R=[B_lsm], W=[B_lsm])
            P.op("pool", lambda e: e.memset(St[:], 0.0), W=B_St)
            P.op("pool", lambda e: e.memset(scrD[:], 0.0), W=B_D)
            P.op("pool", lambda e: e.memset(ctail[:], 0.0), W=B_ctail)
            load_chunk(L, 0)
            for c in range(nch):
                if c + 1 < nch:
                    load_chunk(L, c + 1)
                slot = c % 2
                hh, bh, stt, bst = h_t[slot], B_h[slot], stat[slot], B_stat[slot]
                P.op("pool", lambda e, stt=stt: e.memset(stt[:], 0.0), W=[bst])
                norm_T(hh, bh, stt, bst, 0, NW)
                SMB = B_PS[1]
                for k in range(8):
                    P.op("pe", lambda e, k=k: e.matmul(PS[1][:, 0:32], lhsT=uT[:, k, :], rhs=W_in[:, k, 5120:5152],
                                                       start=(k == 0), stop=(k == 7)), R=[B_uT, B_WB], W=[SMB])
                dtr, dt_, dta, acs, tmp, dte, ea, cd, ee = (sm[:, i, :] for i in range(9))
                P.op("dve", lambda e: e.tensor_tensor(out=dtr, in0=PS[1][:, 0:32], in1=LV[:, DTB:DTB + 32], op=ALU.add),
                     R=[SMB, B_LV], W=[B_sm])
                P.op("act", lambda e: e.activation(out=ee, in_=dtr, func=AF.Exp), R=[B_sm], W=[B_sm])
                P.op("act", lambda e: e.activation(out=dt_, in_=ee, func=AF.Ln, bias=1.0), R=[B_sm], W=[B_sm])
                P.op("dve", lambda e: e.tensor_tensor(out=dta, in0=dt_, in1=lsm[:, 0, :], op=ALU.mult),
                     R=[B_sm, B_lsm], W=[B_sm])
                P.op("pe", lambda e: e.matmul(PS[1][:, 32:64], lhsT=U32, rhs=dta, start=True, stop=True),
                     R=[B_sm, B_cst], W=[SMB])
                P.op("pe", lambda e: e.matmul(PS[1][:, 64:96], lhsT=ones32, rhs=dta, start=True, stop=True),
                     R=[B_sm, B_cst], W=[SMB])
                P.op("dve", lambda e: e.tensor_copy(out=acs, in_=PS[1][:, 32:64]), R=[SMB], W=[B_sm])
                P.op("dve", lambda e: e.tensor_tensor(out=tmp, in0=PS[1][:, 64:96], in1=acs, op=ALU.subtract),
                     R=[SMB, B_sm], W=[B_sm])
                P.op("act", lambda e: e.activation(out=dte, in_=tmp, func=AF.Exp), R=[B_sm], W=[B_sm])
                P.op("act", lambda e: e.activation(out=ea, in_=acs, func=AF.Exp), R=[B_sm], W=[B_sm])
                P.op("act", lambda e: e.activation(out=cd, in_=PS[1][:, 64:96], func=AF.Exp), R=[SMB], W=[B_sm])
                def stageA(g):
                    q = g % 2
                    aT, xd, xs_, xw_, bt = actT2[q], xdt2[q], xsD2[q], xw2[q], Btm2[q]
                    BaT, Bxd, Bxs, Bxw, Bbt = B_actT2[q], B_xdt2[q], B_xsD2[q], B_xw2[q], B_Btm2[q]
                    cols = [2048 + 512 * g + 128 * t for t in range(4)] + [4096 + 128 * g, 4608 + 128 * g]
                    tiles = [4 * g + t for t in range(4)] + [16 + g, 20 + g]
                    for t in range(6):
                        bank, pos = (0, t) if t < 4 else (1, t - 4)
                        for k in range(8):
                            P.op("pe", lambda e, t=t, k=k, bank=bank, pos=pos: e.matmul(
                                PS[bank][:, pos * 128:(pos + 1) * 128], lhsT=W_in[:, k, cols[t]:cols[t] + 128],
                                rhs=uT[:, k, :], start=(k == 0), stop=(k == 7)), R=[B_uT, B_WB], W=[B_PS[bank]])
                        P.op("pool", lambda e, t=t: e.tensor_copy(out=xraw[:, t, 0:3], in_=ctail[:, tiles[t], :]),
                             R=[B_ctail[g]], W=[B_xr[t]])
                        yield
                    P.op("act", lambda e: e.activation(out=xraw[:, 0:4, 3:131],
                                                       in_=PS[0][:].rearrange("p (t n) -> p t n", t=4), func=AF.Copy),
                         R=[B_PS[0]], W=B_xr[0:4])
                    P.op("act", lambda e: e.activation(out=xraw[:, 4:6, 3:131],
                                                       in_=PS[1][:, 0:256].rearrange("p (t n) -> p t n", t=2), func=AF.Copy),
                         R=[B_PS[1]], W=B_xr[4:6])
                    yield
                    for k in range(8):
                        P.op("pe", lambda e, k=k: e.matmul(PS[0][:], lhsT=uT[:, k, :], rhs=W_in[:, k, 512 * g:512 * (g + 1)],
                                                           start=(k == 0), stop=(k == 7)), R=[B_uT, B_WB], W=[B_PS[0]])
                    yield
                    for t in range(6):
                        ti = tiles[t]
                        P.op("pool", lambda e, t=t: e.tensor_copy(out=ctail[:, tiles[t], :], in_=xraw[:, t, 128:131]),
                             R=[B_xr[t]], W=[B_ctail[g]])
                        P.op("act", lambda e, t=t, ti=ti: e.activation(out=cacc[:, t, :], in_=xraw[:, t, 3:131],
                                                                       func=AF.Identity, scale=cw[:, ti, 3:4],
                                                                       bias=LV[:, CB + ti:CB + ti + 1]),
                             R=[B_xr[t], B_LV], W=[B_ca[t]])
                        for k in range(3):
                            P.op("dve", lambda e, t=t, ti=ti, k=k: e.scalar_tensor_tensor(
                                out=cacc[:, t, :], in0=xraw[:, t, k:k + 128], scalar=cw[:, ti, k:k + 1],
                                in1=cacc[:, t, :], op0=ALU.mult, op1=ALU.add), R=[B_xr[t], B_LV, B_ca[t]], W=[B_ca[t]])
                        yield
                    P.op("act", lambda e: e.activation(out=scrB[:, 1024 + 512 * q:1536 + 512 * q], in_=PS[0][:], func=AF.Silu),
                         R=[B_PS[0]], W=[B_B[2 + q]])
                    P.op("act", lambda e: e.activation(out=aT[:], in_=cacc[:], func=AF.Silu), R=B_ca, W=[BaT])
                    yield
                    for t in range(5):
                        P.op("pe", lambda e, t=t: e.transpose(out=PT[:, t, :], in_=aT[:, t, :], identity=ident),
                             R=[BaT, B_cst], W=[B_PT])
                    yield
                    PTx = PT[:, 0:4, :].rearrange("p t (a d) -> p (t a) d", a=2)
                    P.op("dve", lambda e: e.tensor_tensor(out=xd[:].rearrange("p (h d) -> p h d", h=8), in0=PTx,
                                                          in1=dt_[:, 8 * g:8 * g + 8].unsqueeze(2).to_broadcast([128, 8, 64]),
                                                          op=ALU.mult), R=[B_PT, B_sm], W=[Bxd])
                    yield
                    P.op("dve", lambda e: e.tensor_tensor(out=xs_[:].rearrange("p (h d) -> p h d", h=8), in0=PTx,
                                                          in1=LV[:, DSK + 8 * g:DSK + 8 * g + 8].unsqueeze(2).to_broadcast([128, 8, 64]),
                                                          op=ALU.mult), R=[B_PT, B_LV], W=[Bxs])
                    P.op("act", lambda e: e.activation(out=bt[:], in_=PT[:, 4, :], func=AF.Copy), R=[B_PT], W=[Bbt])
                    yield
                    P.op("pool", lambda e: e.tensor_tensor(out=xw_[:].rearrange("p (h d) -> p h d", h=8),
                                                          in0=xd[:].rearrange("p (h d) -> p h d", h=8),
                                                          in1=dte[:, 8 * g:8 * g + 8].unsqueeze(2).to_broadcast([128, 8, 64]),
                                                          op=ALU.mult), R=[Bxd, B_sm], W=[Bxw])
                    yield

                def stageB(g):
                    q = g % 2
                    aT, xd, xs_, xw_, bt = actT2[q], xdt2[q], xsD2[q], xw2[q], Btm2[q]
                    BaT, Bxd, Bxs, Bxw, Bbt = B_actT2[q], B_xdt2[q], B_xsD2[q], B_xw2[q], B_Btm2[q]
                    szz = scrB[:, 1024 + 512 * q:1536 + 512 * q]
                    Bsz = B_B[2 + q]
                    P.op("pe", lambda e: e.matmul(PS[4][:, 0:128], lhsT=aT[:, 4, :], rhs=aT[:, 5, :], start=True,
                                                  stop=True), R=[BaT], W=[B_PS[4]])
                    rseg = scrA[:, 0:1024].rearrange("p (h i) -> p h i", h=8)
                    Ex = scrA[:, 1024:2048].rearrange("p (h i) -> p h i", h=8)
                    P.op("pool", lambda e: e.tensor_tensor(out=rseg, in0=U32.unsqueeze(1).to_broadcast([128, 8, 128]),
                                                          in1=dta[:, 8 * g:8 * g + 8].unsqueeze(2).to_broadcast([128, 8, 128]),
                                                          op=ALU.mult), R=[B_cst, B_sm], W=[B_A[0]])
                    yield
                    P.op("dve", lambda e: e.tensor_tensor(out=cbm[:], in0=PS[4][:, 0:128], in1=U32, op=ALU.mult),
                         R=[B_PS[4], B_cst], W=[B_cbm])
                    for hb in range(2):
                        P.op("pe", lambda e, hb=hb: e.matmul(PS[2 + hb][:], lhsT=L32,
                                                             rhs=scrA[:, hb * 512:(hb + 1) * 512], start=True, stop=True),
                             R=[B_A[0], B_cst], W=[B_PS[2 + hb]])
                    yield
                    for hb in range(2):
                        P.op("act", lambda e, hb=hb: e.activation(out=scrA[:, 1024 + hb * 512:1024 + (hb + 1) * 512],
                                                                  in_=PS[2 + hb][:], func=AF.Exp),
                             R=[B_PS[2 + hb]], W=[B_A[1]])
                        yield
                    MT = scrC[:, 0:1024].rearrange("p (h i) -> p h i", h=8)
                    P.op("dve", lambda e: e.tensor_tensor(out=MT, in0=Ex, in1=cbm[:].unsqueeze(1).to_broadcast([128, 8, 128]),
                                                          op=ALU.mult), R=[B_A[1], B_cbm], W=[B_C[0]])
                    P.op("pe", lambda e: e.matmul(PS[4][:], lhsT=bt[:], rhs=xw_[:], start=True, stop=True),
                         R=[Bbt, Bxw], W=[B_PS[4]])
                    yield
                    for hd in range(8):
                        P.op("pe", lambda e, hd=hd: e.matmul(PS[2][:, hd * 64:(hd + 1) * 64], lhsT=MT[:, hd, :],
                                                             rhs=xd[:, hd * 64:(hd + 1) * 64], start=True, stop=False),
                             R=[B_C[0], Bxd], W=[B_PS[2]])
                        P.op("pe", lambda e, hd=hd: e.matmul(PS[2][:, hd * 64:(hd + 1) * 64], lhsT=ident,
                                                             rhs=xs_[:, hd * 64:(hd + 1) * 64], start=False, stop=True),
                             R=[B_cst, Bxs], W=[B_PS[2]])
                    P.op("pe", lambda e: e.matmul(PS[3][:], lhsT=aT[:, 5, :], rhs=scrD[:, g * 512:(g + 1) * 512],
                                                  start=True, stop=True), R=[BaT, B_D[g]], W=[B_PS[3]])
                    yield
                    Sg = St[:, g * 512:(g + 1) * 512]
                    P.op("pool", lambda e: e.tensor_tensor(out=Sg.rearrange("p (h d) -> p h d", h=8),
                                                          in0=Sg.rearrange("p (h d) -> p h d", h=8),
                                                          in1=cd[:, 8 * g:8 * g + 8].unsqueeze(2).to_broadcast([128, 8, 64]),
                                                          op=ALU.mult), R=[B_St[g], B_sm], W=[B_St[g]])
                    yield
                    P.op("dve", lambda e: e.tensor_tensor(out=Sg, in0=PS[4][:], in1=Sg, op=ALU.add),
                         R=[B_PS[4], B_St[g]], W=[B_St[g]])
                    yield
                    ty = scrB[:, 0:512]
                    yy = scrB[:, 512:1024]
                    P.op("dve", lambda e: e.tensor_tensor(out=ty.rearrange("p (h d) -> p h d", h=8),
                                                          in0=PS[3][:].rearrange("p (h d) -> p h d", h=8),
                                                          in1=ea[:, 8 * g:8 * g + 8].unsqueeze(2).to_broadcast([128, 8, 64]),
                                                          op=ALU.mult), R=[B_PS[3], B_sm], W=[B_B[0]])
                    P.op("act", lambda e: e.activation(out=scrD[:, g * 512:(g + 1) * 512], in_=Sg, func=AF.Copy),
                         R=[B_St[g]], W=[B_D[g]])
                    yield
                    P.op("dve", lambda e: e.tensor_tensor(out=yy, in0=PS[2][:], in1=ty, op=ALU.add),
                         R=[B_PS[2], B_B[0]], W=[B_B[1]])
                    yield
                    P.op("pool", lambda e: e.tensor_tensor(out=yy, in0=yy, in1=szz, op=ALU.mult), R=[B_B[1], Bsz],
                         W=[B_B[1]])
                    yield
                    yn = scrC[:, 1024:1536]
                    ynT = scrC[:, 1536:2048].rearrange("p (t n) -> p t n", t=4)
                    P.op("act", lambda e: e.activation(out=yn, in_=yy, func=AF.Square,
                                                       accum_out=stt[:, 8 + 3 * g:9 + 3 * g]), R=[B_B[1], bst],
                         W=[B_C[1], bst])
                    yield
                    rstd_from_ss(stt, 8 + 3 * g, 512, bst)
                    yield
                    P.op("act", lambda e: e.activation(out=yn, in_=yy, func=AF.Copy, scale=stt[:, 10 + 3 * g:11 + 3 * g]),
                         R=[B_B[1], bst], W=[B_C[1]])
                    yield
                    for half in range(2):
                        for t in range(2):
                            tt = 2 * half + t
                            P.op("pe", lambda e, t=t, tt=tt: e.transpose(out=PT[:, 5 + t, :], in_=yn[:, tt * 128:(tt + 1) * 128],
                                                                         identity=ident), R=[B_C[1], B_cst], W=[B_PT])
                        P.op("dve", lambda e, half=half: e.tensor_tensor(
                            out=ynT[:, 2 * half:2 * half + 2, :], in0=PT[:, 5:7, :],
                            in1=LV[:, GW + 4 * g + 2 * half:GW + 4 * g + 2 * half + 2].unsqueeze(2).to_broadcast([128, 2, 128]),
                            op=ALU.mult), R=[B_PT, B_LV], W=[B_C[2]])
                        yield
                    for nb in range(2):
                        for t in range(4):
                            P.op("pe", lambda e, nb=nb, t=t: e.matmul(PS[5 + nb][:], lhsT=ynT[:, t, :],
                                                                     rhs=W_out[:, 4 * g + t, nb * 512:(nb + 1) * 512],
                                                                     start=(g == 0 and t == 0), stop=(g == 3 and t == 3)),
                                 R=[B_C[2], B_WB], W=[B_PS[5 + nb]])
                    yield

                def interleave(*gens):
                    gens = list(gens)
                    while gens:
                        for gn in list(gens):
                            try:
                                next(gn)
                            except StopIteration:
                                gens.remove(gn)

                interleave(stageA(0))
                interleave(stageA(1), stageB(0))
                interleave(stageA(2), stageB(1))
                interleave(stageA(3), stageB(2))
                interleave(stageB(3))
                for nb in range(2):
                    sl = slice(nb * 512, (nb + 1) * 512)
                    P.op("dve", lambda e, nb=nb, sl=sl: e.tensor_tensor(out=hh[:, sl], in0=PS[5 + nb][:], in1=hh[:, sl],
                                                                        op=ALU.add), R=[B_PS[5 + nb], bh], W=[bh])
                ple(hh, bh, stt, bst, pT_t[slot], B_pT[slot], PNW)
                store_chunk(L, c)

        def rope_tables():
            posf = kf[:, 0, 0:32]
            ivf = kf[:, 0, 32:64]
            pi32 = kf[:, 0, 64:96].bitcast(I32)
            P.dma("sp", lambda e: [e.dma_start(out=pi32, in_=pos_d)], "pos", 1, W=[B_kf])
            P.dma("sp", lambda e: [e.dma_start(out=ivf, in_=invf_d)], "invf", 1, W=[B_kf])
            P.op("dve", lambda e: e.tensor_copy(out=posf, in_=pi32), R=[B_kf], W=[B_kf])
            ang = scrA[:, 0:1024].rearrange("p (c f) -> p c f", c=32)
            red = scrA[:, 1024:2048].rearrange("p (c f) -> p c f", c=32)
            redi = scrB[:, 0:1024].bitcast(I32).rearrange("p (c f) -> p c f", c=32)
            redf = scrB[:, 1024:2048].rearrange("p (c f) -> p c f", c=32)
            P.op("dve", lambda e: e.tensor_tensor(out=ang, in0=posf.unsqueeze(2).to_broadcast([128, 32, 32]),
                                                  in1=ivf.unsqueeze(1).to_broadcast([128, 32, 32]), op=ALU.mult),
                 R=[B_kf], W=[B_A[0]])
            for which, dst in ((0, sinT), (1, cosT)):
                shift = 0.0 if which == 0 else float(np.pi / 2)
                P.op("dve", lambda e, shift=shift: e.tensor_scalar(out=red, in0=ang, scalar1=shift, scalar2=None, op0=ALU.add),
                     R=[B_A[0]], W=[B_A[1]])
                P.op("dve", lambda e: e.tensor_scalar(out=redi, in0=red, scalar1=float(1 / (2 * np.pi)), scalar2=None,
                                                      op0=ALU.mult), R=[B_A[1]], W=[B_B[0], B_B[1]])
                P.op("dve", lambda e: e.tensor_copy(out=redf, in_=redi), R=[B_B[0], B_B[1]], W=[B_B[2], B_B[3]])
                P.op("dve", lambda e: e.scalar_tensor_tensor(out=red, in0=redf, scalar=float(-2 * np.pi), in1=red,
                                                             op0=ALU.mult, op1=ALU.add), R=[B_B[2], B_B[3], B_A[1]], W=[B_A[1]])
                P.op("dve", lambda e: e.tensor_scalar(out=redf, in0=red, scalar1=float(np.pi), scalar2=float(-2 * np.pi),
                                                      op0=ALU.is_ge, op1=ALU.mult), R=[B_A[1]], W=[B_B[2], B_B[3]])
                P.op("dve", lambda e: e.tensor_tensor(out=red, in0=red, in1=redf, op=ALU.add), R=[B_A[1], B_B[2], B_B[3]],
                     W=[B_A[1]])
                P.op("dve", lambda e: e.tensor_scalar(out=redf, in0=red, scalar1=float(-np.pi), scalar2=float(2 * np.pi),
                                                      op0=ALU.is_lt, op1=ALU.mult), R=[B_A[1]], W=[B_B[2], B_B[3]])
                P.op("dve", lambda e: e.tensor_tensor(out=red, in0=red, in1=redf, op=ALU.add), R=[B_A[1], B_B[2], B_B[3]],
                     W=[B_A[1]])
                P.op("act", lambda e, dst=dst: e.activation(out=dst, in_=red, func=AF.Sin), R=[B_A[1]], W=B_St)

        def rope_apply(dst_bf, src, nh, c, R, W):
            s3 = src.rearrange("p (h d) -> p h d", h=nh)
            d3 = dst_bf.rearrange("p (h d) -> p h d", h=nh)
            n = nh * 32
            ta = (scrA[:, 1024:1024 + n] if nh == 16 else kf[:, 1, 0:n]).rearrange("p (h d) -> p h d", h=nh)
            tb = (scrA[:, 1536:1536 + n] if nh == 16 else kf[:, 1, 128:128 + n]).rearrange("p (h d) -> p h d", h=nh)
            Bt = [B_A[1]] if nh == 16 else [B_kf]
            cosb = cosT[:, c, :].unsqueeze(1).to_broadcast([128, nh, 32])
            sinb = sinT[:, c, :].unsqueeze(1).to_broadcast([128, nh, 32])
            x1 = s3[:, :, 0:32]
            x2 = s3[:, :, 32:64]
            P.op("dve", lambda e: e.tensor_tensor(out=ta, in0=x1, in1=cosb, op=ALU.mult), R=R + B_St, W=Bt)
            P.op("dve", lambda e: e.tensor_tensor(out=tb, in0=x2, in1=sinb, op=ALU.mult), R=R + B_St, W=Bt)
            P.op("dve", lambda e: e.tensor_tensor(out=d3[:, :, 0:32], in0=ta, in1=tb, op=ALU.subtract), R=Bt, W=W)
            P.op("dve", lambda e: e.tensor_tensor(out=ta, in0=x2, in1=cosb, op=ALU.mult), R=R + B_St, W=Bt)
            P.op("dve", lambda e: e.tensor_tensor(out=tb, in0=x1, in1=sinb, op=ALU.mult), R=R + B_St, W=Bt)
            P.op("dve", lambda e: e.tensor_tensor(out=d3[:, :, 32:64], in0=ta, in1=tb, op=ALU.add), R=Bt, W=W)

        def head_norm(src, nh, woff, stt, bst, scol, sqbuf, Bsq, R):
            s3 = src.rearrange("p (h d) -> p h d", h=nh)
            q3 = sqbuf.rearrange("p (h d) -> p h d", h=nh)
            P.op("dve", lambda e: e.tensor_tensor(out=sqbuf, in0=src, in1=src, op=ALU.mult), R=R, W=Bsq)
            P.op("dve", lambda e: e.tensor_reduce(out=stt[:, scol:scol + nh], in_=q3, axis=AX.X, op=ALU.add),
                 R=Bsq, W=[bst])
            P.op("act", lambda e: e.activation(out=stt[:, scol + nh:scol + 2 * nh], in_=stt[:, scol:scol + nh], func=AF.Ln,
                                               scale=1.0 / 64, bias=lsm[:, 2, 0:1]), R=[bst, B_lsm], W=[bst])
            P.op("act", lambda e: e.activation(out=stt[:, scol + 2 * nh:scol + 3 * nh], in_=stt[:, scol + nh:scol + 2 * nh],
                                               func=AF.Exp, scale=-0.5), R=[bst], W=[bst])
            P.op("dve", lambda e: e.tensor_tensor(out=s3, in0=s3,
                                                  in1=stt[:, scol + 2 * nh:scol + 3 * nh].unsqueeze(2).to_broadcast([128, nh, 64]),
                                                  op=ALU.mult), R=R + [bst], W=R)
            P.op("dve", lambda e: e.tensor_tensor(out=s3, in0=s3,
                                                  in1=LV[:, woff:woff + 64].unsqueeze(1).to_broadcast([128, nh, 64]),
                                                  op=ALU.mult), R=R + [B_LV], W=R)

        def attn_layer(L):
            j = L - 2
            layer_common_prep(L)
            ANW, PNW, KVNW, KNW, QNW, SNK = 0, 8, 16, 32, 96, 160
            pairs = [(WB[:, 0:16384], w_ai_d[j], 4), (WB[:, 16384:24576], w_ao_d[j], 2),
                     (WB[:, 57600:65792], w_g_d[L], 2), (WB[:, 65792:67840], w_p_d[L], 1)]
            if j == 0:
                pairs.append((WB[:, 24576:28672], w_kv_d, 1))
            load_weights(pairs, L)
            if j == 0:
                rope_tables()
                Vflat = WB[:, 36864:36864 + 8320].rearrange("p (n e) -> p n e", e=65)
                P.op("dve", lambda e: e.memset(Vflat[:, :, 64:65], 1.0), R=[B_WB], W=[B_Vones])
            off = 45184
            q2 = [scrD[:].rearrange("p (t a n) -> p t a n", t=8, a=2),
                  WB[:, off:off + 2048].rearrange("p (t a n) -> p t a n", t=8, a=2)]
            B_q2 = [Buf("q2a"), Buf("q2b")]
            sgb = [scrB[:, 1024:2048], WB[:, off + 2048:off + 4096].bitcast(F32)]
            B_sg = [Buf("sga"), Buf("sgb")]
            uTb = WB[:, off + 4096:off + 5120].rearrange("p (k n) -> p k n", k=8)
            ubb = WB[:, off + 5120:off + 6144]
            B_uTb, B_ubb = Buf("uTb"), Buf("ubb")
            tgb = WB[:, off + 6144:off + 8192].bitcast(F32)
            vvb = WB[:, off + 8192:off + 10240].bitcast(F32)
            B_tgb, B_vvb = Buf("tgb"), Buf("vvb")
            B_den = Buf("den")
            P.op("pool", lambda e: e.memset(scrD[:], 0.0), R=[B_WB], W=B_D + [B_q2[0]])
            P.op("pool", lambda e: e.memset(WB[:, off:off + 2048], 0.0), R=[B_WB], W=[B_q2[1]])
            P.op("act", lambda e: e.activation(out=lsm[:, 1, 0:16], in_=LV[:, SNK:SNK + 16], func=AF.Exp), R=[B_LV],
                 W=[B_lsm])

            def front(c):
                slot = c % 2
                hh, bh, stt, bst = h_t[slot], B_h[slot], stat[slot], B_stat[slot]
                altF = (u_bf[:], B_u, uT[:], B_uT, 0)
                P.op("pool", lambda e, stt=stt: e.memset(stt[:], 0.0), W=[bst])
                yield
                if j == 0:
                    norm_T(hh, bh, stt, bst, 0, KVNW, alt=altF)
                    yield
                    for k in range(8):
                        P.op("pe", lambda e, k=k: e.matmul(PS[3][:], lhsT=uT[:, k, :], rhs=W_kv[:, k, :], start=(k == 0),
                                                           stop=(k == 7)), R=[B_uT, B_WB], W=[B_PS[3]])
                    kfl = kf[:, 0, :]
                    P.op("act", lambda e: e.activation(out=kfl, in_=PS[3][:, 0:256], func=AF.Copy), R=[B_PS[3]], W=[B_kf])
                    P.op("act", lambda e, c=c: e.activation(out=Vst[:, c, :, 0:64],
                                                            in_=PS[3][:, 256:512].rearrange("p (g d) -> p g d", g=4),
                                                            func=AF.Copy), R=[B_PS[3], B_WB], W=[B_V[c]])
                    yield
                    head_norm(kfl, 4, KNW, stt, bst, 24, kf[:, 2, :], [B_kf], [B_kf])
                    yield
                    rope_apply(krbuf, kfl, 4, c, [B_kf], [B_kr])
                    yield
                    for t in range(2):
                        P.op("pe", lambda e, t=t: e.transpose(out=PT[:, t, :], in_=krbuf[:, t * 128:(t + 1) * 128],
                                                              identity=ident), R=[B_kr, B_cst], W=[B_PT])
                    P.op("act", lambda e, c=c: e.activation(out=KT[:, :, c * 128:(c + 1) * 128], in_=PT[:, 0:2, :],
                                                            func=AF.Copy), R=[B_PT, B_WB], W=[B_KT[c]])
                    yield
                norm_T(hh, bh, stt, bst, 0, ANW, have_rstd=(j == 0), alt=altF)
                yield
                qf = scrA[:, 0:1024]
                for nb in range(2):
                    for k in range(8):
                        P.op("pe", lambda e, nb=nb, k=k: e.matmul(PS[3 + nb][:], lhsT=uT[:, k, :],
                                                                 rhs=W_ai[:, k, nb * 512:(nb + 1) * 512],
                                                                 start=(k == 0), stop=(k == 7)),
                             R=[B_uT, B_WB], W=[B_PS[3 + nb]])
                    P.op("act", lambda e, nb=nb: e.activation(out=qf[:, nb * 512:(nb + 1) * 512], in_=PS[3 + nb][:],
                                                              func=AF.Copy), R=[B_PS[3 + nb]], W=[B_A[0]])
                    yield
                for nb in range(2):
                    for k in range(8):
                        P.op("pe", lambda e, nb=nb, k=k: e.matmul(PS[3 + nb][:], lhsT=uT[:, k, :],
                                                                 rhs=W_ai[:, k, 1024 + nb * 512:1024 + (nb + 1) * 512],
                                                                 start=(k == 0), stop=(k == 7)),
                             R=[B_uT, B_WB], W=[B_PS[3 + nb]])
                    P.op("act", lambda e, nb=nb: e.activation(out=sgb[slot][:, nb * 512:(nb + 1) * 512], in_=PS[3 + nb][:],
                                                              func=AF.Silu), R=[B_PS[3 + nb]], W=[B_sg[slot]])
                    yield
                head_norm(qf, 16, QNW, stt, bst, 8, scrA[:, 1024:2048], [B_A[1]], [B_A[0]])
                yield
                qr = scrC[:, 0:1024]
                rope_apply(qr, qf, 16, c, [B_A[0]], [B_C[0]])
                yield
                qT2 = q2[slot]
                for hf in range(2):
                    for i in range(4):
                        t = 4 * hf + i
                        P.op("pe", lambda e, t=t, i=i: e.transpose(out=PT[:, i, :], in_=qr[:, t * 128:(t + 1) * 128],
                                                                   identity=ident), R=[B_C[0], B_cst], W=[B_PT])
                    P.op("act", lambda e, hf=hf: e.activation(out=qT2[0:64, 4 * hf:4 * hf + 4, 0, :], in_=PT[0:64, 0:4, :],
                                                              func=AF.Copy), R=[B_PT], W=[B_q2[slot]])
                    P.op("act", lambda e, hf=hf: e.activation(out=qT2[64:128, 4 * hf:4 * hf + 4, 1, :], in_=PT[64:128, 0:4, :],
                                                              func=AF.Copy), R=[B_PT], W=[B_q2[slot]])
                    yield

            def back(c):
                slot = c % 2
                hh, bh, stt, bst = h_t[slot], B_h[slot], stat[slot], B_stat[slot]
                qT2 = q2[slot]
                blks = [c - 1, c] if c > 0 else [c]
                nb_ = len(blks)
                for r in range(8):
                    sb_ = 5 + (r % 2)
                    es = r % 2
                    for hh_ in range(2):
                        hp = 2 * r + hh_
                        g = PERM[hp] // 4
                        half = hp % 2
                        for bi, blk in enumerate(blks):
                            col = (hh_ * 2 + bi) * 128
                            P.op("pe", lambda e, g=g, half=half, blk=blk, col=col, r=r, sb_=sb_: e.matmul(
                                PS[sb_][:, col:col + 128], lhsT=KT[:, g // 2, blk * 128:(blk + 1) * 128],
                                rhs=qT2[:, r, half, :], start=True, stop=True),
                                 R=[B_KT[blk], B_q2[slot]], W=[B_PS[sb_]])
                    Ev = Eatt[:, es, :].rearrange("p (a b n) -> p a b n", a=2, b=2)
                    Pv = PS[sb_][:].rearrange("p (a b n) -> p a b n", a=2, b=2)
                    if nb_ == 2:
                        P.op("act", lambda e, es=es, sb_=sb_: e.activation(out=Eatt[:, es, :], in_=PS[sb_][:], func=AF.Exp,
                                                                           scale=0.125), R=[B_PS[sb_]], W=[B_Eatt[es]])
                        P.op("dve", lambda e, Ev=Ev: e.tensor_tensor(out=Ev[:, :, 0, :], in0=Ev[:, :, 0, :],
                                                                     in1=Lbf.unsqueeze(1).to_broadcast([128, 2, 128]),
                                                                     op=ALU.mult), R=[B_Eatt[es], B_cst], W=[B_Eatt[es]])
                        P.op("dve", lambda e, Ev=Ev: e.tensor_tensor(out=Ev[:, :, 1, :], in0=Ev[:, :, 1, :],
                                                                     in1=Ubf.unsqueeze(1).to_broadcast([128, 2, 128]),
                                                                     op=ALU.mult), R=[B_Eatt[es], B_cst], W=[B_Eatt[es]])
                    else:
                        P.op("act", lambda e, Ev=Ev, Pv=Pv: e.activation(out=Ev[:, :, 0, :], in_=Pv[:, :, 0, :], func=AF.Exp,
                                                                         scale=0.125), R=[B_PS[sb_]], W=[B_Eatt[es]])
                        P.op("dve", lambda e, Ev=Ev: e.tensor_tensor(out=Ev[:, :, 0, :], in0=Ev[:, :, 0, :],
                                                                     in1=Ubf.unsqueeze(1).to_broadcast([128, 2, 128]),
                                                                     op=ALU.mult), R=[B_Eatt[es], B_cst], W=[B_Eatt[es]])
                    yield
                    for hh_ in range(2):
                        hp = 2 * r + hh_
                        g = PERM[hp] // 4
                        ob = hp // 6
                        oc = (hp % 6) * 65
                        for bi, blk in enumerate(blks):
                            P.op("pe", lambda e, hh_=hh_, bi=bi, blk=blk, g=g, ob=ob, oc=oc, Ev=Ev: e.matmul(
                                PS[ob][:, oc:oc + 65], lhsT=Ev[:, hh_, bi, :], rhs=Vst[:, blk, g, :],
                                start=(bi == 0), stop=(bi == nb_ - 1)),
                                 R=[B_Eatt[es], B_V[blk], B_Vones], W=[B_PS[ob]])
                    yield
                den = lsm[:, 3, 0:16]
                rden = lsm[:, 3, 16:32]
                o_ = scrB[:, 0:1024].rearrange("p (h d) -> p h d", h=16)
                for ob in range(3):
                    nh = 6 if ob < 2 else 4
                    pv = PS[ob][:, 0:nh * 65].rearrange("p (h e) -> p h e", e=65)
                    hs = slice(ob * 6, ob * 6 + nh)
                    P.op("dve", lambda e, pv=pv, hs=hs, nh=nh: e.tensor_tensor(
                        out=den[:, hs].unsqueeze(2), in0=pv[:, :, 64:65], in1=lsm[:, 1, hs].unsqueeze(2), op=ALU.add),
                         R=[B_PS[ob], B_lsm], W=[B_den])
                yield
                P.op("dve", lambda e: e.reciprocal(out=rden, in_=den), R=[B_den], W=[B_den])
                yield
                for ob in range(3):
                    nh = 6 if ob < 2 else 4
                    pv = PS[ob][:, 0:nh * 65].rearrange("p (h e) -> p h e", e=65)
                    hs = slice(ob * 6, ob * 6 + nh)
                    P.op("dve", lambda e, pv=pv, hs=hs, nh=nh: e.tensor_tensor(
                        out=o_[:, hs, :], in0=pv[:, :, 0:64], in1=rden[:, hs].unsqueeze(2).to_broadcast([128, nh, 64]),
                        op=ALU.mult), R=[B_PS[ob], B_den], W=[B_B[0], B_B[1]])
                    yield
                og = scrC[:, 1024:2048]
                P.op("dve", lambda e: e.tensor_tensor(out=og, in0=scrB[:, 0:1024], in1=sgb[slot], op=ALU.mult),
                     R=[B_B[0], B_B[1], B_sg[slot]], W=[B_C[1], B_C[2]])
                yield
                for hf in range(2):
                    for i in range(4):
                        t = 4 * hf + i
                        P.op("pe", lambda e, t=t, i=i: e.transpose(out=PT[:, 4 + i, :], in_=og[:, t * 128:(t + 1) * 128],
                                                                   identity=ident), R=[B_C[1], B_C[2], B_cst], W=[B_PT])
                    P.op("act", lambda e, hf=hf: e.activation(out=uTb[:, 4 * hf:4 * hf + 4, :], in_=PT[:, 4:8, :], func=AF.Copy),
                         R=[B_PT], W=[B_uTb])
                    yield
                for nb in range(2):
                    for k in range(8):
                        P.op("pe", lambda e, nb=nb, k=k: e.matmul(PS[5 + nb][:], lhsT=uTb[:, k, :],
                                                                 rhs=W_ao[:, k, nb * 512:(nb + 1) * 512], start=(k == 0),
                                                                 stop=(k == 7)), R=[B_uTb, B_WB], W=[B_PS[5 + nb]])
                    yield
                for nb in range(2):
                    sl = slice(nb * 512, (nb + 1) * 512)
                    P.op("dve", lambda e, nb=nb, sl=sl: e.tensor_tensor(out=hh[:, sl], in0=PS[5 + nb][:], in1=hh[:, sl],
                                                                        op=ALU.add), R=[B_PS[5 + nb], bh], W=[bh])
                    yield
                ple(hh, bh, stt, bst, pT_t[slot], B_pT[slot], PNW, alt=(ubb, B_ubb, uTb, B_uTb, 4),
                    banks=(0, 1, 5, 6), tv=(tgb, vvb, B_tgb, B_vvb))
                store_chunk(L, c)
                yield

            def interleave(*gens):
                gens = list(gens)
                while gens:
                    for gn in list(gens):
                        try:
                            next(gn)
                        except StopIteration:
                            gens.remove(gn)

            load_chunk(L, 0)
            interleave(front(0))
            for c in range(nch):
                if c + 1 < nch:
                    load_chunk(L, c + 1)
                    interleave(front(c + 1), back(c))
                else:
                    interleave(back(c))

        for L in layers:
            if L < 2:
                mamba_layer(L)
            else:
                attn_layer(L)
        finals = [hstores[(layers[-1], c)] for c in range(nch)]
        P.emit(final_waits=finals)
    return nc


def _ktile(w):
    K, N = w.shape
    return np.ascontiguousarray(w.reshape(K // 128, 128, N).transpose(1, 0, 2).reshape(128, (K // 128) * N))


def _rep(v):
    return np.broadcast_to(np.asarray(v, np.float32).reshape(1, -1), (128, v.size))


def prepare_inputs(x, p, positions, ssm_norm_w, ssm_in_w, ssm_conv_w, ssm_conv_b, ssm_dt_bias, ssm_a_log, ssm_d,
                   ssm_gnorm_w, ssm_out_w, kv_norm_w, kv_w, k_norm_w, attn_norm_w, attn_in_w, q_norm_w, attn_sinks,
                   attn_out_w, ple_norm_w, ple_gate_w, ple_proj_w):
    f = np.float32
    x = np.asarray(x, f)
    p = np.asarray(p, f)
    positions = np.asarray(positions, np.int32)
    cst = np.zeros((128, 4, 128), f)
    k = np.arange(128)
    cst[:, 0, :] = np.eye(128)
    cst[:, 1, :] = (k[:, None] <= k[None, :])
    cst[:, 2, :] = (k[:, None] > k[None, :])
    cst[:, 3, :] = 1.0
    invf = (10000.0 ** (-(np.arange(32, dtype=np.float32) * 2.0 / 64))).astype(f)
    invf = np.ascontiguousarray(_rep(invf))
    qperm = np.concatenate([np.arange(h * 64, (h + 1) * 64) for h in PERM])
    w_in = np.stack([_ktile(np.asarray(ssm_in_w[l], f)) for l in range(2)])
    w_out = np.stack([_ktile(np.asarray(ssm_out_w[l], f)) for l in range(2)])
    w_g = np.stack([_ktile(np.asarray(ple_gate_w[i], f)) for i in range(4)])
    w_p = np.stack([_ktile(np.asarray(ple_proj_w[i], f)) for i in range(4)])
    w_kv = _ktile(np.asarray(kv_w, f))
    ai = []
    ao = []
    for j in range(2):
        w = np.asarray(attn_in_w[j], f)
        w = np.concatenate([w[:, :1024][:, qperm], w[:, 1024:][:, qperm]], axis=1)
        ai.append(_ktile(w))
        ao.append(_ktile(np.asarray(attn_out_w[j], f)[qperm, :]))
    w_ai = np.stack(ai)
    w_ao = np.stack(ao)
    lv = np.zeros((4, 128, LVW), f)

    def fm(v):
        v = np.asarray(v, f)
        return v.reshape(-1, 128).T

    for l in range(2):
        lv[l, :, 0:8] = fm(ssm_norm_w[l])
        lv[l, :, 8:16] = fm(ple_norm_w[l])
        lv[l, :, 16:32] = fm(ssm_gnorm_w[l])
        cw = np.asarray(ssm_conv_w[l], f)
        lv[l, :, 32:128] = cw.reshape(4, 24, 128).transpose(2, 1, 0).reshape(128, 96)
        lv[l, :, 128:152] = np.asarray(ssm_conv_b[l], f).reshape(24, 128).T
        lv[l, :, 152:184] = _rep(np.asarray(ssm_dt_bias[l], f))
        lv[l, :, 184:216] = _rep(np.asarray(ssm_a_log[l], f))
        lv[l, :, 216:248] = _rep(np.asarray(ssm_d[l], f))
    for j in range(2):
        L = 2 + j
        lv[L, :, 0:8] = fm(attn_norm_w[j])
        lv[L, :, 8:16] = fm(ple_norm_w[L])
        lv[L, :, 16:24] = fm(kv_norm_w)
        lv[L, :, 32:96] = _rep(np.asarray(k_norm_w, f))
        lv[L, :, 96:160] = _rep(np.asarray(q_norm_w[j], f))
        lv[L, :, 160:176] = _rep(np.asarray(attn_sinks[j], f)[PERM])
    shared = dict(cst=cst, invf=invf, w_in=w_in, w_out=w_out, w_g=w_g, w_p=w_p, w_kv=w_kv, w_ai=w_ai, w_ao=w_ao, lv=lv)
    in_maps = []
    for b in range(x.shape[0]):
        m = dict(shared)
        m["x"] = np.ascontiguousarray(x[b])
        m["pT"] = np.ascontiguousarray(p[:, b].transpose(0, 2, 1))
        m["pos"] = np.ascontiguousarray(positions[b].reshape(32, 128).T)
        in_maps.append(m)
    return in_maps


_NC_CACHE = {}
LAUNCHES = [(0, 1, 2, 3)]


def kernel(**inputs):
    in_maps = prepare_inputs(**inputs)
    n = len(in_maps)
    outs = None
    for grp in LAUNCHES:
        if grp not in _NC_CACHE:
            _NC_CACHE[grp] = build(layers=grp)
        nc = _NC_CACHE[grp]
        if outs is not None:
            for b in range(n):
                in_maps[b]["x"] = outs[b]
        res = run_bass_kernel_spmd(nc, in_maps, core_ids=list(range(n)))
        outs = [np.ascontiguousarray(np.asarray(r["out"], np.float32)) for r in res.results]
    return np.stack(outs, axis=0)
```

```python
import contextlib
import numpy as np
import concourse.bass as bass
import concourse.mybir as mybir
from concourse.bass_utils import run_bass_kernel_spmd

F32 = mybir.dt.float32
BF16 = mybir.dt.bfloat16
I32 = mybir.dt.int32
AF = mybir.ActivationFunctionType
ALU = mybir.AluOpType
AX = mybir.AxisListType

EPOCH = 512
SCHEDULE = True
QUANT = 0.25
D = 1024
S = 4096
NCH = 32
EPS = 1e-6
SSM_IN = 5152
LVW = 256
PERM = [0, 4, 1, 5, 2, 6, 3, 7, 8, 12, 9, 13, 10, 14, 11, 15]


class Buf:
    __slots__ = ("name", "w", "wd", "rs", "rd", "psum")

    def __init__(self, name, psum=False):
        self.name = name
        self.w = {}
        self.wd = []
        self.rs = {}
        self.rd = []
        self.psum = psum


class Ins:
    __slots__ = ("eng", "fn", "deps", "sig", "dma", "key", "val", "sem", "n", "idx", "odeps", "dur", "end")

    def __init__(self, eng, fn, dma=False, key=None):
        self.eng = eng
        self.fn = fn
        self.deps = []
        self.sig = False
        self.dma = dma
        self.key = key
        self.val = None
        self.sem = None
        self.odeps = []
        self.dur = 0.3
        self.end = 0.0


def _est_dur(eng, calls):
    name, a, k = calls[0]
    out = k.get("out", a[0] if a else None)
    try:
        shp = out.shape
        n = 1
        for d_ in shp[1:]:
            n *= d_
    except Exception:
        n = 256
    if eng == "pe":
        f32 = False
        try:
            f32 = (k.get("lhsT").dtype == F32)
        except Exception:
            pass
        return 0.07 + n / 2000.0 * (4.0 if f32 else 1.0)
    if eng == "act":
        return 0.25 + n / 1200.0
    if eng == "dve":
        return 0.12 + n / 1100.0
    if eng == "pool":
        return 0.2 + n / 450.0
    return 0.3


class _Rec:
    def __init__(self):
        self.calls = []

    def __getattr__(self, name):
        def f(*a, **k):
            self.calls.append((name, a, k))
            return None
        return f


def _replay(calls, eobj):
    return [getattr(eobj, name)(*a, **k) for name, a, k in calls]


class Prog:
    ENGS = ("pe", "act", "dve", "pool", "sp")

    def __init__(self, nc):
        self.nc = nc
        self.streams = {e: [] for e in self.ENGS}
        self.dma_cnt = {}
        self.nins = 0
        self._last_dma = {}

    def _track(self, ins, R, W):
        deps = ins.deps
        eng = ins.eng
        dma = ins.dma
        for b in R:
            deps.extend(b.w.values())
            deps.extend(b.wd)
            if b.psum:
                for e, xs in b.rs.items():
                    if e != eng:
                        deps.extend(xs)
        for b in W:
            for e, x in b.w.items():
                if dma or e != eng or eng != "pe":
                    deps.append(x)
                else:
                    ins.odeps.append(x)
            deps.extend(b.wd)
            for e, xs in b.rs.items():
                if dma or e != eng or eng != "pe":
                    deps.extend(xs)
            deps.extend(b.rd)
        for b in R:
            if dma:
                b.rd.append(ins)
            else:
                b.rs.setdefault(eng, []).append(ins)
        for b in W:
            b.rs = {}
            b.rd = []
            if dma:
                b.w = {}
                b.wd = [ins]
            else:
                b.w = {eng: ins}
                b.wd = []

    def op(self, eng, fn, R=(), W=()):
        rec = _Rec()
        fn(rec)
        assert len(rec.calls) == 1
        ins = Ins(eng, rec.calls)
        ins.dur = _est_dur(eng, rec.calls)
        ins.idx = self.nins
        self._track(ins, R, W)
        self.streams[eng].append(ins)
        self.nins += 1
        return ins

    def dma(self, eng, fn, key, n, R=(), W=()):
        rec = _Rec()
        fn(rec)
        assert len(rec.calls) == n, (len(rec.calls), n)
        ins = Ins(eng, rec.calls, dma=True, key=key)
        ins.n = n
        ins.idx = self.nins
        ins.dur = 2.5 * n
        self._track(ins, R, W)
        if self._last_dma.get(eng) is not None:
            ins.odeps.append(self._last_dma[eng])
        self._last_dma[eng] = ins
        c = self.dma_cnt.get(key, 0) + n
        self.dma_cnt[key] = c
        ins.val = 16 * c
        self.streams[eng].append(ins)
        self.nins += 1
        return ins

    def schedule(self):
        import heapq
        HOP = 0.35
        allins = [i for e in self.ENGS for i in self.streams[e]]
        succ = {}
        nrem = {}
        for i in allins:
            ds = set(id(d) for d in i.deps) | set(id(d) for d in i.odeps)
            nrem[id(i)] = len(ds)
            seen = set()
            for d in list(i.deps) + list(i.odeps):
                if id(d) in seen:
                    continue
                seen.add(id(d))
                succ.setdefault(id(d), []).append(i)
        bl = {}
        for i in sorted(allins, key=lambda x: -x.idx):
            m_ = 0.0
            for sx in succ.get(id(i), ()):
                v_ = bl[id(sx)] + (HOP if sx.eng != i.eng else 0.1)
                if v_ > m_:
                    m_ = v_
            bl[id(i)] = m_ + i.dur
        Q = QUANT

        def key(t_, i):
            return (int(t_ / Q), -bl[id(i)], i.idx)
        free = {e: 0.0 for e in self.ENGS}
        ready_t = {}
        heap = []
        for i in allins:
            if nrem[id(i)] == 0:
                ready_t[id(i)] = 0.0
                heapq.heappush(heap, (key(0.0, i), i.idx, i))
        order = {e: [] for e in self.ENGS}
        done = 0
        while heap:
            st, _, i = heapq.heappop(heap)
            real = max(free[i.eng], ready_t[id(i)])
            if key(real, i) > st:
                heapq.heappush(heap, (key(real, i), i.idx, i))
                continue
            if i.dma:
                free[i.eng] = real + 0.1
            else:
                free[i.eng] = real + i.dur
            i.end = real + i.dur
            order[i.eng].append(i)
            done += 1
            for sx in succ.get(id(i), ()):
                r = max(ready_t.get(id(sx), 0.0), i.end + (HOP if sx.eng != i.eng or i.dma else 0.1))
                ready_t[id(sx)] = r
                nrem[id(sx)] -= 1
                if nrem[id(sx)] == 0:
                    heapq.heappush(heap, (key(max(r, free[sx.eng]), sx), sx.idx, sx))
        assert done == len(allins), (done, len(allins))
        self.streams = order
        print(f"[prog] scheduled: modelled makespan {max(free.values()):.1f} us", flush=True)

    def emit(self, final_waits=()):
        nc = self.nc
        if SCHEDULE:
            self.schedule()
        pos = {}
        for e in self.ENGS:
            for k_, ins in enumerate(self.streams[e]):
                pos[id(ins)] = k_
        for e in self.ENGS:
            for ins in self.streams[e]:
                best = {}
                nd = []
                for d in ins.deps:
                    if d.dma:
                        nd.append(d)
                    else:
                        b_ = best.get(d.eng)
                        if b_ is None or pos[id(d)] > pos[id(b_)]:
                            best[d.eng] = d
                ins.deps = nd + list(best.values())
        for e in self.ENGS:
            for ins in self.streams[e]:
                for d in ins.deps:
                    if not d.dma:
                        d.sig = True
        nsig = {}
        for e in self.ENGS:
            s = 0
            for ins in self.streams[e]:
                if ins.sig and not ins.dma:
                    ins.sem = (e, s // EPOCH)
                    ins.val = s % EPOCH + 1
                    s += 1
            nsig[e] = s
        sem_names = []
        for e in self.ENGS:
            for k in range((nsig[e] + EPOCH - 1) // EPOCH):
                sem_names.append((e, k))
        for key in self.dma_cnt:
            sem_names.append(("dma", key))
        print(f"[prog] instructions={self.nins} sems={len(sem_names)} sig={nsig}", flush=True)
        with contextlib.ExitStack() as st:
            sems = {}
            for nm in sem_names:
                sems[nm] = st.enter_context(nc.semaphore(f"s_{nm[0]}_{nm[1]}"))
            block = st.enter_context(nc.Block())

            def run_stream(ename, eobj):
                waited = {}
                for ins in self.streams[ename]:
                    need = {}
                    for d in ins.deps:
                        k = ("dma", d.key) if d.dma else d.sem
                        v = d.val
                        if waited.get(k, 0) >= v:
                            continue
                        if need.get(k, 0) < v:
                            need[k] = v
                    for k, v in need.items():
                        eobj.wait_ge(sems[k], v)
                        waited[k] = v
                    if ins.dma:
                        for r in _replay(ins.fn, eobj):
                            r.then_inc(sems[("dma", ins.key)], 16)
                    else:
                        r = _replay(ins.fn, eobj)[0]
                        if ins.sig:
                            r.then_inc(sems[ins.sem], 1)
                if ename == "sp":
                    for d in final_waits:
                        eobj.wait_ge(sems[("dma", d.key)], d.val)

            @block.tensor
            def _(pe):
                run_stream("pe", pe)

            @block.scalar
            def _(act):
                run_stream("act", act)

            @block.vector
            def _(dve):
                run_stream("dve", dve)

            @block.gpsimd
            def _(pool):
                run_stream("pool", pool)

            @block.sync
            def _(sp):
                run_stream("sp", sp)


def build(layers=(0, 1, 2, 3), nch=NCH):
    nc = bass.Bass("TRN2", target_bir_lowering=False)

    def dram(name, shape, dt, kind="ExternalInput"):
        return nc.dram_tensor(name, shape, dt, kind=kind).ap()

    x_d = dram("x", [S, D], F32)
    pT_d = dram("pT", [4, 256, S], F32)
    pos_d = dram("pos", [128, 32], I32)
    cst_d = dram("cst", [128, 4, 128], F32)
    invf_d = dram("invf", [128, 32], F32)
    w_in_d = dram("w_in", [2, 128, 8 * SSM_IN], F32)
    w_out_d = dram("w_out", [2, 128, 16 * 1024], F32)
    w_g_d = dram("w_g", [4, 128, 8 * 1024], F32)
    w_p_d = dram("w_p", [4, 128, 2 * 1024], F32)
    w_kv_d = dram("w_kv", [128, 8 * 512], F32)
    w_ai_d = dram("w_ai", [2, 128, 8 * 2048], F32)
    w_ao_d = dram("w_ao", [2, 128, 8 * 1024], F32)
    lv_d = dram("lv", [4, 128, LVW], F32)
    out_d = dram("out", [S, D], F32, kind="ExternalOutput")

    st = contextlib.ExitStack()
    with st:
        def sb(name, shape, dt):
            return st.enter_context(nc.sbuf_tensor(name, shape, dt))

        def ps(name, shape, dt):
            return st.enter_context(nc.psum_tensor(name, shape, dt))

        P = Prog(nc)
        WB = sb("WB", [128, 67840], BF16)
        B_WB = Buf("WB")
        LV = sb("LV", [128, LVW], F32)
        B_LV = Buf("LV")
        cbf = sb("cbf", [128, 4, 128], BF16)
        c32 = sb("c32", [128, 4, 128], F32)
        B_cst = Buf("cst")
        ident = cbf[:, 0, :]
        Ubf = cbf[:, 1, :]
        Lbf = cbf[:, 2, :]
        U32 = c32[:, 1, :]
        L32 = c32[:, 2, :]
        ones32 = c32[:, 3, :]
        h_t = [sb(f"h{i}", [128, D], F32) for i in range(2)]
        B_h = [Buf(f"h{i}") for i in range(2)]
        pT_t = [sb(f"pTt{i}", [128, 2, 128], BF16) for i in range(2)]
        B_pT = [Buf(f"pTt{i}") for i in range(2)]
        stat = [sb(f"stat{i}", [128, 64], F32) for i in range(2)]
        B_stat = [Buf(f"stat{i}") for i in range(2)]
        u_bf = sb("u_bf", [128, D], BF16)
        B_u = Buf("u_bf")
        uT = sb("uT", [128, 8, 128], BF16)
        B_uT = Buf("uT")
        sm = sb("sm", [128, 16, 32], F32)
        B_sm = Buf("sm")
        lsm = sb("lsm", [128, 4, 32], F32)
        B_lsm = Buf("lsm")
        xraw = sb("xraw", [128, 6, 131], F32)
        B_xraw = Buf("xraw")
        B_xr = [Buf(f"xr{t}") for t in range(6)]
        B_ca = [Buf(f"ca{t}") for t in range(6)]
        ctail = sb("ctail", [128, 24, 3], F32)
        B_ctail = [Buf(f"ctail{g}") for g in range(4)]
        cacc = sb("cacc", [128, 6, 128], F32)
        B_cacc = Buf("cacc")
        actT = sb("actT", [128, 6, 128], BF16)
        B_actT = Buf("actT")
        xdt = sb("xdt", [128, 512], BF16)
        B_xdt = Buf("xdt")
        xsD = sb("xsD", [128, 512], BF16)
        B_xsD = Buf("xsD")
        xw = sb("xw", [128, 512], BF16)
        B_xw = Buf("xw")
        Btm = sb("Btm", [128, 128], BF16)
        B_Btm = Buf("Btm")
        cbm = sb("cbm", [128, 128], F32)
        B_cbm = Buf("cbm")
        scrA = sb("scrA", [128, 2048], F32)
        B_A = [Buf("scrA0"), Buf("scrA1")]
        scrB = sb("scrB", [128, 2048], F32)
        B_B = [Buf(f"scrB{i}") for i in range(4)]
        scrC = sb("scrC", [128, 2048], BF16)
        B_C = [Buf("scrC0"), Buf("scrC1"), Buf("scrC2")]
        scrD = sb("scrD", [128, 2048], BF16)
        B_D = [Buf(f"scrD{i}") for i in range(4)]
        St = sb("St", [128, 2048], F32)
        B_St = [Buf(f"St{i}") for i in range(4)]
        _cb16 = cacc[:].rearrange("p t n -> p (t n)").bitcast(BF16)
        Eatt = _cb16[:, 0:1024].rearrange("p (a n) -> p a n", a=2)
        B_Eatt = [Buf("Eatt0"), Buf("Eatt1")]
        krbuf = _cb16[:, 1024:1280]
        B_kr = Buf("kr")
        kf = xraw[:].rearrange("p t n -> p (t n)")[:, 0:768].rearrange("p (a b) -> p a b", a=3)
        B_kf = B_xraw
        actTb = sb("actTb", [128, 6, 128], BF16)
        xdtb = sb("xdtb", [128, 512], BF16)
        xsDb = sb("xsDb", [128, 512], BF16)
        xwb = sb("xwb", [128, 512], BF16)
        Btmb = sb("Btmb", [128, 128], BF16)
        actT2, xdt2, xsD2, xw2, Btm2 = [actT, actTb], [xdt, xdtb], [xsD, xsDb], [xw, xwb], [Btm, Btmb]
        B_actT2 = [B_actT, Buf("actTb")]
        B_xdt2 = [B_xdt, Buf("xdtb")]
        B_xsD2 = [B_xsD, Buf("xsDb")]
        B_xw2 = [B_xw, Buf("xwb")]
        B_Btm2 = [B_Btm, Buf("Btmb")]
        PT = ps("PT", [128, 8, 128], BF16)
        B_PT = Buf("PT", psum=True)
        PS = [ps(f"PS{i}", [128, 512], F32) for i in range(7)]
        B_PS = [Buf(f"PS{i}", psum=True) for i in range(7)]

        def wv(off, k, n):
            return WB[:, off:off + k * n].rearrange("p (k n) -> p k n", k=k)
        W_in = wv(0, 8, SSM_IN)
        W_out = wv(41216, 16, 1024)
        W_g = wv(57600, 8, 1024)
        W_p = wv(65792, 2, 1024)
        W_ai = wv(0, 8, 2048)
        W_ao = wv(16384, 8, 1024)
        W_kv = wv(24576, 8, 512)
        KT = wv(28672, 2, 4096)
        Vst = WB[:, 36864:36864 + 8320].rearrange("p (c g e) -> p c g e", c=32, g=4)
        B_KT = [Buf(f"KT{c}") for c in range(NCH)]
        B_V = [Buf(f"V{c}") for c in range(NCH)]
        B_Vones = Buf("Vones")
        cosT = St[:, 0:1024].rearrange("p (c f) -> p c f", c=32)
        sinT = St[:, 1024:2048].rearrange("p (c f) -> p c f", c=32)

        P.dma("sp", lambda e: [e.dma_start(out=c32[:], in_=cst_d)], "cst", 1, W=[B_cst])
        P.dma("pool", lambda e: [e.dma_start(out=cbf[:], in_=cst_d)], "cstb", 1, W=[B_cst])

        hstores = {}

        def load_weights(pairs, L):
            fns = []
            for dst, src, ns in pairs:
                n = dst.shape[1]
                step = n // ns
                for i in range(ns):
                    fns.append((dst[:, i * step:(i + 1) * step], src[:, i * step:(i + 1) * step]))
            P.dma("pool", lambda e, fns=fns: [e.dma_start(out=d_, in_=s_) for d_, s_ in fns], f"wload{L}", len(fns),
                  W=[B_WB])

        def rstd_from_ss(stt, col, n, bst):
            P.op("act", lambda e: e.activation(out=stt[:, col + 1:col + 2], in_=stt[:, col:col + 1], func=AF.Ln,
                                               scale=1.0 / n, bias=lsm[:, 2, 0:1]), R=[bst, B_lsm], W=[bst])
            P.op("act", lambda e: e.activation(out=stt[:, col + 2:col + 3], in_=stt[:, col + 1:col + 2], func=AF.Exp,
                                               scale=-0.5), R=[bst], W=[bst])

        def norm_T(hh, bh, stt, bst, col, nwoff, have_rstd=False, alt=None, extra=None):
            ub, Bub, uTx, BuTx, pt0 = alt if alt is not None else (u_bf[:], B_u, uT[:], B_uT, None)
            if not have_rstd:
                P.op("act", lambda e: e.activation(out=ub, in_=hh[:], func=AF.Square,
                                                   accum_out=stt[:, col:col + 1]), R=[bh, bst], W=[Bub, bst])
                rstd_from_ss(stt, col, D, bst)
            P.op("act", lambda e: e.activation(out=ub, in_=hh[:], func=AF.Copy, scale=stt[:, col + 2:col + 3]),
                 R=[bh, bst], W=[Bub])
            if pt0 is None:
                for k in range(8):
                    P.op("pe", lambda e, k=k: e.transpose(out=PT[:, k, :], in_=ub[:, k * 128:(k + 1) * 128],
                                                          identity=ident), R=[Bub, B_cst], W=[B_PT])
                P.op("dve", lambda e: e.tensor_tensor(out=uTx, in0=PT[:],
                                                      in1=LV[:, nwoff:nwoff + 8].unsqueeze(2).to_broadcast([128, 8, 128]),
                                                      op=ALU.mult), R=[B_PT, B_LV], W=[BuTx])
            else:
                for hf in range(2):
                    for i in range(4):
                        k = 4 * hf + i
                        P.op("pe", lambda e, k=k, i=i: e.transpose(out=PT[:, pt0 + i, :], in_=ub[:, k * 128:(k + 1) * 128],
                                                                   identity=ident), R=[Bub, B_cst], W=[B_PT])
                    P.op("dve", lambda e, hf=hf: e.tensor_tensor(
                        out=uTx[:, 4 * hf:4 * hf + 4, :], in0=PT[:, pt0:pt0 + 4, :],
                        in1=LV[:, nwoff + 4 * hf:nwoff + 4 * hf + 4].unsqueeze(2).to_broadcast([128, 4, 128]),
                        op=ALU.mult), R=[B_PT, B_LV], W=[BuTx])
                    if extra is not None:
                        uT2, BuT2, nwoff2 = extra
                        P.op("dve", lambda e, hf=hf: e.tensor_tensor(
                            out=uT2[:, 4 * hf:4 * hf + 4, :], in0=PT[:, pt0:pt0 + 4, :],
                            in1=LV[:, nwoff2 + 4 * hf:nwoff2 + 4 * hf + 4].unsqueeze(2).to_broadcast([128, 4, 128]),
                            op=ALU.mult), R=[B_PT, B_LV], W=[BuT2])

        def ple(hh, bh, stt, bst, pt, bpt, pnwoff, alt=None, banks=(0, 1, 2, 3), tv=None):
            norm_T(hh, bh, stt, bst, 3, pnwoff, alt=alt)
            uTx, BuTx = (alt[2], alt[3]) if alt is not None else (uT[:], B_uT)
            if tv is None:
                tg, vv, Btg, Bvv = scrA[:, 0:1024], scrA[:, 1024:2048], B_A[0], B_A[1]
            else:
                tg, vv, Btg, Bvv = tv
            for nb in range(2):
                for k in range(8):
                    P.op("pe", lambda e, nb=nb, k=k: e.matmul(PS[banks[nb]][:], lhsT=uTx[:, k, :],
                                                             rhs=W_g[:, k, nb * 512:(nb + 1) * 512],
                                                             start=(k == 0), stop=(k == 7)),
                         R=[BuTx, B_WB], W=[B_PS[banks[nb]]])
            for nb in range(2):
                for k in range(2):
                    P.op("pe", lambda e, nb=nb, k=k: e.matmul(PS[banks[2 + nb]][:], lhsT=pt[:, k, :],
                                                             rhs=W_p[:, k, nb * 512:(nb + 1) * 512],
                                                             start=(k == 0), stop=(k == 1)),
                         R=[bpt, B_WB], W=[B_PS[banks[2 + nb]]])
            for nb in range(2):
                sl = slice(nb * 512, (nb + 1) * 512)
                P.op("act", lambda e, nb=nb, sl=sl: e.activation(out=tg[:, sl], in_=PS[banks[nb]][:], func=AF.Tanh, scale=0.5),
                     R=[B_PS[banks[nb]]], W=[Btg])
                P.op("dve", lambda e, nb=nb, sl=sl: e.scalar_tensor_tensor(out=vv[:, sl], in0=tg[:, sl], scalar=1.0,
                                                                           in1=PS[banks[2 + nb]][:], op0=ALU.add, op1=ALU.mult),
                     R=[Btg, B_PS[banks[2 + nb]]], W=[Bvv])
            P.op("dve", lambda e: e.scalar_tensor_tensor(out=hh[:], in0=vv, scalar=0.5, in1=hh[:], op0=ALU.mult,
                                                         op1=ALU.add), R=[Bvv, bh], W=[bh])

        def load_chunk(L, c):
            slot = c % 2
            src = x_d if L == layers[0] else out_d
            deps = [] if L == layers[0] else [hstores[(L - 1, c)]]
            i1 = P.dma("sp", lambda e: [e.dma_start(out=h_t[slot][:], in_=src[c * 128:(c + 1) * 128, :])],
                       f"hld{slot}_{L}", 1, W=[B_h[slot]])
            i1.deps.extend(deps)
            P.dma("pool", lambda e: [e.dma_start(out=pT_t[slot][:],
                                                 in_=pT_d[L, :, c * 128:(c + 1) * 128].rearrange("(k p) t -> p k t", p=128))],
                  f"pld{slot}_{L}", 1, W=[B_pT[slot]])

        def store_chunk(L, c):
            slot = c % 2
            hstores[(L, c)] = P.dma("sp", lambda e: [e.dma_start(out=out_d[c * 128:(c + 1) * 128, :], in_=h_t[slot][:])],
                                    f"hst{slot}_{L}", 1, R=[B_h[slot]])

        def layer_common_prep(L):
            P.dma("sp", lambda e: [e.dma_start(out=LV[:], in_=lv_d[L])], f"lv{L}", 1, W=[B_LV])
            P.op("dve", lambda e: e.memset(lsm[:, 2, :], EPS), W=[B_lsm])

        def mamba_layer(L):
            layer_common_prep(L)
            load_weights([(WB[:, 0:41216], w_in_d[L], 8), (WB[:, 41216:57600], w_out_d[L], 4),
                          (WB[:, 57600:65792], w_g_d[L], 2), (WB[:, 65792:67840], w_p_d[L], 1)], L)
            NW, PNW, GW, CW, CB, DTB, ALOG, DSK = 0, 8, 16, 32, 128, 152, 184, 216
            cw = LV[:, CW:CW + 96].rearrange("p (t k) -> p t k", k=4)
            P.op("act", lambda e: e.activation(out=lsm[:, 0, :], in_=LV[:, ALOG:ALOG + 32], func=AF.Exp),
                 R=[B_LV], W=[B_lsm])
            P.op("dve", lambda e: e.tensor_scalar(out=lsm[:, 0, :], in0=lsm[:, 0, :], scalar1=-1.0, scalar2=None,
                                                  op0=ALU.mult), R=[B_lsm], W=[B_lsm])
            P.op("pool", lambda e: e.memset(St[:], 0.0), W=B_St)
            P.op("pool", lambda e: e.memset(scrD[:], 0.0), W=B_D)
            P.op("pool", lambda e: e.memset(ctail[:], 0.0), W=B_ctail)
            load_chunk(L, 0)
            for c in range(nch):
                if c + 1 < nch:
                    load_chunk(L, c + 1)
                slot = c % 2
                hh, bh, stt, bst = h_t[slot], B_h[slot], stat[slot], B_stat[slot]
                P.op("pool", lambda e, stt=stt: e.memset(stt[:], 0.0), W=[bst])
                norm_T(hh, bh, stt, bst, 0, NW)
                SMB = B_PS[1]
                for k in range(8):
                    P.op("pe", lambda e, k=k: e.matmul(PS[1][:, 0:32], lhsT=uT[:, k, :], rhs=W_in[:, k, 5120:5152],
                                                       start=(k == 0), stop=(k == 7)), R=[B_uT, B_WB], W=[SMB])
                dtr, dt_, dta, acs, tmp, dte, ea, cd, ee = (sm[:, i, :] for i in range(9))
                P.op("dve", lambda e: e.tensor_tensor(out=dtr, in0=PS[1][:, 0:32], in1=LV[:, DTB:DTB + 32], op=ALU.add),
                     R=[SMB, B_LV], W=[B_sm])
                P.op("act", lambda e: e.activation(out=ee, in_=dtr, func=AF.Exp), R=[B_sm], W=[B_sm])
                P.op("act", lambda e: e.activation(out=dt_, in_=ee, func=AF.Ln, bias=1.0), R=[B_sm], W=[B_sm])
                P.op("dve", lambda e: e.tensor_tensor(out=dta, in0=dt_, in1=lsm[:, 0, :], op=ALU.mult),
                     R=[B_sm, B_lsm], W=[B_sm])
                P.op("pe", lambda e: e.matmul(PS[1][:, 32:64], lhsT=U32, rhs=dta, start=True, stop=True),
                     R=[B_sm, B_cst], W=[SMB])
                P.op("pe", lambda e: e.matmul(PS[1][:, 64:96], lhsT=ones32, rhs=dta, start=True, stop=True),
                     R=[B_sm, B_cst], W=[SMB])
                P.op("dve", lambda e: e.tensor_copy(out=acs, in_=PS[1][:, 32:64]), R=[SMB], W=[B_sm])
                P.op("dve", lambda e: e.tensor_tensor(out=tmp, in0=PS[1][:, 64:96], in1=acs, op=ALU.subtract),
                     R=[SMB, B_sm], W=[B_sm])
                P.op("act", lambda e: e.activation(out=dte, in_=tmp, func=AF.Exp), R=[B_sm], W=[B_sm])
                P.op("act", lambda e: e.activation(out=ea, in_=acs, func=AF.Exp), R=[B_sm], W=[B_sm])
                P.op("act", lambda e: e.activation(out=cd, in_=PS[1][:, 64:96], func=AF.Exp), R=[SMB], W=[B_sm])
                def stageA(g):
                    q = g % 2
                    aT, xd, xs_, xw_, bt = actT2[q], xdt2[q], xsD2[q], xw2[q], Btm2[q]
                    BaT, Bxd, Bxs, Bxw, Bbt = B_actT2[q], B_xdt2[q], B_xsD2[q], B_xw2[q], B_Btm2[q]
                    cols = [2048 + 512 * g + 128 * t for t in range(4)] + [4096 + 128 * g, 4608 + 128 * g]
                    tiles = [4 * g + t for t in range(4)] + [16 + g, 20 + g]
                    for t in range(6):
                        bank, pos = (0, t) if t < 4 else (1, t - 4)
                        for k in range(8):
                            P.op("pe", lambda e, t=t, k=k, bank=bank, pos=pos: e.matmul(
                                PS[bank][:, pos * 128:(pos + 1) * 128], lhsT=W_in[:, k, cols[t]:cols[t] + 128],
                                rhs=uT[:, k, :], start=(k == 0), stop=(k == 7)), R=[B_uT, B_WB], W=[B_PS[bank]])
                        P.op("pool", lambda e, t=t: e.tensor_copy(out=xraw[:, t, 0:3], in_=ctail[:, tiles[t], :]),
                             R=[B_ctail[g]], W=[B_xr[t]])
                        yield
                    P.op("act", lambda e: e.activation(out=xraw[:, 0:4, 3:131],
                                                       in_=PS[0][:].rearrange("p (t n) -> p t n", t=4), func=AF.Copy),
                         R=[B_PS[0]], W=B_xr[0:4])
                    P.op("act", lambda e: e.activation(out=xraw[:, 4:6, 3:131],
                                                       in_=PS[1][:, 0:256].rearrange("p (t n) -> p t n", t=2), func=AF.Copy),
                         R=[B_PS[1]], W=B_xr[4:6])
                    yield
                    for k in range(8):
                        P.op("pe", lambda e, k=k: e.matmul(PS[0][:], lhsT=uT[:, k, :], rhs=W_in[:, k, 512 * g:512 * (g + 1)],
                                                           start=(k == 0), stop=(k == 7)), R=[B_uT, B_WB], W=[B_PS[0]])
                    yield
                    for t in range(6):
                        ti = tiles[t]
                        P.op("pool", lambda e, t=t: e.tensor_copy(out=ctail[:, tiles[t], :], in_=xraw[:, t, 128:131]),
                             R=[B_xr[t]], W=[B_ctail[g]])
                        P.op("act", lambda e, t=t, ti=ti: e.activation(out=cacc[:, t, :], in_=xraw[:, t, 3:131],
                                                                       func=AF.Identity, scale=cw[:, ti, 3:4],
                                                                       bias=LV[:, CB + ti:CB + ti + 1]),
                             R=[B_xr[t], B_LV], W=[B_ca[t]])
                        for k in range(3):
                            P.op("dve", lambda e, t=t, ti=ti, k=k: e.scalar_tensor_tensor(
                                out=cacc[:, t, :], in0=xraw[:, t, k:k + 128], scalar=cw[:, ti, k:k + 1],
                                in1=cacc[:, t, :], op0=ALU.mult, op1=ALU.add), R=[B_xr[t], B_LV, B_ca[t]], W=[B_ca[t]])
                        yield
                    P.op("act", lambda e: e.activation(out=scrB[:, 1024 + 512 * q:1536 + 512 * q], in_=PS[0][:], func=AF.Silu),
                         R=[B_PS[0]], W=[B_B[2 + q]])
                    P.op("act", lambda e: e.activation(out=aT[:], in_=cacc[:], func=AF.Silu), R=B_ca, W=[BaT])
                    yield
                    for t in range(5):
                        P.op("pe", lambda e, t=t: e.transpose(out=PT[:, t, :], in_=aT[:, t, :], identity=ident),
                             R=[BaT, B_cst], W=[B_PT])
                    yield
                    PTx = PT[:, 0:4, :].rearrange("p t (a d) -> p (t a) d", a=2)
                    P.op("dve", lambda e: e.tensor_tensor(out=xd[:].rearrange("p (h d) -> p h d", h=8), in0=PTx,
                                                          in1=dt_[:, 8 * g:8 * g + 8].unsqueeze(2).to_broadcast([128, 8, 64]),
                                                          op=ALU.mult), R=[B_PT, B_sm], W=[Bxd])
                    yield
                    P.op("dve", lambda e: e.tensor_tensor(out=xs_[:].rearrange("p (h d) -> p h d", h=8), in0=PTx,
                                                          in1=LV[:, DSK + 8 * g:DSK + 8 * g + 8].unsqueeze(2).to_broadcast([128, 8, 64]),
                                                          op=ALU.mult), R=[B_PT, B_LV], W=[Bxs])
                    P.op("act", lambda e: e.activation(out=bt[:], in_=PT[:, 4, :], func=AF.Copy), R=[B_PT], W=[Bbt])
                    yield
                    P.op("pool", lambda e: e.tensor_tensor(out=xw_[:].rearrange("p (h d) -> p h d", h=8),
                                                          in0=xd[:].rearrange("p (h d) -> p h d", h=8),
                                                          in1=dte[:, 8 * g:8 * g + 8].unsqueeze(2).to_broadcast([128, 8, 64]),
                                                          op=ALU.mult), R=[Bxd, B_sm], W=[Bxw])
                    yield

                def stageB(g):
                    q = g % 2
                    aT, xd, xs_, xw_, bt = actT2[q], xdt2[q], xsD2[q], xw2[q], Btm2[q]
                    BaT, Bxd, Bxs, Bxw, Bbt = B_actT2[q], B_xdt2[q], B_xsD2[q], B_xw2[q], B_Btm2[q]
                    szz = scrB[:, 1024 + 512 * q:1536 + 512 * q]
                    Bsz = B_B[2 + q]
                    P.op("pe", lambda e: e.matmul(PS[4][:, 0:128], lhsT=aT[:, 4, :], rhs=aT[:, 5, :], start=True,
                                                  stop=True), R=[BaT], W=[B_PS[4]])
                    rseg = scrA[:, 0:1024].rearrange("p (h i) -> p h i", h=8)
                    Ex = scrA[:, 1024:2048].rearrange("p (h i) -> p h i", h=8)
                    P.op("pool", lambda e: e.tensor_tensor(out=rseg, in0=U32.unsqueeze(1).to_broadcast([128, 8, 128]),
                                                          in1=dta[:, 8 * g:8 * g + 8].unsqueeze(2).to_broadcast([128, 8, 128]),
                                                          op=ALU.mult), R=[B_cst, B_sm], W=[B_A[0]])
                    yield
                    P.op("dve", lambda e: e.tensor_tensor(out=cbm[:], in0=PS[4][:, 0:128], in1=U32, op=ALU.mult),
                         R=[B_PS[4], B_cst], W=[B_cbm])
                    for hb in range(2):
                        P.op("pe", lambda e, hb=hb: e.matmul(PS[2 + hb][:], lhsT=L32,
                                                             rhs=scrA[:, hb * 512:(hb + 1) * 512], start=True, stop=True),
                             R=[B_A[0], B_cst], W=[B_PS[2 + hb]])
                    yield
                    for hb in range(2):
                        P.op("act", lambda e, hb=hb: e.activation(out=scrA[:, 1024 + hb * 512:1024 + (hb + 1) * 512],
                                                                  in_=PS[2 + hb][:], func=AF.Exp),
                             R=[B_PS[2 + hb]], W=[B_A[1]])
                        yield
                    MT = scrC[:, 0:1024].rearrange("p (h i) -> p h i", h=8)
                    P.op("dve", lambda e: e.tensor_tensor(out=MT, in0=Ex, in1=cbm[:].unsqueeze(1).to_broadcast([128, 8, 128]),
                                                          op=ALU.mult), R=[B_A[1], B_cbm], W=[B_C[0]])
                    P.op("pe", lambda e: e.matmul(PS[4][:], lhsT=bt[:], rhs=xw_[:], start=True, stop=True),
                         R=[Bbt, Bxw], W=[B_PS[4]])
                    yield
                    for hd in range(8):
                        P.op("pe", lambda e, hd=hd: e.matmul(PS[2][:, hd * 64:(hd + 1) * 64], lhsT=MT[:, hd, :],
                                                             rhs=xd[:, hd * 64:(hd + 1) * 64], start=True, stop=False),
                             R=[B_C[0], Bxd], W=[B_PS[2]])
                        P.op("pe", lambda e, hd=hd: e.matmul(PS[2][:, hd * 64:(hd + 1) * 64], lhsT=ident,
                                                             rhs=xs_[:, hd * 64:(hd + 1) * 64], start=False, stop=True),
                             R=[B_cst, Bxs], W=[B_PS[2]])
                    P.op("pe", lambda e: e.matmul(PS[3][:], lhsT=aT[:, 5, :], rhs=scrD[:, g * 512:(g + 1) * 512],
                                                  start=True, stop=True), R=[BaT, B_D[g]], W=[B_PS[3]])
                    yield
                    Sg = St[:, g * 512:(g + 1) * 512]
                    P.op("pool", lambda e: e.tensor_tensor(out=Sg.rearrange("p (h d) -> p h d", h=8),
                                                          in0=Sg.rearrange("p (h d) -> p h d", h=8),
                                                          in1=cd[:, 8 * g:8 * g + 8].unsqueeze(2).to_broadcast([128, 8, 64]),
                                                          op=ALU.mult), R=[B_St[g], B_sm], W=[B_St[g]])
                    yield
                    P.op("dve", lambda e: e.tensor_tensor(out=Sg, in0=PS[4][:], in1=Sg, op=ALU.add),
                         R=[B_PS[4], B_St[g]], W=[B_St[g]])
                    yield
                    ty = scrB[:, 0:512]
                    yy = scrB[:, 512:1024]
                    P.op("dve", lambda e: e.tensor_tensor(out=ty.rearrange("p (h d) -> p h d", h=8),
                                                          in0=PS[3][:].rearrange("p (h d) -> p h d", h=8),
                                                          in1=ea[:, 8 * g:8 * g + 8].unsqueeze(2).to_broadcast([128, 8, 64]),
                                                          op=ALU.mult), R=[B_PS[3], B_sm], W=[B_B[0]])
                    P.op("act", lambda e: e.activation(out=scrD[:, g * 512:(g + 1) * 512], in_=Sg, func=AF.Copy),
                         R=[B_St[g]], W=[B_D[g]])
                    yield
                    P.op("dve", lambda e: e.tensor_tensor(out=yy, in0=PS[2][:], in1=ty, op=ALU.add),
                         R=[B_PS[2], B_B[0]], W=[B_B[1]])
                    yield
                    P.op("pool", lambda e: e.tensor_tensor(out=yy, in0=yy, in1=szz, op=ALU.mult), R=[B_B[1], Bsz],
                         W=[B_B[1]])
                    yield
                    yn = scrC[:, 1024:1536]
                    ynT = scrC[:, 1536:2048].rearrange("p (t n) -> p t n", t=4)
                    P.op("act", lambda e: e.activation(out=yn, in_=yy, func=AF.Square,
                                                       accum_out=stt[:, 8 + 3 * g:9 + 3 * g]), R=[B_B[1], bst],
                         W=[B_C[1], bst])
                    yield
                    rstd_from_ss(stt, 8 + 3 * g, 512, bst)
                    yield
                    P.op("act", lambda e: e.activation(out=yn, in_=yy, func=AF.Copy, scale=stt[:, 10 + 3 * g:11 + 3 * g]),
                         R=[B_B[1], bst], W=[B_C[1]])
                    yield
                    for half in range(2):
                        for t in range(2):
                            tt = 2 * half + t
                            P.op("pe", lambda e, t=t, tt=tt: e.transpose(out=PT[:, 5 + t, :], in_=yn[:, tt * 128:(tt + 1) * 128],
                                                                         identity=ident), R=[B_C[1], B_cst], W=[B_PT])
                        P.op("dve", lambda e, half=half: e.tensor_tensor(
                            out=ynT[:, 2 * half:2 * half + 2, :], in0=PT[:, 5:7, :],
                            in1=LV[:, GW + 4 * g + 2 * half:GW + 4 * g + 2 * half + 2].unsqueeze(2).to_broadcast([128, 2, 128]),
                            op=ALU.mult), R=[B_PT, B_LV], W=[B_C[2]])
                        yield
                    for nb in range(2):
                        for t in range(4):
                            P.op("pe", lambda e, nb=nb, t=t: e.matmul(PS[5 + nb][:], lhsT=ynT[:, t, :],
                                                                     rhs=W_out[:, 4 * g + t, nb * 512:(nb + 1) * 512],
                                                                     start=(g == 0 and t == 0), stop=(g == 3 and t == 3)),
                                 R=[B_C[2], B_WB], W=[B_PS[5 + nb]])
                    yield

                def interleave(*gens):
                    gens = list(gens)
                    while gens:
                        for gn in list(gens):
                            try:
                                next(gn)
                            except StopIteration:
                                gens.remove(gn)

                interleave(stageA(0))
                interleave(stageA(1), stageB(0))
                interleave(stageA(2), stageB(1))
                interleave(stageA(3), stageB(2))
                interleave(stageB(3))
                for nb in range(2):
                    sl = slice(nb * 512, (nb + 1) * 512)
                    P.op("dve", lambda e, nb=nb, sl=sl: e.tensor_tensor(out=hh[:, sl], in0=PS[5 + nb][:], in1=hh[:, sl],
                                                                        op=ALU.add), R=[B_PS[5 + nb], bh], W=[bh])
                ple(hh, bh, stt, bst, pT_t[slot], B_pT[slot], PNW)
                store_chunk(L, c)

        def rope_tables():
            posf = kf[:, 0, 0:32]
            ivf = kf[:, 0, 32:64]
            pi32 = kf[:, 0, 64:96].bitcast(I32)
            P.dma("sp", lambda e: [e.dma_start(out=pi32, in_=pos_d)], "pos", 1, W=[B_kf])
            P.dma("sp", lambda e: [e.dma_start(out=ivf, in_=invf_d)], "invf", 1, W=[B_kf])
            P.op("dve", lambda e: e.tensor_copy(out=posf, in_=pi32), R=[B_kf], W=[B_kf])
            ang = scrA[:, 0:1024].rearrange("p (c f) -> p c f", c=32)
            red = scrA[:, 1024:2048].rearrange("p (c f) -> p c f", c=32)
            redi = scrB[:, 0:1024].bitcast(I32).rearrange("p (c f) -> p c f", c=32)
            redf = scrB[:, 1024:2048].rearrange("p (c f) -> p c f", c=32)
            P.op("dve", lambda e: e.tensor_tensor(out=ang, in0=posf.unsqueeze(2).to_broadcast([128, 32, 32]),
                                                  in1=ivf.unsqueeze(1).to_broadcast([128, 32, 32]), op=ALU.mult),
                 R=[B_kf], W=[B_A[0]])
            for which, dst in ((0, sinT), (1, cosT)):
                shift = 0.0 if which == 0 else float(np.pi / 2)
                P.op("dve", lambda e, shift=shift: e.tensor_scalar(out=red, in0=ang, scalar1=shift, scalar2=None, op0=ALU.add),
                     R=[B_A[0]], W=[B_A[1]])
                P.op("dve", lambda e: e.tensor_scalar(out=redi, in0=red, scalar1=float(1 / (2 * np.pi)), scalar2=None,
                                                      op0=ALU.mult), R=[B_A[1]], W=[B_B[0], B_B[1]])
                P.op("dve", lambda e: e.tensor_copy(out=redf, in_=redi), R=[B_B[0], B_B[1]], W=[B_B[2], B_B[3]])
                P.op("dve", lambda e: e.scalar_tensor_tensor(out=red, in0=redf, scalar=float(-2 * np.pi), in1=red,
                                                             op0=ALU.mult, op1=ALU.add), R=[B_B[2], B_B[3], B_A[1]], W=[B_A[1]])
                P.op("dve", lambda e: e.tensor_scalar(out=redf, in0=red, scalar1=float(np.pi), scalar2=float(-2 * np.pi),
                                                      op0=ALU.is_ge, op1=ALU.mult), R=[B_A[1]], W=[B_B[2], B_B[3]])
                P.op("dve", lambda e: e.tensor_tensor(out=red, in0=red, in1=redf, op=ALU.add), R=[B_A[1], B_B[2], B_B[3]],
                     W=[B_A[1]])
                P.op("dve", lambda e: e.tensor_scalar(out=redf, in0=red, scalar1=float(-np.pi), scalar2=float(2 * np.pi),
                                                      op0=ALU.is_lt, op1=ALU.mult), R=[B_A[1]], W=[B_B[2], B_B[3]])
                P.op("dve", lambda e: e.tensor_tensor(out=red, in0=red, in1=redf, op=ALU.add), R=[B_A[1], B_B[2], B_B[3]],
                     W=[B_A[1]])
                P.op("act", lambda e, dst=dst: e.activation(out=dst, in_=red, func=AF.Sin), R=[B_A[1]], W=B_St)

        def rope_apply(dst_bf, src, nh, c, R, W):
            s3 = src.rearrange("p (h d) -> p h d", h=nh)
            d3 = dst_bf.rearrange("p (h d) -> p h d", h=nh)
            n = nh * 32
            ta = (scrA[:, 1024:1024 + n] if nh == 16 else kf[:, 1, 0:n]).rearrange("p (h d) -> p h d", h=nh)
            tb = (scrA[:, 1536:1536 + n] if nh == 16 else kf[:, 1, 128:128 + n]).rearrange("p (h d) -> p h d", h=nh)
            Bt = [B_A[1]] if nh == 16 else [B_kf]
            cosb = cosT[:, c, :].unsqueeze(1).to_broadcast([128, nh, 32])
            sinb = sinT[:, c, :].unsqueeze(1).to_broadcast([128, nh, 32])
            x1 = s3[:, :, 0:32]
            x2 = s3[:, :, 32:64]
            P.op("dve", lambda e: e.tensor_tensor(out=ta, in0=x1, in1=cosb, op=ALU.mult), R=R + B_St, W=Bt)
            P.op("dve", lambda e: e.tensor_tensor(out=tb, in0=x2, in1=sinb, op=ALU.mult), R=R + B_St, W=Bt)
            P.op("dve", lambda e: e.tensor_tensor(out=d3[:, :, 0:32], in0=ta, in1=tb, op=ALU.subtract), R=Bt, W=W)
            P.op("dve", lambda e: e.tensor_tensor(out=ta, in0=x2, in1=cosb, op=ALU.mult), R=R + B_St, W=Bt)
            P.op("dve", lambda e: e.tensor_tensor(out=tb, in0=x1, in1=sinb, op=ALU.mult), R=R + B_St, W=Bt)
            P.op("dve", lambda e: e.tensor_tensor(out=d3[:, :, 32:64], in0=ta, in1=tb, op=ALU.add), R=Bt, W=W)

        def head_norm(src, nh, woff, stt, bst, scol, sqbuf, Bsq, R):
            s3 = src.rearrange("p (h d) -> p h d", h=nh)
            q3 = sqbuf.rearrange("p (h d) -> p h d", h=nh)
            P.op("dve", lambda e: e.tensor_tensor(out=sqbuf, in0=src, in1=src, op=ALU.mult), R=R, W=Bsq)
            P.op("dve", lambda e: e.tensor_reduce(out=stt[:, scol:scol + nh], in_=q3, axis=AX.X, op=ALU.add),
                 R=Bsq, W=[bst])
            P.op("act", lambda e: e.activation(out=stt[:, scol + nh:scol + 2 * nh], in_=stt[:, scol:scol + nh], func=AF.Ln,
                                               scale=1.0 / 64, bias=lsm[:, 2, 0:1]), R=[bst, B_lsm], W=[bst])
            P.op("act", lambda e: e.activation(out=stt[:, scol + 2 * nh:scol + 3 * nh], in_=stt[:, scol + nh:scol + 2 * nh],
                                               func=AF.Exp, scale=-0.5), R=[bst], W=[bst])
            P.op("dve", lambda e: e.tensor_tensor(out=s3, in0=s3,
                                                  in1=stt[:, scol + 2 * nh:scol + 3 * nh].unsqueeze(2).to_broadcast([128, nh, 64]),
                                                  op=ALU.mult), R=R + [bst], W=R)
            P.op("dve", lambda e: e.tensor_tensor(out=s3, in0=s3,
                                                  in1=LV[:, woff:woff + 64].unsqueeze(1).to_broadcast([128, nh, 64]),
                                                  op=ALU.mult), R=R + [B_LV], W=R)

        def attn_layer(L):
            j = L - 2
            layer_common_prep(L)
            ANW, PNW, KVNW, KNW, QNW, SNK = 0, 8, 16, 32, 96, 160
            pairs = [(WB[:, 0:16384], w_ai_d[j], 4), (WB[:, 16384:24576], w_ao_d[j], 2),
                     (WB[:, 57600:65792], w_g_d[L], 2), (WB[:, 65792:67840], w_p_d[L], 1)]
            if j == 0:
                pairs.append((WB[:, 24576:28672], w_kv_d, 1))
            load_weights(pairs, L)
            if j == 0:
                rope_tables()
                Vflat = WB[:, 36864:36864 + 8320].rearrange("p (n e) -> p n e", e=65)
                P.op("dve", lambda e: e.memset(Vflat[:, :, 64:65], 1.0), R=[B_WB], W=[B_Vones])
            off = 45184
            q2 = [scrD[:].rearrange("p (t a n) -> p t a n", t=8, a=2),
                  WB[:, off:off + 2048].rearrange("p (t a n) -> p t a n", t=8, a=2)]
            B_q2 = [Buf("q2a"), Buf("q2b")]
            sgb = [scrB[:, 1024:2048], WB[:, off + 2048:off + 4096].bitcast(F32)]
            B_sg = [Buf("sga"), Buf("sgb")]
            uTb = WB[:, off + 4096:off + 5120].rearrange("p (k n) -> p k n", k=8)
            ubb = WB[:, off + 5120:off + 6144]
            B_uTb, B_ubb = Buf("uTb"), Buf("ubb")
            tgb = WB[:, off + 6144:off + 8192].bitcast(F32)
            vvb = WB[:, off + 8192:off + 10240].bitcast(F32)
            B_tgb, B_vvb = Buf("tgb"), Buf("vvb")
            B_den = Buf("den")
            uTkv = WB[:, off + 10240:off + 11264].rearrange("p (k n) -> p k n", k=8)
            B_uTkv = Buf("uTkv")
            P.op("pool", lambda e: e.memset(scrD[:], 0.0), R=[B_WB], W=B_D + [B_q2[0]])
            P.op("pool", lambda e: e.memset(WB[:, off:off + 2048], 0.0), R=[B_WB], W=[B_q2[1]])
            P.op("act", lambda e: e.activation(out=lsm[:, 1, 0:16], in_=LV[:, SNK:SNK + 16], func=AF.Exp), R=[B_LV],
                 W=[B_lsm])

            def front(c):
                slot = c % 2
                hh, bh, stt, bst = h_t[slot], B_h[slot], stat[slot], B_stat[slot]
                altF = (u_bf[:], B_u, uT[:], B_uT, 0)
                P.op("pool", lambda e, stt=stt: e.memset(stt[:], 0.0), W=[bst])
                yield
                if j == 0:
                    norm_T(hh, bh, stt, bst, 0, ANW, alt=altF, extra=(uTkv, B_uTkv, KVNW))
                    yield
                    for k in range(8):
                        P.op("pe", lambda e, k=k: e.matmul(PS[3][:], lhsT=uTkv[:, k, :], rhs=W_kv[:, k, :], start=(k == 0),
                                                           stop=(k == 7)), R=[B_uTkv, B_WB], W=[B_PS[3]])
                    kfl = kf[:, 0, :]
                    P.op("act", lambda e: e.activation(out=kfl, in_=PS[3][:, 0:256], func=AF.Copy), R=[B_PS[3]], W=[B_kf])
                    P.op("act", lambda e, c=c: e.activation(out=Vst[:, c, :, 0:64],
                                                            in_=PS[3][:, 256:512].rearrange("p (g d) -> p g d", g=4),
                                                            func=AF.Copy), R=[B_PS[3], B_WB], W=[B_V[c]])
                    yield
                    head_norm(kfl, 4, KNW, stt, bst, 24, kf[:, 2, :], [B_kf], [B_kf])
                    yield
                    rope_apply(krbuf, kfl, 4, c, [B_kf], [B_kr])
                    yield
                    for t in range(2):
                        P.op("pe", lambda e, t=t: e.transpose(out=PT[:, t, :], in_=krbuf[:, t * 128:(t + 1) * 128],
                                                              identity=ident), R=[B_kr, B_cst], W=[B_PT])
                    P.op("act", lambda e, c=c: e.activation(out=KT[:, :, c * 128:(c + 1) * 128], in_=PT[:, 0:2, :],
                                                            func=AF.Copy), R=[B_PT, B_WB], W=[B_KT[c]])
                    yield
                if j != 0:
                    norm_T(hh, bh, stt, bst, 0, ANW, alt=altF)
                yield
                qf = scrA[:, 0:1024]
                for nb in range(2):
                    for k in range(8):
                        P.op("pe", lambda e, nb=nb, k=k: e.matmul(PS[3 + nb][:], lhsT=uT[:, k, :],
                                                                 rhs=W_ai[:, k, nb * 512:(nb + 1) * 512],
                                                                 start=(k == 0), stop=(k == 7)),
                             R=[B_uT, B_WB], W=[B_PS[3 + nb]])
                    P.op("act", lambda e, nb=nb: e.activation(out=qf[:, nb * 512:(nb + 1) * 512], in_=PS[3 + nb][:],
                                                              func=AF.Copy), R=[B_PS[3 + nb]], W=[B_A[0]])
                    yield
                for nb in range(2):
                    for k in range(8):
                        P.op("pe", lambda e, nb=nb, k=k: e.matmul(PS[3 + nb][:], lhsT=uT[:, k, :],
                                                                 rhs=W_ai[:, k, 1024 + nb * 512:1024 + (nb + 1) * 512],
                                                                 start=(k == 0), stop=(k == 7)),
                             R=[B_uT, B_WB], W=[B_PS[3 + nb]])
                    P.op("act", lambda e, nb=nb: e.activation(out=sgb[slot][:, nb * 512:(nb + 1) * 512], in_=PS[3 + nb][:],
                                                              func=AF.Silu), R=[B_PS[3 + nb]], W=[B_sg[slot]])
                    yield
                head_norm(qf, 16, QNW, stt, bst, 8, scrA[:, 1024:2048], [B_A[1]], [B_A[0]])
                yield
                qr = scrC[:, 0:1024]
                rope_apply(qr, qf, 16, c, [B_A[0]], [B_C[0]])
                yield
                qT2 = q2[slot]
                for hf in range(2):
                    for i in range(4):
                        t = 4 * hf + i
                        P.op("pe", lambda e, t=t, i=i: e.transpose(out=PT[:, i, :], in_=qr[:, t * 128:(t + 1) * 128],
                                                                   identity=ident), R=[B_C[0], B_cst], W=[B_PT])
                    P.op("act", lambda e, hf=hf: e.activation(out=qT2[0:64, 4 * hf:4 * hf + 4, 0, :], in_=PT[0:64, 0:4, :],
                                                              func=AF.Copy), R=[B_PT], W=[B_q2[slot]])
                    P.op("act", lambda e, hf=hf: e.activation(out=qT2[64:128, 4 * hf:4 * hf + 4, 1, :], in_=PT[64:128, 0:4, :],
                                                              func=AF.Copy), R=[B_PT], W=[B_q2[slot]])
                    yield

            def back(c):
                slot = c % 2
                hh, bh, stt, bst = h_t[slot], B_h[slot], stat[slot], B_stat[slot]
                qT2 = q2[slot]
                blks = [c - 1, c] if c > 0 else [c]
                nb_ = len(blks)
                for r in range(8):
                    sb_ = 5 + (r % 2)
                    es = r % 2
                    for hh_ in range(2):
                        hp = 2 * r + hh_
                        g = PERM[hp] // 4
                        half = hp % 2
                        for bi, blk in enumerate(blks):
                            col = (hh_ * 2 + bi) * 128
                            P.op("pe", lambda e, g=g, half=half, blk=blk, col=col, r=r, sb_=sb_: e.matmul(
                                PS[sb_][:, col:col + 128], lhsT=KT[:, g // 2, blk * 128:(blk + 1) * 128],
                                rhs=qT2[:, r, half, :], start=True, stop=True),
                                 R=[B_KT[blk], B_q2[slot]], W=[B_PS[sb_]])
                    Ev = Eatt[:, es, :].rearrange("p (a b n) -> p a b n", a=2, b=2)
                    Pv = PS[sb_][:].rearrange("p (a b n) -> p a b n", a=2, b=2)
                    if nb_ == 2:
                        P.op("act", lambda e, es=es, sb_=sb_: e.activation(out=Eatt[:, es, :], in_=PS[sb_][:], func=AF.Exp,
                                                                           scale=0.125), R=[B_PS[sb_]], W=[B_Eatt[es]])
                        P.op("dve", lambda e, Ev=Ev: e.tensor_tensor(out=Ev[:, :, 0, :], in0=Ev[:, :, 0, :],
                                                                     in1=Lbf.unsqueeze(1).to_broadcast([128, 2, 128]),
                                                                     op=ALU.mult), R=[B_Eatt[es], B_cst], W=[B_Eatt[es]])
                        P.op("dve", lambda e, Ev=Ev: e.tensor_tensor(out=Ev[:, :, 1, :], in0=Ev[:, :, 1, :],
                                                                     in1=Ubf.unsqueeze(1).to_broadcast([128, 2, 128]),
                                                                     op=ALU.mult), R=[B_Eatt[es], B_cst], W=[B_Eatt[es]])
                    else:
                        P.op("act", lambda e, Ev=Ev, Pv=Pv: e.activation(out=Ev[:, :, 0, :], in_=Pv[:, :, 0, :], func=AF.Exp,
                                                                         scale=0.125), R=[B_PS[sb_]], W=[B_Eatt[es]])
                        P.op("dve", lambda e, Ev=Ev: e.tensor_tensor(out=Ev[:, :, 0, :], in0=Ev[:, :, 0, :],
                                                                     in1=Ubf.unsqueeze(1).to_broadcast([128, 2, 128]),
                                                                     op=ALU.mult), R=[B_Eatt[es], B_cst], W=[B_Eatt[es]])
                    yield
                    for hh_ in range(2):
                        hp = 2 * r + hh_
                        g = PERM[hp] // 4
                        ob = hp // 6
                        oc = (hp % 6) * 65
                        for bi, blk in enumerate(blks):
                            P.op("pe", lambda e, hh_=hh_, bi=bi, blk=blk, g=g, ob=ob, oc=oc, Ev=Ev: e.matmul(
                                PS[ob][:, oc:oc + 65], lhsT=Ev[:, hh_, bi, :], rhs=Vst[:, blk, g, :],
                                start=(bi == 0), stop=(bi == nb_ - 1)),
                                 R=[B_Eatt[es], B_V[blk], B_Vones], W=[B_PS[ob]])
                    yield
                den = lsm[:, 3, 0:16]
                rden = lsm[:, 3, 16:32]
                o_ = scrB[:, 0:1024].rearrange("p (h d) -> p h d", h=16)
                for ob in range(3):
                    nh = 6 if ob < 2 else 4
                    pv = PS[ob][:, 0:nh * 65].rearrange("p (h e) -> p h e", e=65)
                    hs = slice(ob * 6, ob * 6 + nh)
                    P.op("dve", lambda e, pv=pv, hs=hs, nh=nh: e.tensor_tensor(
                        out=den[:, hs].unsqueeze(2), in0=pv[:, :, 64:65], in1=lsm[:, 1, hs].unsqueeze(2), op=ALU.add),
                         R=[B_PS[ob], B_lsm], W=[B_den])
                yield
                P.op("dve", lambda e: e.reciprocal(out=rden, in_=den), R=[B_den], W=[B_den])
                yield
                for ob in range(3):
                    nh = 6 if ob < 2 else 4
                    pv = PS[ob][:, 0:nh * 65].rearrange("p (h e) -> p h e", e=65)
                    hs = slice(ob * 6, ob * 6 + nh)
                    P.op("dve", lambda e, pv=pv, hs=hs, nh=nh: e.tensor_tensor(
                        out=o_[:, hs, :], in0=pv[:, :, 0:64], in1=rden[:, hs].unsqueeze(2).to_broadcast([128, nh, 64]),
                        op=ALU.mult), R=[B_PS[ob], B_den], W=[B_B[0], B_B[1]])
                    yield
                og = scrC[:, 1024:2048]
                P.op("dve", lambda e: e.tensor_tensor(out=og, in0=scrB[:, 0:1024], in1=sgb[slot], op=ALU.mult),
                     R=[B_B[0], B_B[1], B_sg[slot]], W=[B_C[1], B_C[2]])
                yield
                for hf in range(2):
                    for i in range(4):
                        t = 4 * hf + i
                        P.op("pe", lambda e, t=t, i=i: e.transpose(out=PT[:, 4 + i, :], in_=og[:, t * 128:(t + 1) * 128],
                                                                   identity=ident), R=[B_C[1], B_C[2], B_cst], W=[B_PT])
                    P.op("act", lambda e, hf=hf: e.activation(out=uTb[:, 4 * hf:4 * hf + 4, :], in_=PT[:, 4:8, :], func=AF.Copy),
                         R=[B_PT], W=[B_uTb])
                    yield
                for nb in range(2):
                    for k in range(8):
                        P.op("pe", lambda e, nb=nb, k=k: e.matmul(PS[5 + nb][:], lhsT=uTb[:, k, :],
                                                                 rhs=W_ao[:, k, nb * 512:(nb + 1) * 512], start=(k == 0),
                                                                 stop=(k == 7)), R=[B_uTb, B_WB], W=[B_PS[5 + nb]])
                    yield
                for nb in range(2):
                    sl = slice(nb * 512, (nb + 1) * 512)
                    P.op("dve", lambda e, nb=nb, sl=sl: e.tensor_tensor(out=hh[:, sl], in0=PS[5 + nb][:], in1=hh[:, sl],
                                                                        op=ALU.add), R=[B_PS[5 + nb], bh], W=[bh])
                    yield
                ple(hh, bh, stt, bst, pT_t[slot], B_pT[slot], PNW, alt=(ubb, B_ubb, uTb, B_uTb, 4),
                    banks=(0, 1, 5, 6), tv=(tgb, vvb, B_tgb, B_vvb))
                store_chunk(L, c)
                yield

            def interleave(*gens):
                gens = list(gens)
                while gens:
                    for gn in list(gens):
                        try:
                            next(gn)
                        except StopIteration:
                            gens.remove(gn)

            load_chunk(L, 0)
            interleave(front(0))
            for c in range(nch):
                if c + 1 < nch:
                    load_chunk(L, c + 1)
                    interleave(front(c + 1), back(c))
                else:
                    interleave(back(c))

        for L in layers:
            if L < 2:
                mamba_layer(L)
            else:
                attn_layer(L)
        finals = [hstores[(layers[-1], c)] for c in range(nch)]
        P.emit(final_waits=finals)
    return nc


def _ktile(w):
    K, N = w.shape
    return np.ascontiguousarray(w.reshape(K // 128, 128, N).transpose(1, 0, 2).reshape(128, (K // 128) * N))


def _rep(v):
    return np.broadcast_to(np.asarray(v, np.float32).reshape(1, -1), (128, v.size))


def prepare_inputs(x, p, positions, ssm_norm_w, ssm_in_w, ssm_conv_w, ssm_conv_b, ssm_dt_bias, ssm_a_log, ssm_d,
                   ssm_gnorm_w, ssm_out_w, kv_norm_w, kv_w, k_norm_w, attn_norm_w, attn_in_w, q_norm_w, attn_sinks,
                   attn_out_w, ple_norm_w, ple_gate_w, ple_proj_w):
    f = np.float32
    x = np.asarray(x, f)
    p = np.asarray(p, f)
    positions = np.asarray(positions, np.int32)
    cst = np.zeros((128, 4, 128), f)
    k = np.arange(128)
    cst[:, 0, :] = np.eye(128)
    cst[:, 1, :] = (k[:, None] <= k[None, :])
    cst[:, 2, :] = (k[:, None] > k[None, :])
    cst[:, 3, :] = 1.0
    invf = (10000.0 ** (-(np.arange(32, dtype=np.float32) * 2.0 / 64))).astype(f)
    invf = np.ascontiguousarray(_rep(invf))
    qperm = np.concatenate([np.arange(h * 64, (h + 1) * 64) for h in PERM])
    w_in = np.stack([_ktile(np.asarray(ssm_in_w[l], f)) for l in range(2)])
    w_out = np.stack([_ktile(np.asarray(ssm_out_w[l], f)) for l in range(2)])
    w_g = np.stack([_ktile(np.asarray(ple_gate_w[i], f)) for i in range(4)])
    w_p = np.stack([_ktile(np.asarray(ple_proj_w[i], f)) for i in range(4)])
    w_kv = _ktile(np.asarray(kv_w, f))
    ai = []
    ao = []
    for j in range(2):
        w = np.asarray(attn_in_w[j], f)
        w = np.concatenate([w[:, :1024][:, qperm], w[:, 1024:][:, qperm]], axis=1)
        ai.append(_ktile(w))
        ao.append(_ktile(np.asarray(attn_out_w[j], f)[qperm, :]))
    w_ai = np.stack(ai)
    w_ao = np.stack(ao)
    lv = np.zeros((4, 128, LVW), f)

    def fm(v):
        v = np.asarray(v, f)
        return v.reshape(-1, 128).T

    for l in range(2):
        lv[l, :, 0:8] = fm(ssm_norm_w[l])
        lv[l, :, 8:16] = fm(ple_norm_w[l])
        lv[l, :, 16:32] = fm(ssm_gnorm_w[l])
        cw = np.asarray(ssm_conv_w[l], f)
        lv[l, :, 32:128] = cw.reshape(4, 24, 128).transpose(2, 1, 0).reshape(128, 96)
        lv[l, :, 128:152] = np.asarray(ssm_conv_b[l], f).reshape(24, 128).T
        lv[l, :, 152:184] = _rep(np.asarray(ssm_dt_bias[l], f))
        lv[l, :, 184:216] = _rep(np.asarray(ssm_a_log[l], f))
        lv[l, :, 216:248] = _rep(np.asarray(ssm_d[l], f))
    for j in range(2):
        L = 2 + j
        lv[L, :, 0:8] = fm(attn_norm_w[j])
        lv[L, :, 8:16] = fm(ple_norm_w[L])
        lv[L, :, 16:24] = fm(kv_norm_w)
        lv[L, :, 32:96] = _rep(np.asarray(k_norm_w, f))
        lv[L, :, 96:160] = _rep(np.asarray(q_norm_w[j], f))
        lv[L, :, 160:176] = _rep(np.asarray(attn_sinks[j], f)[PERM])
    shared = dict(cst=cst, invf=invf, w_in=w_in, w_out=w_out, w_g=w_g, w_p=w_p, w_kv=w_kv, w_ai=w_ai, w_ao=w_ao, lv=lv)
    in_maps = []
    for b in range(x.shape[0]):
        m = dict(shared)
        m["x"] = np.ascontiguousarray(x[b])
        m["pT"] = np.ascontiguousarray(p[:, b].transpose(0, 2, 1))
        m["pos"] = np.ascontiguousarray(positions[b].reshape(32, 128).T)
        in_maps.append(m)
    return in_maps


_NC_CACHE = {}
LAUNCHES = [(0, 1, 2, 3)]


def kernel(**inputs):
    in_maps = prepare_inputs(**inputs)
    n = len(in_maps)
    outs = None
    for grp in LAUNCHES:
        if grp not in _NC_CACHE:
            _NC_CACHE[grp] = build(layers=grp)
        nc = _NC_CACHE[grp]
        if outs is not None:
            for b in range(n):
                in_maps[b]["x"] = outs[b]
        res = run_bass_kernel_spmd(nc, in_maps, core_ids=list(range(n)))
        outs = [np.ascontiguousarray(np.asarray(r["out"], np.float32)) for r in res.results]
    return np.stack(outs, axis=0)
```
